# Optimizing a Trainium2 kernel written in Bass

```python
import math
import jax, jax.numpy as jnp
from jax import lax
import numpy as np

D_MODEL = 1024
BATCH = 8
SEQ = 2048
DEPTH = 4
DEC_BATCH = 128
DEC_SEQ = 8
PAST_LEN = 16384
PAGE_SIZE = 128

D_RNN = D_MODEL
D_CONV = D_MODEL
N_BLOCKS = 8
BLOCK = D_RNN // N_BLOCKS
CONV4_W = 4
CONV3_W = 3
C_RG = 8.0
ALPHA = (2.0 * DEPTH) ** 0.25
BETA = (8.0 * DEPTH) ** -0.25
LN_EPS = 1e-5
SPLITS = np.cumsum([D_RNN, D_RNN, D_CONV, D_CONV, D_CONV, D_CONV, D_MODEL]).tolist()
N_IN = 2 * D_RNN + 4 * D_CONV + 2 * D_MODEL

kernel_name = "hawk_shortconv_gated_parallel_deepnorm_step"


def _causal_dwconv(buf, u, w):
    width = w.shape[0]
    T = u.shape[1]
    up = jnp.concatenate([buf.astype(u.dtype), u], axis=1)
    y = up[:, 0:T] * w[0]
    for k in range(1, width):
        y = y + up[:, k:k + T] * w[k]
    return y, up[:, T:]


def _rglru(xr, h0, reset, w_a, b_a, w_x, b_x, lam):
    bsz, T, _ = xr.shape
    xb = xr.reshape(bsz, T, N_BLOCKS, BLOCK)
    ga = jnp.einsum('btni,nij->btnj', xb, w_a).reshape(bsz, T, D_RNN) + b_a
    gx = jnp.einsum('btni,nij->btnj', xb, w_x).reshape(bsz, T, D_RNN) + b_x
    r = jax.nn.sigmoid(ga.astype(jnp.float32))
    i = jax.nn.sigmoid(gx.astype(jnp.float32))
    log_a = -C_RG * r * jax.nn.softplus(-lam.astype(jnp.float32))
    a = jnp.exp(log_a)
    mult = jnp.sqrt(-jnp.expm1(2.0 * log_a))
    mult = jnp.where(reset[None, :, None], 1.0, mult)
    b = mult * i * xr.astype(jnp.float32)

    def step(h, ab):
        a_t, b_t = ab
        h = a_t * h + b_t
        return h, h

    hT, hs = lax.scan(step, h0.astype(jnp.float32), (a.swapaxes(0, 1), b.swapaxes(0, 1)))
    return hs.swapaxes(0, 1).astype(xr.dtype), hT.astype(xr.dtype)


def _layernorm(x, g, b):
    xf = x.astype(jnp.float32)
    mu = jnp.mean(xf, axis=-1, keepdims=True)
    var = jnp.mean(jnp.square(xf - mu), axis=-1, keepdims=True)
    y = (xf - mu) * lax.rsqrt(var + LN_EPS)
    return (y * g.astype(jnp.float32) + b.astype(jnp.float32)).astype(x.dtype)


def _layer(x, h0, c4_buf, c3_buf, reset, w_in, b_in, conv4_w, conv4_b, w_rg_a, b_rg_a,
           w_rg_x, b_rg_x, rg_lambda, conv3_w, w_rnn_out, w_conv_out, w_out, ln_g, ln_b):
    z = x @ w_in + b_in
    xr, gr, cb, cc, ch, gc, g_rnn, g_conv = jnp.split(z, SPLITS, axis=-1)
    xr_c, c4_new = _causal_dwconv(c4_buf, xr, conv4_w)
    xr_c = xr_c + conv4_b
    hs, hT = _rglru(xr_c, h0, reset, w_rg_a, b_rg_a, w_rg_x, b_rg_x, rg_lambda)
    y_rnn = (hs * jax.nn.silu(gr)) @ w_rnn_out
    v, c3_new = _causal_dwconv(c3_buf, cc * ch, conv3_w)
    y_conv = (cb * v * jax.nn.silu(gc)) @ w_conv_out
    m = jax.nn.sigmoid(g_rnn) * y_rnn + jax.nn.sigmoid(g_conv) * y_conv
    out = m @ w_out
    x = _layernorm(ALPHA * x + out, ln_g, ln_b)
    return x, hT, c4_new, c3_new


def setup_inputs(seed: int = 0) -> dict:
    key = jax.random.key(seed)
    ks = jax.random.split(key, 24)
    f32 = jnp.float32
    n = lambda k, s, sc: jax.random.normal(k, s, f32) * sc
    u = jax.random.uniform(ks[13], (DEPTH, D_RNN), f32, 0.9, 0.999)
    a_base = u ** (1.0 / C_RG)
    rg_lambda = jnp.log(a_base) - jnp.log1p(-a_base)
    return {
        "x_prompt": n(ks[0], (BATCH, SEQ, D_MODEL), 1.0),
        "x_sample": n(ks[1], (DEC_BATCH, DEC_SEQ, D_MODEL), 1.0),
        "state_rglru": n(ks[2], (DEPTH, DEC_BATCH, D_RNN), 0.5),
        "state_conv4": n(ks[3], (DEPTH, DEC_BATCH, CONV4_W - 1, D_RNN), 1.0),
        "state_conv3": n(ks[4], (DEPTH, DEC_BATCH, CONV3_W - 1, D_CONV), 1.0),
        "w_in": n(ks[5], (DEPTH, D_MODEL, N_IN), D_MODEL ** -0.5),
        "b_in": n(ks[6], (DEPTH, N_IN), 0.02),
        "conv4_w": n(ks[7], (DEPTH, CONV4_W, D_RNN), CONV4_W ** -0.5),
        "conv4_b": n(ks[8], (DEPTH, D_RNN), 0.02),
        "w_rg_a": n(ks[9], (DEPTH, N_BLOCKS, BLOCK, BLOCK), BLOCK ** -0.5),
        "b_rg_a": n(ks[10], (DEPTH, D_RNN), 0.02),
        "w_rg_x": n(ks[11], (DEPTH, N_BLOCKS, BLOCK, BLOCK), BLOCK ** -0.5),
        "b_rg_x": n(ks[12], (DEPTH, D_RNN), 0.02),
        "rg_lambda": rg_lambda,
        "conv3_w": n(ks[14], (DEPTH, CONV3_W, D_CONV), CONV3_W ** -0.5),
        "w_rnn_out": n(ks[15], (DEPTH, D_RNN, D_MODEL), BETA * D_RNN ** -0.5),
        "w_conv_out": n(ks[16], (DEPTH, D_CONV, D_MODEL), BETA * D_CONV ** -0.5),
        "w_out": n(ks[17], (DEPTH, D_MODEL, D_MODEL), BETA * D_MODEL ** -0.5),
        "ln_g": 1.0 + n(ks[18], (DEPTH, D_MODEL), 0.02),
        "ln_b": n(ks[19], (DEPTH, D_MODEL), 0.02),
    }


def reference(x_prompt, x_sample, state_rglru, state_conv4, state_conv3, w_in, b_in, conv4_w,
              conv4_b, w_rg_a, b_rg_a, w_rg_x, b_rg_x, rg_lambda, conv3_w, w_rnn_out,
              w_conv_out, w_out, ln_g, ln_b):
    bp, tp, _ = x_prompt.shape
    ts = x_sample.shape[1]
    reset_p = jnp.arange(tp) == 0
    reset_s = jnp.zeros((ts,), dtype=bool)
    h0_p = jnp.zeros((bp, D_RNN), x_prompt.dtype)
    c4_p = jnp.zeros((bp, CONV4_W - 1, D_RNN), x_prompt.dtype)
    c3_p = jnp.zeros((bp, CONV3_W - 1, D_CONV), x_prompt.dtype)
    yp, ys = x_prompt, x_sample
    ph, pc4, pc3, sh, sc4, sc3 = [], [], [], [], [], []
    for l in range(DEPTH):
        w = (w_in[l], b_in[l], conv4_w[l], conv4_b[l], w_rg_a[l], b_rg_a[l], w_rg_x[l],
             b_rg_x[l], rg_lambda[l], conv3_w[l], w_rnn_out[l], w_conv_out[l], w_out[l],
             ln_g[l], ln_b[l])
        yp, h, c4, c3 = _layer(yp, h0_p, c4_p, c3_p, reset_p, *w)
        ph.append(h); pc4.append(c4); pc3.append(c3)
        ys, h, c4, c3 = _layer(ys, state_rglru[l], state_conv4[l], state_conv3[l], reset_s, *w)
        sh.append(h); sc4.append(c4); sc3.append(c3)
    return (yp, ys, jnp.stack(ph), jnp.stack(pc4), jnp.stack(pc3),
            jnp.stack(sh), jnp.stack(sc4), jnp.stack(sc3))
```

```python
import contextlib
import numpy as np
import concourse.bass as bass
import concourse.mybir as mybir
from concourse.bass_utils import run_bass_kernel_spmd

F32 = mybir.dt.float32
BF16 = mybir.dt.bfloat16
AF = mybir.ActivationFunctionType
ALU = mybir.AluOpType

DEPTH = 4
D = 1024
NCH = 8
NPR = 2048
NSM = 128
NTOK = NPR + NSM
ALPHA = (2.0 * DEPTH) ** 0.25
EPS = 1e-5
NMAX = 384
NSLOT = 8
UW = 256
NPAR = 21

TILES = [
    (0, 0, [("P", 0, 384, 0)]),
    (0, 384, [("P", 384, 256, 0), ("S", 2048, 128, 256)]),
    (1, 0, [("P", 640, 352, 0)]),
    (1, 352, [("P", 992, 352, 0)]),
    (2, 0, [("P", 1344, 352, 0)]),
    (2, 352, [("P", 1696, 352, 0)]),
]
PASS_TILES = {0: [0, 1], 1: [2, 3], 2: [4, 5]}
PASSW = 768


def tile_n(u):
    return sum(s[2] for s in TILES[u][2])


class _Op:
    __slots__ = ("eng", "fn", "deps", "sig", "cnt", "dma", "dma_ord")


class Prog:
    ENGS = ("pe", "act", "dve", "pool", "sp")

    def __init__(self):
        self.streams = {e: [] for e in self.ENGS}
        self.last_w = {}
        self.readers = {}
        self.dma_last = {}
        self.dma_cnt = {}

    def op(self, eng, fn, reads=(), writes=(), dma=None):
        o = _Op()
        o.eng, o.fn, o.sig, o.cnt, o.dma, o.dma_ord = eng, fn, False, 0, dma, 0
        deps = []
        for r in reads:
            w = self.last_w.get(r)
            if w is not None:
                deps.append(w)
            if r[0] == "ps":
                rd = self.readers.get(r)
                if rd:
                    deps.extend(v for k, v in rd[0].items() if k != eng)
        for r in writes:
            w = self.last_w.get(r)
            if w is not None:
                deps.append(w)
            rd = self.readers.get(r)
            if rd:
                deps.extend(rd[0].values())
                deps.extend(rd[1])
        if dma is not None:
            prev = self.dma_last.get(dma)
            if prev is not None:
                deps.append(prev)
            self.dma_last[dma] = o
            self.dma_cnt[dma] = self.dma_cnt.get(dma, 0) + 1
            o.dma_ord = self.dma_cnt[dma]
        seen = set()
        o.deps = []
        for d in deps:
            if id(d) in seen or d is o:
                continue
            seen.add(id(d))
            if eng == "pe" and d.eng == "pe" and d.dma is None and dma is None:
                continue
            d.sig = True
            o.deps.append(d)
        for r in writes:
            self.last_w[r] = o
            self.readers[r] = ({}, [])
        for r in reads:
            rd = self.readers.setdefault(r, ({}, []))
            if dma is None:
                rd[0][eng] = o
            else:
                rd[1].append(o)
        self.streams[eng].append(o)
        return o

    def finalize(self):
        for e in self.ENGS:
            c = 0
            for o in self.streams[e]:
                if o.sig and o.dma is None:
                    c += 1
                o.cnt = c

    def emit_stream(self, e, eng, esem, dsem):
        waited = {}
        for o in self.streams[e]:
            for d in o.deps:
                if d.dma is not None:
                    key, sem, val = ("d", d.dma), dsem[d.dma], 16 * d.dma_ord
                else:
                    key, sem, val = ("e", d.eng), esem[d.eng], d.cnt
                if waited.get(key, 0) < val:
                    eng.wait_ge(sem, val)
                    waited[key] = val
            ins = o.fn(eng)
            if o.dma is not None:
                ins.then_inc(dsem[o.dma], 16)
            elif o.sig:
                ins.then_inc(esem[e], 1)


def build_nc(depth=DEPTH, npass=3, nphase=4, small=True, p3=9):
    nc = bass.Bass("TRN2", target_bir_lowering=False)
    xin = nc.dram_tensor("xin", [NTOK, D], F32, kind="ExternalInput").ap()
    st_in = nc.dram_tensor("st_in", [DEPTH, 96, D], F32, kind="ExternalInput").ap()
    params = nc.dram_tensor("params", [DEPTH * NPAR, D], F32, kind="ExternalInput").ap()
    w_in = nc.dram_tensor("w_in", [DEPTH, D, 8 * D], F32, kind="ExternalInput").ap()
    w_ro = nc.dram_tensor("w_ro", [DEPTH, D, D], F32, kind="ExternalInput").ap()
    w_co = nc.dram_tensor("w_co", [DEPTH, D, D], F32, kind="ExternalInput").ap()
    w_o = nc.dram_tensor("w_o", [DEPTH, D, D], F32, kind="ExternalInput").ap()
    w_a = nc.dram_tensor("w_a", [DEPTH, NCH, 128, 128], F32, kind="ExternalInput").ap()
    w_x = nc.dram_tensor("w_x", [DEPTH, NCH, 128, 128], F32, kind="ExternalInput").ap()
    ident_d = nc.dram_tensor("ident", [128, 128], F32, kind="ExternalInput").ap()
    y = nc.dram_tensor("y", [NTOK, D], F32, kind="ExternalOutput").ap()
    st_out = nc.dram_tensor("st_out", [DEPTH, 102, D], F32, kind="ExternalOutput").ap()

    P = Prog()
    es = contextlib.ExitStack()
    with es:
        def sb(name, shape, dt):
            return es.enter_context(nc.sbuf_tensor(name, shape, dt))

        x32 = sb("x32", [128, NCH, NTOK], F32)
        xb = sb("xb", [128, NCH, PASSW], BF16)
        pb = sb("pb", [128, NCH, PASSW], BF16)
        qb = sb("qb", [128, NCH, PASSW], BF16)
        mb = sb("mb", [128, NCH, PASSW], BF16)
        wring = sb("wring", [128, NSLOT, NCH, UW], BF16)
        wab = sb("wab", [128, 2, NCH, 128], BF16)
        dg = sb("dg", [128, 2, 8, 128], BF16)
        parT = sb("parT", [128, NCH, DEPTH * NPAR], F32)
        der = sb("der", [128, 4, DEPTH, NCH], F32)
        ident = sb("identf", [128, 128], F32)
        identb = sb("identb", [128, 128], BF16)
        ones = sb("ones", [128, NMAX], BF16)
        stS = sb("stS", [128, NCH, 96], F32)
        sout = sb("sout", [128, NCH, 102], F32)
        c4h = sb("c4h", [128, NCH, 3], F32)
        c3h = sb("c3h", [128, NCH, 2], F32)
        hcar = sb("hcar", [128, NCH], F32)
        epst = sb("epst", [128, 1], F32)
        stage = sb("stage", [128, 2, D], F32)
        NTF, NTB = 18, 4
        tf = sb("tf", [128, NTF, NMAX], F32)
        tb = sb("tb", [128, NTB, 448], BF16)
        ps = es.enter_context(nc.psum_tensor("ps", [128, 8, 512], F32))

        tff0 = tf[:].rearrange("p s n -> p (s n)")
        XS = [(stage[:, 0, :], [("stg", 0)]), (stage[:, 1, :], [("stg", 1)])]
        for i_ in range(4):
            XS.append((tff0[:, i_ * 3 * NMAX:i_ * 3 * NMAX + D], [("tf", 3 * i_ + k_) for k_ in range(3)]))
        esem = {e: es.enter_context(nc.semaphore("s_" + e)) for e in ("pe", "act", "dve", "pool")}
        dnames = (["w%d" % i for i in range(NSLOT)] + ["stg%d" % i for i in range(6)] + ["misc", "wab"]
                  + ["out%d" % i for i in range(6)])
        dsem = {n: es.enter_context(nc.semaphore("d_" + n)) for n in dnames}

        def par(l, r, c):
            return parT[:, c, l * NPAR + r: l * NPAR + r + 1]

        P.op("sp", lambda e: e.dma_start(out=ident[:], in_=ident_d), writes=[("ident",)], dma="misc")
        P.op("sp", lambda e: e.dma_start(out=stage[0:DEPTH * NPAR, 0, :], in_=params),
             writes=[("stg", 0)], dma="stg0")
        P.op("dve", lambda e: e.tensor_copy(out=identb[:], in_=ident[:]), reads=[("ident",)],
             writes=[("identb",)])
        P.op("dve", lambda e: e.memset(ones[:], 1.0), writes=[("ones",)])
        P.op("dve", lambda e: e.memset(epst[:], EPS), writes=[("eps",)])

        def tr_params(pe):
            for c in range(NCH):
                ins = pe.transpose(ps[:, c // 4, (c % 4) * 128:(c % 4) * 128 + DEPTH * NPAR],
                                   stage[0:DEPTH * NPAR, 0, c * 128:(c + 1) * 128],
                                   ident[0:DEPTH * NPAR, 0:DEPTH * NPAR])
            return ins
        P.op("pe", tr_params, reads=[("stg", 0), ("ident",)], writes=[("ps", 0), ("ps", 1)])
        for h in range(2):
            P.op("act", lambda e, h=h: e.activation(
                out=parT[:, 4 * h:4 * h + 4, :],
                in_=ps[:, h, :].rearrange("p (c n) -> p c n", n=128)[:, :, 0:DEPTH * NPAR],
                func=AF.Identity), reads=[("ps", h)], writes=[("parT", h)])
        PARK = [("parT", 0), ("parT", 1)]
        for l in range(DEPTH):
            def mk(l):
                def v(ri):
                    return parT[:, :, l * NPAR + ri]
                P.op("act", lambda e: e.activation(out=der[:, 0, l, :], in_=v(13), func=AF.Identity, scale=0.5),
                     reads=PARK, writes=[("der", l, 0)])
                P.op("act", lambda e: e.activation(out=der[:, 1, l, :], in_=v(14), func=AF.Identity, scale=0.5),
                     reads=PARK, writes=[("der", l, 1)])
                P.op("act", lambda e: e.activation(out=der[:, 2, l, :], in_=v(15), func=AF.Exp, scale=-1.0),
                     reads=PARK, writes=[("der", l, 2)])
                P.op("act", lambda e: e.activation(out=der[:, 3, l, :], in_=der[:, 2, l, :], func=AF.Ln,
                                                   bias=1.0, scale=1.0),
                     reads=[("der", l, 2)], writes=[("der", l, 3)])
                P.op("act", lambda e: e.activation(out=der[:, 2, l, :], in_=der[:, 3, l, :], func=AF.Identity,
                                                   scale=-8.0),
                     reads=[("der", l, 3)], writes=[("der", l, 2)])
                P.op("act", lambda e: e.activation(out=der[:, 3, l, :], in_=der[:, 2, l, :], func=AF.Identity,
                                                   scale=0.5),
                     reads=[("der", l, 2)], writes=[("der", l, 3)])
            mk(l)
        DERK = lambda l: [("der", l, i) for i in range(4)]

        def tiles_of_block(b):
            g0, g1 = b * 128, (b + 1) * 128
            res = []
            for u, (_, _, segs) in enumerate(TILES):
                for (_, sg0, n, _) in segs:
                    if sg0 < g1 and sg0 + n > g0:
                        res.append(u)
            return sorted(set(res))

        def x32keys(u):
            return [("x32", u, c) for c in range(NCH)]

        NBLK = NTOK // 128
        for b in range(NBLK):
            s = b % 6
            sap, skeys = XS[s]
            P.op("sp", lambda e, b=b, sap=sap: e.dma_start(out=sap, in_=xin[b * 128:(b + 1) * 128, :]),
                 writes=skeys, dma="stg%d" % s)
            pbk = 2 * (b % 4)

            def tr_in(pe, sap=sap, pbk=pbk):
                for c in range(NCH):
                    ins = pe.transpose(ps[:, pbk + c // 4, (c % 4) * 128:(c % 4 + 1) * 128],
                                       sap[:, c * 128:(c + 1) * 128], ident[:])
                return ins
            P.op("pe", tr_in, reads=skeys + [("ident",)], writes=[("ps", pbk), ("ps", pbk + 1)])
            wk = []
            for u in tiles_of_block(b):
                wk += x32keys(u)
            for h in range(2):
                eng = "act" if h == 0 else "dve"

                def ev(e, b=b, h=h, pbk=pbk, eng=eng):
                    o = x32[:, 4 * h:4 * h + 4, b * 128:(b + 1) * 128]
                    i = ps[:, pbk + h, :].rearrange("p (c n) -> p c n", n=128)
                    if eng == "act":
                        return e.activation(out=o, in_=i, func=AF.Identity)
                    return e.tensor_copy(out=o, in_=i)
                P.op(eng, ev, reads=[("ps", pbk + h)], writes=[k for k in wk if k[2] // 4 == h])

        wstate = {"n": 0}

        def w_unit_ap(l, kind, col):
            if kind == "in":
                src = w_in[l]
            elif kind == "ro":
                src = w_ro[l]
            elif kind == "co":
                src = w_co[l]
            else:
                src = w_o[l]
            return src.rearrange("(k p) n -> p k n", p=128)[:, :, col:col + UW]

        def load_group(units):
            slots = []
            for (l, kind, col) in units:
                s = wstate["n"] % NSLOT
                wstate["n"] += 1
                src = w_unit_ap(l, kind, col)
                P.op("pool", lambda e, s=s, src=src: e.dma_start(out=wring[:, s, :, :], in_=src),
                     writes=[("ws", s)], dma="w%d" % s)
                slots.append(s)
            return slots

        groups = []
        for l in range(depth):
            for pa in range(npass):
                for cp in range(4 if nphase >= 1 else 0):
                    groups.append(("rnn", l, pa, cp, [(l, "in", 0 * D + cp * UW), (l, "in", 1 * D + cp * UW)]))
                for cp in range(4 if nphase >= 2 else 0):
                    groups.append(("conv", l, pa, cp, [(l, "in", g * D + cp * UW) for g in (3, 4, 5, 2)]))
                for jp in range(4 if nphase >= 3 else 0):
                    groups.append(("ph2", l, pa, jp, [(l, "ro", jp * UW), (l, "co", jp * UW),
                                                      (l, "in", 6 * D + jp * UW), (l, "in", 7 * D + jp * UW)]))
                if nphase >= 4:
                    groups.append(("ph3", l, pa, 0, [(l, "o", jp * UW) for jp in range(4)]))
        gslots = {}

        def ensure_loaded(gi):
            if gi < len(groups) and gi not in gslots:
                gslots[gi] = load_group(groups[gi][4])

        def seg_views(u):
            segs = TILES[u][2]
            np_ = segs[0][2]
            has_s = len(segs) > 1
            return tile_n(u), np_, has_s

        def load_layer_small(l):
            P.op("pool", lambda e: e.dma_start(out=wab[:, 0, :, :], in_=w_a[l].rearrange("n i j -> i n j")),
                 writes=[("wab", 0)], dma="wab")
            P.op("pool", lambda e: e.dma_start(out=wab[:, 1, :, :], in_=w_x[l].rearrange("n i j -> i n j")),
                 writes=[("wab", 1)], dma="wab")
            P.op("sp", lambda e: e.dma_start(out=stg_in[0:96, :], in_=st_in[l]), writes=SINK, dma="stg1")

            def tr_st(pe):
                for c in range(NCH):
                    ins = pe.transpose(ps[:, 2 + c // 4, (c % 4) * 128:(c % 4) * 128 + 96],
                                       stg_in[0:96, c * 128:(c + 1) * 128], ident[0:96, 0:96])
                return ins
            P.op("pe", tr_st, reads=SINK + [("ident",)], writes=[("ps", 2), ("ps", 3)])
            for h in range(2):
                P.op("act", lambda e, h=h: e.activation(
                    out=stS[:, 4 * h:4 * h + 4, :],
                    in_=ps[:, 2 + h, :].rearrange("p (c n) -> p c n", n=128)[:, :, 0:96], func=AF.Identity),
                    reads=[("ps", 2 + h)], writes=[("stS", h)])
            for c in range(NCH):
                P.op("dve", lambda e, c=c: e.memset(c4h[:, c, :], 0.0), writes=[("c4h", c)])
                P.op("dve", lambda e, c=c: e.memset(c3h[:, c, :], 0.0), writes=[("c3h", c)])
                P.op("dve", lambda e, c=c: e.memset(hcar[:, c:c + 1], 0.0), writes=[("hcar", c)])

        def build_diag(l, c, par_, which):
            if which == "rnn":
                r0, n, base = 8, 5, 0
            else:
                r0, n, base = 16, 3, 5
            P.op("pool", lambda e: e.tensor_tensor(
                out=dg[:, par_, base:base + n, :],
                in0=identb[:].unsqueeze(1).broadcast_to([128, n, 128]),
                in1=parT[:, c, l * NPAR + r0:l * NPAR + r0 + n].unsqueeze(2).broadcast_to([128, n, 128]),
                op=ALU.mult),
                reads=[("identb",)] + PARK, writes=[("dg", par_, which)])


        itctr = {"rnn": 0, "conv": 0, "ph2": 0, "ph3": 0}

        nrm = stage[:].rearrange("p a d -> p (a d)").rearrange("p (s n) -> p s n", n=512)
        tff = tf[:].rearrange("p s n -> p (s n)")
        stg_out = tff[:, 10 * NMAX:10 * NMAX + D]
        stg_in = tff[:, 13 * NMAX:13 * NMAX + D]
        SOUTK = [("tf", 10), ("tf", 11), ("tf", 12)]
        SINK = [("tf", 13), ("tf", 14), ("tf", 15)]
        pending_norm = []

        def ph3_norm_chunk(u, si, j, ll):
            segs = TILES[u][2]
            Av, Bv = nrm[:, 2 * si, :], nrm[:, 2 * si + 1, :]
            SK = ("stg", si)
            for (kind, g0, n, l0) in segs:
                P.op("dve", lambda e, g0=g0, n=n, l0=l0: e.tensor_tensor(
                    out=x32[:, j, g0:g0 + n], in0=x32[:, j, g0:g0 + n], in1=Av[:, l0:l0 + n], op=ALU.mult),
                    reads=[("x32", u, j), SK], writes=[("x32", u, j)])
            for (kind, g0, n, l0) in segs:
                P.op("dve", lambda e, g0=g0, n=n, l0=l0: e.tensor_tensor(
                    out=x32[:, j, g0:g0 + n], in0=x32[:, j, g0:g0 + n], in1=Bv[:, l0:l0 + n], op=ALU.add),
                    reads=[("x32", u, j), SK], writes=[("x32", u, j)])
            for (kind, g0, n, l0) in segs:
                P.op("act", lambda e, g0=g0, n=n: e.activation(
                    out=x32[:, j, g0:g0 + n], in_=x32[:, j, g0:g0 + n], func=AF.Identity,
                    bias=par(ll, 20, j), scale=par(ll, 19, j)),
                    reads=[("x32", u, j)] + PARK, writes=[("x32", u, j)])

        def emit_deferred(j):
            for (u, si, ll) in pending_norm:
                ph3_norm_chunk(u, si, j, ll)
            if j == NCH - 1:
                del pending_norm[:]

        def flush_deferred():
            if pending_norm:
                for j in range(NCH):
                    emit_deferred(j)

        def run_layer(l, gi0):
            gi = gi0
            if small:
                load_layer_small(l)
            for pa in range(npass):
                tl_list = PASS_TILES[pa]
                rchain = {"f": None}
                for cp in range(4 if nphase >= 1 else 0):
                    ensure_loaded(gi)
                    ensure_loaded(gi + 1)
                    sxr, sgr = gslots[gi]
                    gi += 1
                    def rnn_body(u, pk, prev_chain, cp=cp, sxr=sxr, sgr=sgr):
                        N, np_, has_s = seg_views(u)
                        toff = TILES[u][1]
                        lt = u % 2
                        first = (u == 0)
                        last = (u == 5)
                        sbase = 3 + np_
                        dk = DERK(l)
                        ctx = []
                        for ci in range(2):
                            st_ = 4 * (2 * pk + ci)
                            ctx.append(dict(
                                c=2 * cp + ci, ci=ci, b_xr=ci, b_gr=2 + ci, b_xc=4 + 2 * pk + ci,
                                sg=tf[:, st_, :], ta=tf[:, st_ + 1, :], tx=tf[:, st_ + 2, :], av=tf[:, st_ + 3, :],
                                SG=("tf", st_), TA=("tf", st_ + 1), TX=("tf", st_ + 2), AV=("tf", st_ + 3),
                                hs=tf[:, 16 + ci, :], HS=("tf", 16 + ci),
                                xrb=tb[:, 2 * ci, :], xcb=tb[:, 2 * ci + 1, :], XRB=("tb", 2 * ci), XCB=("tb", 2 * ci + 1)))
                        b_ga, b_gx = 0, 1

                        def mm(pe, slot, bank, ci, N=N, toff=toff):
                            for k in range(NCH):
                                ins = pe.matmul(ps[:, bank, 0:N], wring[:, slot, k, ci * 128:(ci + 1) * 128],
                                                xb[:, k, toff:toff + N], start=(k == 0), stop=(k == NCH - 1))
                            return ins
                        for d in ctx:
                            P.op("pe", lambda pe, mm=mm, s=sxr, b=d["b_xr"], ci=d["ci"]: mm(pe, s, b, ci),
                                 reads=[("ws", sxr), ("xb", lt)], writes=[("ps", d["b_xr"])])
                            P.op("pe", lambda pe, mm=mm, s=sgr, b=d["b_gr"], ci=d["ci"]: mm(pe, s, b, ci),
                                 reads=[("ws", sgr), ("xb", lt)], writes=[("ps", d["b_gr"])])
                        if u == tl_list[0]:
                            for d in ctx:
                                build_diag(l, d["c"], d["ci"], "rnn")
                        for d in ctx:
                            c, xrb, XRB, b = d["c"], d["xrb"], d["XRB"], d["b_xr"]
                            P.op("pool", lambda e, xrb=xrb, c=c: e.tensor_copy(out=xrb[:, 0:3], in_=c4h[:, c, :]),
                                 reads=[("c4h", c)], writes=[XRB])
                            if has_s:
                                P.op("pool", lambda e, xrb=xrb, c=c: e.tensor_copy(
                                    out=xrb[:, sbase:sbase + 176].rearrange("p (s k) -> p s k", k=11)[:, :, 0:3],
                                    in_=stS[:, c, 0:48].rearrange("p (s k) -> p s k", k=3)),
                                    reads=[("stS", c // 4)], writes=[XRB])
                            P.op("dve", lambda e, xrb=xrb, c=c, b=b: e.tensor_scalar(
                                out=xrb[:, 3:3 + np_], in0=ps[:, b, 0:np_], scalar1=par(l, 0, c), scalar2=None,
                                op0=ALU.add),
                                reads=[("ps", b)] + PARK, writes=[XRB])
                            P.op("dve", lambda e, c=c, b=b: e.tensor_scalar(
                                out=c4h[:, c, :], in0=ps[:, b, np_ - 3:np_], scalar1=par(l, 0, c), scalar2=None,
                                op0=ALU.add),
                                reads=[("ps", b)] + PARK, writes=[("c4h", c)])
                            if has_s:
                                P.op("dve", lambda e, xrb=xrb, c=c, b=b: e.tensor_scalar(
                                    out=xrb[:, sbase:sbase + 176].rearrange("p (s k) -> p s k", k=11)[:, :, 3:11],
                                    in0=ps[:, b, np_:np_ + 128].rearrange("p (s k) -> p s k", k=8),
                                    scalar1=par(l, 0, c), scalar2=None, op0=ALU.add),
                                    reads=[("ps", b)] + PARK, writes=[XRB])
                                P.op("dve", lambda e, c=c, b=b: e.tensor_scalar(
                                    out=sout[:, c, 22:70].rearrange("p (s k) -> p s k", k=3),
                                    in0=ps[:, b, np_:np_ + 128].rearrange("p (s k) -> p s k", k=8)[:, :, 5:8],
                                    scalar1=par(l, 0, c), scalar2=None, op0=ALU.add),
                                    reads=[("ps", b)] + PARK, writes=[("sout", c)])
                            if last:
                                P.op("dve", lambda e, c=c: e.tensor_copy(out=sout[:, c, 1:4], in_=c4h[:, c, :]),
                                     reads=[("c4h", c)], writes=[("sout", c)])
                        for d in ctx:
                            c = d["c"]
                            P.op("act", lambda e, d=d, c=c: e.activation(
                                out=d["sg"][:, 0:N], in_=ps[:, d["b_gr"], 0:N], func=AF.Silu, bias=par(l, 1, c)),
                                reads=[("ps", d["b_gr"])] + PARK, writes=[d["SG"]])
                        for d in ctx:
                            def conv4(pe, d=d):
                                xrb, b_xc, ci = d["xrb"], d["b_xc"], d["ci"]
                                for k in range(4):
                                    pe.matmul(ps[:, b_xc, 0:np_], dg[:, ci, k, :], xrb[:, k:k + np_],
                                              start=(k == 0), stop=False)
                                ins = pe.matmul(ps[:, b_xc, 0:np_], dg[:, ci, 4, :], ones[:, 0:np_],
                                                start=False, stop=True)
                                if has_s:
                                    xs = xrb[:, sbase:sbase + 176].rearrange("p (s k) -> p s k", k=11)
                                    o = ps[:, b_xc, np_:np_ + 128].rearrange("p (s k) -> p s k", k=8)
                                    for k in range(4):
                                        pe.matmul(o, dg[:, ci, k, :], xs[:, :, k:k + 8], start=(k == 0), stop=False)
                                    ins = pe.matmul(o, dg[:, ci, 4, :],
                                                    ones[:, 0:128].rearrange("p (s k) -> p s k", k=8),
                                                    start=False, stop=True)
                                return ins
                            P.op("pe", conv4, reads=[d["XRB"], ("dg", d["ci"], "rnn"), ("ones",)],
                                 writes=[("ps", d["b_xc"])])
                        for d in ctx:
                            P.op("dve", lambda e, d=d: e.tensor_copy(out=d["xcb"][:, 0:N], in_=ps[:, d["b_xc"], 0:N]),
                                 reads=[("ps", d["b_xc"])], writes=[d["XCB"]])
                        for d in ctx:
                            c = d["c"]
                            P.op("pe", lambda pe, d=d, c=c: pe.matmul(ps[:, b_ga, 0:N], wab[:, 0, c, :],
                                                                      d["xcb"][:, 0:N], start=True, stop=True),
                                 reads=[d["XCB"], ("wab", 0)], writes=[("ps", b_ga)])
                            P.op("pe", lambda pe, d=d, c=c: pe.matmul(ps[:, b_gx, 0:N], wab[:, 1, c, :],
                                                                      d["xcb"][:, 0:N], start=True, stop=True),
                                 reads=[d["XCB"], ("wab", 1)], writes=[("ps", b_gx)])
                            P.op("act", lambda e, d=d, c=c: e.activation(out=d["ta"][:, 0:N], in_=ps[:, b_ga, 0:N],
                                                                         func=AF.Tanh, bias=der[:, 0, l, c:c + 1],
                                                                         scale=0.5),
                                 reads=[("ps", b_ga)] + dk, writes=[d["TA"]])
                            P.op("act", lambda e, d=d, c=c: e.activation(out=d["tx"][:, 0:N], in_=ps[:, b_gx, 0:N],
                                                                         func=AF.Tanh, bias=der[:, 1, l, c:c + 1],
                                                                         scale=0.5),
                                 reads=[("ps", b_gx)] + dk, writes=[d["TX"]])
                        if prev_chain is not None:
                            prev_chain()
                        for d in ctx:
                            c = d["c"]
                            P.op("act", lambda e, d=d, c=c: e.activation(
                                out=d["av"][:, 0:N], in_=d["ta"][:, 0:N], func=AF.Exp,
                                bias=der[:, 3, l, c:c + 1], scale=der[:, 3, l, c:c + 1]),
                                reads=[d["TA"]] + dk, writes=[d["AV"]])
                        s0_ = 8 * pk
                        TAs_ = [ctx[0]["TA"], ctx[1]["TA"]]
                        AVs_ = [ctx[0]["AV"], ctx[1]["AV"]]
                        P.op("act", lambda e: e.activation(out=tf[:, s0_ + 1:s0_ + 6:4, 0:N],
                                                           in_=tf[:, s0_ + 3:s0_ + 8:4, 0:N], func=AF.Square),
                             reads=AVs_, writes=TAs_)
                        P.op("act", lambda e: e.activation(out=tf[:, s0_ + 1:s0_ + 6:4, 0:N],
                                                           in_=tf[:, s0_ + 1:s0_ + 6:4, 0:N], func=AF.Sqrt,
                                                           bias=0.25, scale=-0.25),
                             reads=TAs_, writes=TAs_)
                        def chain():
                            for d in ctx:
                                c, ta, tx, av, hs, sg = d["c"], d["ta"], d["tx"], d["av"], d["hs"], d["sg"]
                                TA, TX, AV, HS, SG, b_xc = d["TA"], d["TX"], d["AV"], d["HS"], d["SG"], d["b_xc"]
                                if first:
                                    P.op("dve", lambda e, ta=ta: e.memset(ta[:, 0:1], 0.5), writes=[TA])
                                P.op("dve", lambda e, tx=tx, b_xc=b_xc: e.scalar_tensor_tensor(
                                    out=tx[:, 0:N], in0=tx[:, 0:N], scalar=1.0, in1=ps[:, b_xc, 0:N],
                                    op0=ALU.add, op1=ALU.mult),
                                    reads=[TX, ("ps", b_xc)], writes=[TX])
                                P.op("dve", lambda e, tx=tx, ta=ta: e.tensor_tensor(out=tx[:, 0:N], in0=tx[:, 0:N],
                                                                                    in1=ta[:, 0:N], op=ALU.mult),
                                     reads=[TX, TA], writes=[TX])
                                P.op("dve", lambda e, hs=hs, av=av, tx=tx, c=c: e.tensor_tensor_scan(
                                    out=hs[:, 0:np_], data0=av[:, 0:np_], data1=tx[:, 0:np_],
                                    initial=hcar[:, c:c + 1], op0=ALU.mult, op1=ALU.add),
                                    reads=[AV, TX, ("hcar", c)], writes=[HS])
                                P.op("dve", lambda e, hs=hs, c=c: e.tensor_copy(out=hcar[:, c:c + 1],
                                                                                in_=hs[:, np_ - 1:np_]),
                                     reads=[HS], writes=[("hcar", c)])
                                if last:
                                    P.op("dve", lambda e, hs=hs, c=c: e.tensor_copy(out=sout[:, c, 0:1],
                                                                                    in_=hs[:, np_ - 1:np_]),
                                         reads=[HS], writes=[("sout", c)])
                                if has_s:
                                    a_s = av[:, np_:np_ + 128].rearrange("p (s k) -> p s k", k=8)
                                    b_s = tx[:, np_:np_ + 128].rearrange("p (s k) -> p s k", k=8)
                                    h_s = hs[:, np_:np_ + 128].rearrange("p (s k) -> p s k", k=8)
                                    h0 = stS[:, c, 80:96]
                                    P.op("dve", lambda e, h_s=h_s, a_s=a_s, h0=h0: e.tensor_tensor(
                                        out=h_s[:, :, 0], in0=a_s[:, :, 0], in1=h0, op=ALU.mult),
                                        reads=[AV, ("stS", c // 4)], writes=[HS])
                                    P.op("dve", lambda e, h_s=h_s, b_s=b_s: e.tensor_tensor(
                                        out=b_s[:, :, 0], in0=b_s[:, :, 0], in1=h_s[:, :, 0], op=ALU.add),
                                        reads=[HS, TX], writes=[TX])
                                    P.op("dve", lambda e, a_s=a_s: e.memset(a_s[:, :, 0], 0.0), writes=[AV])
                                    P.op("dve", lambda e, hs=hs, av=av, tx=tx: e.tensor_tensor_scan(
                                        out=hs[:, np_:np_ + 128], data0=av[:, np_:np_ + 128],
                                        data1=tx[:, np_:np_ + 128], initial=0.0, op0=ALU.mult, op1=ALU.add),
                                        reads=[AV, TX], writes=[HS])
                                    P.op("dve", lambda e, h_s=h_s, c=c: e.tensor_copy(out=sout[:, c, 6:22],
                                                                                      in_=h_s[:, :, 7]),
                                         reads=[HS], writes=[("sout", c)])
                                P.op("dve", lambda e, hs=hs, sg=sg, c=c: e.tensor_tensor(
                                    out=pb[:, c, toff:toff + N], in0=hs[:, 0:N], in1=sg[:, 0:N], op=ALU.mult),
                                    reads=[HS, SG], writes=[("p", lt, c)])
                        return chain

                    for u in tl_list:
                        rchain["f"] = rnn_body(u, itctr["rnn"] % 2, rchain["f"])
                        itctr["rnn"] += 1

                if rchain["f"] is not None:
                    rchain["f"]()
                    rchain["f"] = None
                for cp in range(4 if nphase >= 2 else 0):
                    ensure_loaded(gi)
                    ensure_loaded(gi + 1)
                    scc, sch, sgc_, scb = gslots[gi]
                    gi += 1
                    its = [(u, c) for u in tl_list for c in (2 * cp, 2 * cp + 1)]
                    pend = None
                    for (u, c) in its:
                        it = itctr["conv"]
                        itctr["conv"] += 1
                        par_ = it % 2
                        civ = cp * 4 + its.index((u, c))
                        N, np_, has_s = seg_views(u)
                        toff = TILES[u][1]
                        lt = u % 2
                        cc_ = c % 2
                        b_cc, b_ch, b_gc, b_cb, b_v = 0, 1, 2 + par_, 4 + par_, 6 + par_
                        last = (u == 5)
                        ccs, sgc = tf[:, 6 * par_ + 0, :], tf[:, 6 * par_ + 1, :]
                        CCS, SGC = ("tf", 6 * par_ + 0), ("tf", 6 * par_ + 1)
                        ub = tb[:, 2 * par_, :]
                        UB = ("tb", 2 * par_)
                        sbase = 2 + np_

                        def mm(pe, slot, bank, N=N, toff=toff, cc_=cc_):
                            for k in range(NCH):
                                ins = pe.matmul(ps[:, bank, 0:N], wring[:, slot, k, cc_ * 128:(cc_ + 1) * 128],
                                                xb[:, k, toff:toff + N], start=(k == 0), stop=(k == NCH - 1))
                            return ins
                        for (s_, b_) in ((scc, b_cc), (sch, b_ch), (sgc_, b_gc), (scb, b_cb)):
                            P.op("pe", lambda pe, mm=mm, s=s_, b=b_: mm(pe, s, b),
                                 reads=[("ws", s_), ("xb", lt)], writes=[("ps", b_)])
                        if pend is not None:
                            pend()
                        if u == tl_list[0]:
                            build_diag(l, c, par_, "conv")
                        P.op("act", lambda e, ccs=ccs, c=c, N=N: e.activation(
                            out=ccs[:, 0:N], in_=ps[:, b_cc, 0:N], func=AF.Identity, bias=par(l, 3, c)),
                            reads=[("ps", b_cc)] + PARK, writes=[CCS])
                        P.op("act", lambda e, ub=ub, c=c: e.activation(out=ub[:, 0:2], in_=c3h[:, c, :],
                                                                       func=AF.Identity),
                             reads=[("c3h", c)], writes=[UB])
                        if has_s:
                            P.op("act", lambda e, ub=ub, c=c, sbase=sbase: e.activation(
                                out=ub[:, sbase:sbase + 160].rearrange("p (s k) -> p s k", k=10)[:, :, 0:2],
                                in_=stS[:, c, 48:80].rearrange("p (s k) -> p s k", k=2), func=AF.Identity),
                                reads=[("stS", c // 4)], writes=[UB])
                        P.op("dve", lambda e, ub=ub, ccs=ccs, c=c, np_=np_: e.scalar_tensor_tensor(
                            out=ub[:, 2:2 + np_], in0=ps[:, b_ch, 0:np_], scalar=par(l, 4, c), in1=ccs[:, 0:np_],
                            op0=ALU.add, op1=ALU.mult),
                            reads=[("ps", b_ch), CCS] + PARK, writes=[UB])
                        P.op("dve", lambda e, ccs=ccs, c=c, np_=np_: e.scalar_tensor_tensor(
                            out=c3h[:, c, :], in0=ps[:, b_ch, np_ - 2:np_], scalar=par(l, 4, c),
                            in1=ccs[:, np_ - 2:np_], op0=ALU.add, op1=ALU.mult),
                            reads=[("ps", b_ch), CCS] + PARK, writes=[("c3h", c)])
                        if has_s:
                            P.op("dve", lambda e, ub=ub, ccs=ccs, c=c, np_=np_, sbase=sbase: e.scalar_tensor_tensor(
                                out=ub[:, sbase:sbase + 160].rearrange("p (s k) -> p s k", k=10)[:, :, 2:10],
                                in0=ps[:, b_ch, np_:np_ + 128].rearrange("p (s k) -> p s k", k=8),
                                scalar=par(l, 4, c),
                                in1=ccs[:, np_:np_ + 128].rearrange("p (s k) -> p s k", k=8),
                                op0=ALU.add, op1=ALU.mult),
                                reads=[("ps", b_ch), CCS] + PARK, writes=[UB])
                            P.op("dve", lambda e, ccs=ccs, c=c, np_=np_: e.scalar_tensor_tensor(
                                out=sout[:, c, 70:102].rearrange("p (s k) -> p s k", k=2),
                                in0=ps[:, b_ch, np_:np_ + 128].rearrange("p (s k) -> p s k", k=8)[:, :, 6:8],
                                scalar=par(l, 4, c),
                                in1=ccs[:, np_:np_ + 128].rearrange("p (s k) -> p s k", k=8)[:, :, 6:8],
                                op0=ALU.add, op1=ALU.mult),
                                reads=[("ps", b_ch), CCS] + PARK, writes=[("sout", c)])
                        if last:
                            P.op("dve", lambda e, c=c: e.tensor_copy(out=sout[:, c, 4:6], in_=c3h[:, c, :]),
                                 reads=[("c3h", c)], writes=[("sout", c)])
                        P.op("act", lambda e, sgc=sgc, c=c, N=N, b=b_gc: e.activation(
                            out=sgc[:, 0:N], in_=ps[:, b, 0:N], func=AF.Silu, bias=par(l, 5, c)),
                            reads=[("ps", b_gc)] + PARK, writes=[SGC])
                        P.op("dve", lambda e, sgc=sgc, c=c, N=N, b=b_cb: e.scalar_tensor_tensor(
                            out=sgc[:, 0:N], in0=ps[:, b, 0:N], scalar=par(l, 2, c), in1=sgc[:, 0:N],
                            op0=ALU.add, op1=ALU.mult),
                            reads=[("ps", b_cb), SGC] + PARK, writes=[SGC])
                        if pending_norm:
                            (u_, si_, ll_) = pending_norm[civ % 2]
                            ph3_norm_chunk(u_, si_, civ // 2, ll_)
                            if civ == 2 * NCH - 1:
                                del pending_norm[:]

                        def pe2(u=u, c=c, par_=par_, N=N, np_=np_, has_s=has_s, ub=ub, UB=UB, b_v=b_v, sgc=sgc,
                                SGC=SGC, toff=toff, lt=lt, sbase=sbase):
                            def conv3(pe):
                                for k in range(3):
                                    ins = pe.matmul(ps[:, b_v, 0:np_], dg[:, par_, 5 + k, :], ub[:, k:k + np_],
                                                    start=(k == 0), stop=(k == 2))
                                if has_s:
                                    us = ub[:, sbase:sbase + 160].rearrange("p (s k) -> p s k", k=10)
                                    o = ps[:, b_v, np_:np_ + 128].rearrange("p (s k) -> p s k", k=8)
                                    for k in range(3):
                                        ins = pe.matmul(o, dg[:, par_, 5 + k, :], us[:, :, k:k + 8],
                                                        start=(k == 0), stop=(k == 2))
                                return ins
                            P.op("pe", conv3, reads=[UB, ("dg", par_, "conv")], writes=[("ps", b_v)])
                            P.op("dve", lambda e: e.tensor_tensor(out=qb[:, c, toff:toff + N], in0=sgc[:, 0:N],
                                                                  in1=ps[:, b_v, 0:N], op=ALU.mult),
                                 reads=[SGC, ("ps", b_v)], writes=[("q", lt, c)])
                        pend = pe2
                    if pend is not None:
                        pend()
                        pend = None

                for jp in range(4 if nphase >= 3 else 0):
                    ensure_loaded(gi)
                    ensure_loaded(gi + 1)
                    sro, sco, sg1, sg2 = gslots[gi]
                    gi += 1
                    for u in tl_list:
                        for j in (2 * jp, 2 * jp + 1):
                            it = itctr["ph2"]
                            itctr["ph2"] += 1
                            par_ = it % 2
                            N, np_, has_s = seg_views(u)
                            toff = TILES[u][1]
                            lt = u % 2
                            jj = j % 2
                            b_yr, b_yc, b_g1, b_g2 = par_, 2 + par_, 4 + par_, 6 + par_
                            s1, s2 = tf[:, 6 * par_ + 0, :], tf[:, 6 * par_ + 1, :]
                            S1K, S2K = ("tf", 6 * par_ + 0), ("tf", 6 * par_ + 1)

                            def mm(pe, slot, bank, src, N=N, toff=toff, jj=jj):
                                for k in range(NCH):
                                    ins = pe.matmul(ps[:, bank, 0:N], wring[:, slot, k, jj * 128:(jj + 1) * 128],
                                                    src[:, k, toff:toff + N], start=(k == 0), stop=(k == NCH - 1))
                                return ins
                            P.op("pe", lambda pe, mm=mm, s=sg1, b=b_g1: mm(pe, s, b, xb),
                                 reads=[("ws", sg1), ("xb", lt)], writes=[("ps", b_g1)])
                            P.op("pe", lambda pe, mm=mm, s=sg2, b=b_g2: mm(pe, s, b, xb),
                                 reads=[("ws", sg2), ("xb", lt)], writes=[("ps", b_g2)])
                            P.op("pe", lambda pe, mm=mm, s=sro, b=b_yr: mm(pe, s, b, pb),
                                 reads=[("ws", sro)] + [("p", lt, c) for c in range(NCH)], writes=[("ps", b_yr)])
                            P.op("pe", lambda pe, mm=mm, s=sco, b=b_yc: mm(pe, s, b, qb),
                                 reads=[("ws", sco)] + [("q", lt, c) for c in range(NCH)], writes=[("ps", b_yc)])
                            P.op("act", lambda e, s1=s1, j=j, N=N, b=b_g1: e.activation(
                                out=s1[:, 0:N], in_=ps[:, b, 0:N], func=AF.Sigmoid, bias=par(l, 6, j)),
                                reads=[("ps", b_g1)] + PARK, writes=[S1K])
                            P.op("act", lambda e, s2=s2, j=j, N=N, b=b_g2: e.activation(
                                out=s2[:, 0:N], in_=ps[:, b, 0:N], func=AF.Sigmoid, bias=par(l, 7, j)),
                                reads=[("ps", b_g2)] + PARK, writes=[S2K])
                            P.op("dve", lambda e, s1=s1, N=N, b=b_yr: e.tensor_tensor(
                                out=s1[:, 0:N], in0=s1[:, 0:N], in1=ps[:, b, 0:N], op=ALU.mult),
                                reads=[S1K, ("ps", b_yr)], writes=[S1K])
                            P.op("dve", lambda e, s2=s2, N=N, b=b_yc: e.tensor_tensor(
                                out=s2[:, 0:N], in0=s2[:, 0:N], in1=ps[:, b, 0:N], op=ALU.mult),
                                reads=[S2K, ("ps", b_yc)], writes=[S2K])
                            P.op("dve", lambda e, s1=s1, s2=s2, N=N, j=j, toff=toff: e.tensor_tensor(
                                out=mb[:, j, toff:toff + N], in0=s1[:, 0:N], in1=s2[:, 0:N], op=ALU.add),
                                reads=[S1K, S2K], writes=[("m", lt, j)])

                ensure_loaded(gi)
                ensure_loaded(gi + 1)
                swo = gslots[gi]
                gi += 1
                nxt = (l, pa + 1) if pa < npass - 1 else ((l + 1, 0) if l + 1 < depth else None)
                def ph3_loop1(u, si, swo=swo, cast_pa=None, hook=None):
                    b_S1, b_S2 = 2 + 2 * si, 3 + 2 * si
                    N, np_, has_s = seg_views(u)
                    toff = TILES[u][1]
                    lt = u % 2
                    segs = TILES[u][2]
                    pend_s = None
                    for j in range(NCH):
                        it = itctr["ph3"]
                        itctr["ph3"] += 1
                        par_ = it % 2
                        b_o = par_
                        vb, vsq = tb[:, 2 * par_, :], tb[:, 2 * par_ + 1, :]
                        VB, VSQ = ("tb", 2 * par_), ("tb", 2 * par_ + 1)

                        def mmo(pe, j=j, b_o=b_o):
                            slot = swo[j // 2]
                            jj = j % 2
                            for k in range(NCH):
                                ins = pe.matmul(ps[:, b_o, 0:N], wring[:, slot, k, jj * 128:(jj + 1) * 128],
                                                mb[:, k, toff:toff + N], start=(k == 0), stop=(k == NCH - 1))
                            return ins
                        P.op("pe", mmo, reads=[("ws", swo[j // 2])] + [("m", lt, k) for k in range(NCH)],
                             writes=[("ps", b_o)])
                        if pend_s is not None:
                            pend_s()
                        for (kind, g0, n, l0) in segs:
                            P.op("dve", lambda e, j=j, g0=g0, n=n, l0=l0, b_o=b_o: e.scalar_tensor_tensor(
                                out=x32[:, j, g0:g0 + n], in0=x32[:, j, g0:g0 + n], scalar=ALPHA,
                                in1=ps[:, b_o, l0:l0 + n], op0=ALU.mult, op1=ALU.add),
                                reads=[("x32", u, j), ("ps", b_o)], writes=[("x32", u, j)])
                        for (kind, g0, n, l0) in segs:
                            P.op("act", lambda e, j=j, g0=g0, n=n, l0=l0, vb=vb: e.activation(
                                out=vb[:, l0:l0 + n], in_=x32[:, j, g0:g0 + n], func=AF.Copy),
                                reads=[("x32", u, j)], writes=[VB])
                            P.op("act", lambda e, j=j, g0=g0, n=n, l0=l0, vsq=vsq: e.activation(
                                out=vsq[:, l0:l0 + n], in_=x32[:, j, g0:g0 + n], func=AF.Square),
                                reads=[("x32", u, j)], writes=[VSQ])

                        if cast_pa is not None:
                            cast_xb(cast_pa, "act", j)

                        def smm(j=j, vb=vb, vsq=vsq, VB=VB, VSQ=VSQ):
                            P.op("pe", lambda pe: pe.matmul(
                                ps[:, b_S1, 0:N], ones[:, 0:128], vb[:, 0:N], start=(j == 0), stop=(j == NCH - 1)),
                                reads=[VB, ("ones",)], writes=[("ps", b_S1)])
                            P.op("pe", lambda pe: pe.matmul(
                                ps[:, b_S2, 0:N], ones[:, 0:128], vsq[:, 0:N], start=(j == 0), stop=(j == NCH - 1)),
                                reads=[VSQ, ("ones",)], writes=[("ps", b_S2)])
                        pend_s = smm
                        if hook is not None and j == 1:
                            hook()
                    pend_s()

                def ph3_stats(u, si):
                    N, np_, has_s = seg_views(u)
                    b_S1, b_S2 = 2 + 2 * si, 3 + 2 * si
                    mean, msq = tf[:, 8, :], tf[:, 9, :]
                    MEAN, MSQ = ("tf", 8), ("tf", 9)
                    Av, Bv = nrm[:, 2 * si, :], nrm[:, 2 * si + 1, :]
                    SK = ("stg", si)
                    P.op("dve", lambda e: e.tensor_scalar(out=mean[:, 0:N], in0=ps[:, b_S1, 0:N],
                                                          scalar1=1.0 / D, scalar2=None, op0=ALU.mult),
                         reads=[("ps", b_S1)], writes=[MEAN])
                    P.op("act", lambda e: e.activation(out=msq[:, 0:N], in_=ps[:, b_S1, 0:N], func=AF.Square,
                                                       scale=1.0 / D),
                         reads=[("ps", b_S1)], writes=[MSQ])
                    P.op("dve", lambda e: e.scalar_tensor_tensor(out=msq[:, 0:N], in0=ps[:, b_S2, 0:N],
                                                                 scalar=1.0 / D, in1=msq[:, 0:N],
                                                                 op0=ALU.mult, op1=ALU.subtract),
                         reads=[("ps", b_S2), MSQ], writes=[MSQ])
                    P.op("act", lambda e: e.activation(out=msq[:, 0:N], in_=msq[:, 0:N], func=AF.Sqrt,
                                                       bias=epst[:, 0:1], scale=1.0),
                         reads=[MSQ, ("eps",)], writes=[MSQ])
                    P.op("dve", lambda e: e.reciprocal(out=Av[:, 0:N], in_=msq[:, 0:N]),
                         reads=[MSQ], writes=[SK])
                    P.op("dve", lambda e: e.scalar_tensor_tensor(out=Bv[:, 0:N], in0=mean[:, 0:N], scalar=-1.0,
                                                                 in1=Av[:, 0:N], op0=ALU.mult, op1=ALU.mult),
                         reads=[MEAN, SK], writes=[SK])

                u0, u1 = tl_list
                flush_deferred()
                ph3_loop1(u0, 0, cast_pa=(nxt[1] if nxt is not None else None))
                ph3_loop1(u1, 1, hook=lambda: ph3_stats(u0, 0))
                ph3_stats(u1, 1)
                pending_norm.extend([(u0, 0, l), (u1, 1, l)])
                if pa == npass - 1 and l == depth - 1:
                    flush_deferred()

            return gi

        def cast_xb(pa, eng="pool", only_k=None):
            for u in PASS_TILES[pa]:
                toff = TILES[u][1]
                lt = u % 2
                for (kind, g0, n, l0) in TILES[u][2]:
                    ks = range(NCH) if only_k is None else [only_k]
                    if eng == "pool":
                        P.op("pool", lambda e, g0=g0, n=n, o=toff + l0: e.tensor_copy(
                            out=xb[:, :, o:o + n], in_=x32[:, :, g0:g0 + n]),
                            reads=x32keys(u), writes=[("xb", lt)])
                    else:
                        for k in ks:
                            P.op("act", lambda e, g0=g0, n=n, o=toff + l0, k=k: e.activation(
                                out=xb[:, k, o:o + n], in_=x32[:, k, g0:g0 + n], func=AF.Copy),
                                reads=[("x32", u, k)], writes=[("xb", lt)])

        def store_states(l):
            def tr(pe):
                for c in range(NCH):
                    ins = pe.transpose(ps[0:102, 4 + c // 4, (c % 4) * 128:(c % 4 + 1) * 128], sout[:, c, :], ident[:])
                return ins
            P.op("pe", tr, reads=[("sout", c) for c in range(NCH)] + [("ident",)], writes=[("ps", 4), ("ps", 5)])
            for h in range(2):
                P.op("act", lambda e, h=h: e.activation(out=stg_out[0:102, 512 * h:512 * (h + 1)],
                                                        in_=ps[0:102, 4 + h, :], func=AF.Identity),
                     reads=[("ps", 4 + h)], writes=SOUTK)
            P.op("sp", lambda e: e.dma_start(out=st_out[l], in_=stg_out[0:102, :]),
                 reads=SOUTK, writes=[("st_out", l)], dma="out0")

        ensure_loaded(0)
        ensure_loaded(1)
        cast_xb(0)
        gi = 0
        for l in range(depth):
            gi = run_layer(l, gi)
            store_states(l)

        for b in range(NBLK):
            s = b % 6
            sap, skeys = XS[s]
            pbk = 2 * (b % 4)
            rk = []
            for u in tiles_of_block(b):
                rk += x32keys(u)

            def tr_out(pe, b=b, pbk=pbk):
                for c in range(NCH):
                    ins = pe.transpose(ps[:, pbk + c // 4, (c % 4) * 128:(c % 4 + 1) * 128],
                                       x32[:, c, b * 128:(b + 1) * 128], ident[:])
                return ins
            P.op("pe", tr_out, reads=rk + [("ident",)], writes=[("ps", pbk), ("ps", pbk + 1)])
            for h in range(2):
                eng = "act" if h == 0 else "dve"

                def ev(e, sap=sap, h=h, pbk=pbk, eng=eng):
                    o = sap[:, 512 * h:512 * (h + 1)]
                    i = ps[:, pbk + h, :]
                    if eng == "act":
                        return e.activation(out=o, in_=i, func=AF.Identity)
                    return e.tensor_copy(out=o, in_=i)
                P.op(eng, ev, reads=[("ps", pbk + h)], writes=skeys)
            P.op("sp", lambda e, b=b, sap=sap: e.dma_start(out=y[b * 128:(b + 1) * 128, :], in_=sap),
                 reads=skeys, writes=[("y", b)], dma="out%d" % s)
        P.op("sp", lambda e: e.nop(), reads=[("y", b) for b in range(NBLK)] + [("st_out", l) for l in range(depth)])

        P.finalize()
        with nc.Block() as block:
            @block.tensor
            def _(e):
                P.emit_stream("pe", e, esem, dsem)

            @block.scalar
            def _(e):
                P.emit_stream("act", e, esem, dsem)

            @block.vector
            def _(e):
                P.emit_stream("dve", e, esem, dsem)

            @block.gpsimd
            def _(e):
                P.emit_stream("pool", e, esem, dsem)

            @block.sync
            def _(e):
                P.emit_stream("sp", e, esem, dsem)
    return nc


_NC_CACHE = {}


def _prep_inputs(inputs):
    f = lambda k: np.ascontiguousarray(np.asarray(inputs[k], dtype=np.float32))
    x_prompt, x_sample = f("x_prompt"), f("x_sample")
    s_h, s_c4, s_c3 = f("state_rglru"), f("state_conv4"), f("state_conv3")
    rows = []
    for l in range(DEPTH):
        rows.append(f("b_in")[l].reshape(8, D))
        rows.append(f("conv4_w")[l].reshape(4, D))
        rows.append(f("conv4_b")[l].reshape(1, D))
        rows.append(f("b_rg_a")[l].reshape(1, D))
        rows.append(f("b_rg_x")[l].reshape(1, D))
        rows.append(f("rg_lambda")[l].reshape(1, D))
        rows.append(f("conv3_w")[l].reshape(3, D))
        rows.append(f("ln_g")[l].reshape(1, D))
        rows.append(f("ln_b")[l].reshape(1, D))
    params = np.ascontiguousarray(np.concatenate(rows, axis=0))
    shared = {
        "params": params, "w_in": f("w_in"), "w_ro": f("w_rnn_out"), "w_co": f("w_conv_out"), "w_o": f("w_out"),
        "w_a": f("w_rg_a"), "w_x": f("w_rg_x"), "ident": np.eye(128, dtype=np.float32),
    }
    in_maps = []
    for c in range(8):
        sl = slice(16 * c, 16 * c + 16)
        xin = np.concatenate([x_prompt[c], x_sample[sl].reshape(128, D)], axis=0)
        st = np.concatenate([s_c4[:, sl].reshape(DEPTH, 48, D), s_c3[:, sl].reshape(DEPTH, 32, D),
                             s_h[:, sl].reshape(DEPTH, 16, D)], axis=1)
        m = dict(shared)
        m["xin"] = np.ascontiguousarray(xin)
        m["st_in"] = np.ascontiguousarray(st)
        in_maps.append(m)
    return in_maps


def kernel(**inputs):
    if "nc" not in _NC_CACHE:
        _NC_CACHE["nc"] = build_nc()
    nc = _NC_CACHE["nc"]
    in_maps = _prep_inputs(inputs)
    res = run_bass_kernel_spmd(nc, in_maps, core_ids=list(range(8)))
    ys = [np.asarray(r["y"]) for r in res.results]
    sts = [np.asarray(r["st_out"]) for r in res.results]
    y_prompt = np.stack([yy[0:NPR] for yy in ys], axis=0)
    y_sample = np.concatenate([yy[NPR:].reshape(16, 8, D) for yy in ys], axis=0)
    ph = np.stack([s[:, 0] for s in sts], axis=1)
    pc4 = np.stack([s[:, 1:4] for s in sts], axis=1)
    pc3 = np.stack([s[:, 4:6] for s in sts], axis=1)
    sh = np.concatenate([s[:, 6:22] for s in sts], axis=1)
    sc4 = np.concatenate([s[:, 22:70].reshape(DEPTH, 16, 3, D) for s in sts], axis=1)
    sc3 = np.concatenate([s[:, 70:102].reshape(DEPTH, 16, 2, D) for s in sts], axis=1)
    f32 = lambda a: np.ascontiguousarray(a, dtype=np.float32)
    return (f32(y_prompt), f32(y_sample), f32(ph), f32(pc4), f32(pc3), f32(sh), f32(sc4), f32(sc3))
```

```python
import contextlib
import numpy as np
import concourse.bass as bass
import concourse.mybir as mybir
from concourse.bass_utils import run_bass_kernel_spmd

F32 = mybir.dt.float32
BF16 = mybir.dt.bfloat16
AF = mybir.ActivationFunctionType
ALU = mybir.AluOpType

DEPTH = 4
D = 1024
NCH = 8
NPR = 2048
NSM = 128
NTOK = NPR + NSM
ALPHA = (2.0 * DEPTH) ** 0.25
EPS = 1e-5
NMAX = 384
NSLOT = 8
UW = 256
NPAR = 21

TILES = [
    (0, 0, [("P", 0, 384, 0)]),
    (0, 384, [("P", 384, 256, 0), ("S", 2048, 128, 256)]),
    (1, 0, [("P", 640, 352, 0)]),
    (1, 352, [("P", 992, 352, 0)]),
    (2, 0, [("P", 1344, 352, 0)]),
    (2, 352, [("P", 1696, 352, 0)]),
]
PASS_TILES = {0: [0, 1], 1: [2, 3], 2: [4, 5]}
PASSW = 768


def tile_n(u):
    return sum(s[2] for s in TILES[u][2])


class _Op:
    __slots__ = ("eng", "fn", "deps", "sig", "cnt", "dma", "dma_ord")


class Prog:
    ENGS = ("pe", "act", "dve", "pool", "sp")

    def __init__(self):
        self.streams = {e: [] for e in self.ENGS}
        self.last_w = {}
        self.readers = {}
        self.dma_last = {}
        self.dma_cnt = {}

    def op(self, eng, fn, reads=(), writes=(), dma=None):
        o = _Op()
        o.eng, o.fn, o.sig, o.cnt, o.dma, o.dma_ord = eng, fn, False, 0, dma, 0
        deps = []
        for r in reads:
            w = self.last_w.get(r)
            if w is not None:
                deps.append(w)
            if r[0] == "ps":
                rd = self.readers.get(r)
                if rd:
                    deps.extend(v for k, v in rd[0].items() if k != eng)
        for r in writes:
            w = self.last_w.get(r)
            if w is not None:
                deps.append(w)
            rd = self.readers.get(r)
            if rd:
                deps.extend(rd[0].values())
                deps.extend(rd[1])
        if dma is not None:
            prev = self.dma_last.get(dma)
            if prev is not None:
                deps.append(prev)
            self.dma_last[dma] = o
            self.dma_cnt[dma] = self.dma_cnt.get(dma, 0) + 1
            o.dma_ord = self.dma_cnt[dma]
        seen = set()
        o.deps = []
        for d in deps:
            if id(d) in seen or d is o:
                continue
            seen.add(id(d))
            if eng == "pe" and d.eng == "pe" and d.dma is None and dma is None:
                continue
            d.sig = True
            o.deps.append(d)
        for r in writes:
            self.last_w[r] = o
            self.readers[r] = ({}, [])
        for r in reads:
            rd = self.readers.setdefault(r, ({}, []))
            if dma is None:
                rd[0][eng] = o
            else:
                rd[1].append(o)
        self.streams[eng].append(o)
        return o

    def finalize(self):
        for e in self.ENGS:
            c = 0
            for o in self.streams[e]:
                if o.sig and o.dma is None:
                    c += 1
                o.cnt = c

    def emit_stream(self, e, eng, esem, dsem):
        waited = {}
        for o in self.streams[e]:
            for d in o.deps:
                if d.dma is not None:
                    key, sem, val = ("d", d.dma), dsem[d.dma], 16 * d.dma_ord
                else:
                    key, sem, val = ("e", d.eng), esem[d.eng], d.cnt
                if waited.get(key, 0) < val:
                    eng.wait_ge(sem, val)
                    waited[key] = val
            ins = o.fn(eng)
            if o.dma is not None:
                ins.then_inc(dsem[o.dma], 16)
            elif o.sig:
                ins.then_inc(esem[e], 1)


def build_nc(depth=DEPTH, npass=3, nphase=4, small=True, p3=9):
    nc = bass.Bass("TRN2", target_bir_lowering=False)
    xin = nc.dram_tensor("xin", [NTOK, D], F32, kind="ExternalInput").ap()
    st_in = nc.dram_tensor("st_in", [DEPTH, 96, D], F32, kind="ExternalInput").ap()
    params = nc.dram_tensor("params", [DEPTH * NPAR, D], F32, kind="ExternalInput").ap()
    w_in = nc.dram_tensor("w_in", [DEPTH, D, 8 * D], F32, kind="ExternalInput").ap()
    w_ro = nc.dram_tensor("w_ro", [DEPTH, D, D], F32, kind="ExternalInput").ap()
    w_co = nc.dram_tensor("w_co", [DEPTH, D, D], F32, kind="ExternalInput").ap()
    w_o = nc.dram_tensor("w_o", [DEPTH, D, D], F32, kind="ExternalInput").ap()
    w_a = nc.dram_tensor("w_a", [DEPTH, NCH, 128, 128], F32, kind="ExternalInput").ap()
    w_x = nc.dram_tensor("w_x", [DEPTH, NCH, 128, 128], F32, kind="ExternalInput").ap()
    ident_d = nc.dram_tensor("ident", [128, 128], F32, kind="ExternalInput").ap()
    y = nc.dram_tensor("y", [NTOK, D], F32, kind="ExternalOutput").ap()
    st_out = nc.dram_tensor("st_out", [DEPTH, 102, D], F32, kind="ExternalOutput").ap()

    P = Prog()
    es = contextlib.ExitStack()
    with es:
        def sb(name, shape, dt):
            return es.enter_context(nc.sbuf_tensor(name, shape, dt))

        x32 = sb("x32", [128, NCH, NTOK], F32)
        xb = sb("xb", [128, NCH, PASSW], BF16)
        pb = sb("pb", [128, NCH, PASSW], BF16)
        qb = sb("qb", [128, NCH, PASSW], BF16)
        mb = sb("mb", [128, NCH, PASSW], BF16)
        wring = sb("wring", [128, NSLOT, NCH, UW], BF16)
        wab = sb("wab", [128, 2, NCH, 128], BF16)
        dg = sb("dg", [128, 2, 8, 128], BF16)
        parT = sb("parT", [128, NCH, DEPTH * NPAR], F32)
        der = sb("der", [128, 4, DEPTH, NCH], F32)
        ident = sb("identf", [128, 128], F32)
        identb = sb("identb", [128, 128], BF16)
        ones = sb("ones", [128, NMAX], BF16)
        stS = sb("stS", [128, NCH, 96], F32)
        sout = sb("sout", [128, NCH, 102], F32)
        c4h = sb("c4h", [128, NCH, 3], F32)
        c3h = sb("c3h", [128, NCH, 2], F32)
        hcar = sb("hcar", [128, NCH], F32)
        epst = sb("epst", [128, 1], F32)
        stage = sb("stage", [128, 2, D], F32)
        NTF, NTB = 18, 4
        tf = sb("tf", [128, NTF, NMAX], F32)
        tb = sb("tb", [128, NTB, 448], BF16)
        ps = es.enter_context(nc.psum_tensor("ps", [128, 8, 512], F32))

        tff0 = tf[:].rearrange("p s n -> p (s n)")
        XS = [(stage[:, 0, :], [("stg", 0)]), (stage[:, 1, :], [("stg", 1)])]
        for i_ in range(4):
            XS.append((tff0[:, i_ * 3 * NMAX:i_ * 3 * NMAX + D], [("tf", 3 * i_ + k_) for k_ in range(3)]))
        esem = {e: es.enter_context(nc.semaphore("s_" + e)) for e in ("pe", "act", "dve", "pool")}
        dnames = (["w%d" % i for i in range(NSLOT)] + ["stg%d" % i for i in range(6)] + ["misc", "wab"]
                  + ["out%d" % i for i in range(6)])
        dsem = {n: es.enter_context(nc.semaphore("d_" + n)) for n in dnames}

        def par(l, r, c):
            return parT[:, c, l * NPAR + r: l * NPAR + r + 1]

        P.op("sp", lambda e: e.dma_start(out=ident[:], in_=ident_d), writes=[("ident",)], dma="misc")
        P.op("sp", lambda e: e.dma_start(out=stage[0:DEPTH * NPAR, 0, :], in_=params),
             writes=[("stg", 0)], dma="stg0")
        P.op("dve", lambda e: e.tensor_copy(out=identb[:], in_=ident[:]), reads=[("ident",)],
             writes=[("identb",)])
        P.op("dve", lambda e: e.memset(ones[:], 1.0), writes=[("ones",)])
        P.op("dve", lambda e: e.memset(epst[:], EPS), writes=[("eps",)])

        def tr_params(pe):
            for c in range(NCH):
                ins = pe.transpose(ps[:, c // 4, (c % 4) * 128:(c % 4) * 128 + DEPTH * NPAR],
                                   stage[0:DEPTH * NPAR, 0, c * 128:(c + 1) * 128],
                                   ident[0:DEPTH * NPAR, 0:DEPTH * NPAR])
            return ins
        P.op("pe", tr_params, reads=[("stg", 0), ("ident",)], writes=[("ps", 0), ("ps", 1)])
        for h in range(2):
            P.op("act", lambda e, h=h: e.activation(
                out=parT[:, 4 * h:4 * h + 4, :],
                in_=ps[:, h, :].rearrange("p (c n) -> p c n", n=128)[:, :, 0:DEPTH * NPAR],
                func=AF.Identity), reads=[("ps", h)], writes=[("parT", h)])
        PARK = [("parT", 0), ("parT", 1)]
        for l in range(DEPTH):
            def mk(l):
                def v(ri):
                    return parT[:, :, l * NPAR + ri]
                P.op("act", lambda e: e.activation(out=der[:, 0, l, :], in_=v(13), func=AF.Identity, scale=0.5),
                     reads=PARK, writes=[("der", l, 0)])
                P.op("act", lambda e: e.activation(out=der[:, 1, l, :], in_=v(14), func=AF.Identity, scale=0.5),
                     reads=PARK, writes=[("der", l, 1)])
                P.op("act", lambda e: e.activation(out=der[:, 2, l, :], in_=v(15), func=AF.Exp, scale=-1.0),
                     reads=PARK, writes=[("der", l, 2)])
                P.op("act", lambda e: e.activation(out=der[:, 3, l, :], in_=der[:, 2, l, :], func=AF.Ln,
                                                   bias=1.0, scale=1.0),
                     reads=[("der", l, 2)], writes=[("der", l, 3)])
                P.op("act", lambda e: e.activation(out=der[:, 2, l, :], in_=der[:, 3, l, :], func=AF.Identity,
                                                   scale=-8.0),
                     reads=[("der", l, 3)], writes=[("der", l, 2)])
                P.op("act", lambda e: e.activation(out=der[:, 3, l, :], in_=der[:, 2, l, :], func=AF.Identity,
                                                   scale=0.5),
                     reads=[("der", l, 2)], writes=[("der", l, 3)])
            mk(l)
        DERK = lambda l: [("der", l, i) for i in range(4)]

        def tiles_of_block(b):
            g0, g1 = b * 128, (b + 1) * 128
            res = []
            for u, (_, _, segs) in enumerate(TILES):
                for (_, sg0, n, _) in segs:
                    if sg0 < g1 and sg0 + n > g0:
                        res.append(u)
            return sorted(set(res))

        def x32keys(u):
            return [("x32", u, c) for c in range(NCH)]

        NBLK = NTOK // 128
        for b in range(NBLK):
            s = b % 6
            sap, skeys = XS[s]
            P.op("sp", lambda e, b=b, sap=sap: e.dma_start(out=sap, in_=xin[b * 128:(b + 1) * 128, :]),
                 writes=skeys, dma="stg%d" % s)
            pbk = 2 * (b % 4)

            def tr_in(pe, sap=sap, pbk=pbk):
                for c in range(NCH):
                    ins = pe.transpose(ps[:, pbk + c // 4, (c % 4) * 128:(c % 4 + 1) * 128],
                                       sap[:, c * 128:(c + 1) * 128], ident[:])
                return ins
            P.op("pe", tr_in, reads=skeys + [("ident",)], writes=[("ps", pbk), ("ps", pbk + 1)])
            wk = []
            for u in tiles_of_block(b):
                wk += x32keys(u)
            for h in range(2):
                eng = "act" if h == 0 else "dve"

                def ev(e, b=b, h=h, pbk=pbk, eng=eng):
                    o = x32[:, 4 * h:4 * h + 4, b * 128:(b + 1) * 128]
                    i = ps[:, pbk + h, :].rearrange("p (c n) -> p c n", n=128)
                    if eng == "act":
                        return e.activation(out=o, in_=i, func=AF.Identity)
                    return e.tensor_copy(out=o, in_=i)
                P.op(eng, ev, reads=[("ps", pbk + h)], writes=[k for k in wk if k[2] // 4 == h])

        wstate = {"n": 0}

        def w_unit_ap(l, kind, col):
            if kind == "in":
                src = w_in[l]
            elif kind == "ro":
                src = w_ro[l]
            elif kind == "co":
                src = w_co[l]
            else:
                src = w_o[l]
            return src.rearrange("(k p) n -> p k n", p=128)[:, :, col:col + UW]

        def load_group(units):
            slots = []
            for (l, kind, col) in units:
                s = wstate["n"] % NSLOT
                wstate["n"] += 1
                src = w_unit_ap(l, kind, col)
                P.op("pool", lambda e, s=s, src=src: e.dma_start(out=wring[:, s, :, :], in_=src),
                     writes=[("ws", s)], dma="w%d" % s)
                slots.append(s)
            return slots

        groups = []
        for l in range(depth):
            for pa in range(npass):
                for cp in range(4 if nphase >= 1 else 0):
                    groups.append(("rnn", l, pa, cp, [(l, "in", 0 * D + cp * UW), (l, "in", 1 * D + cp * UW)]))
                for cp in range(4 if nphase >= 2 else 0):
                    groups.append(("conv", l, pa, cp, [(l, "in", g * D + cp * UW) for g in (3, 4, 5, 2)]))
                for jp in range(4 if nphase >= 3 else 0):
                    groups.append(("ph2", l, pa, jp, [(l, "ro", jp * UW), (l, "co", jp * UW),
                                                      (l, "in", 6 * D + jp * UW), (l, "in", 7 * D + jp * UW)]))
                if nphase >= 4:
                    groups.append(("ph3", l, pa, 0, [(l, "o", jp * UW) for jp in range(4)]))
        gslots = {}

        def ensure_loaded(gi):
            if gi < len(groups) and gi not in gslots:
                gslots[gi] = load_group(groups[gi][4])

        def seg_views(u):
            segs = TILES[u][2]
            np_ = segs[0][2]
            has_s = len(segs) > 1
            return tile_n(u), np_, has_s

        def load_layer_small(l):
            P.op("pool", lambda e: e.dma_start(out=wab[:, 0, :, :], in_=w_a[l].rearrange("n i j -> i n j")),
                 writes=[("wab", 0)], dma="wab")
            P.op("pool", lambda e: e.dma_start(out=wab[:, 1, :, :], in_=w_x[l].rearrange("n i j -> i n j")),
                 writes=[("wab", 1)], dma="wab")
            P.op("sp", lambda e: e.dma_start(out=stg_in[0:96, :], in_=st_in[l]), writes=SINK, dma="stg1")

            def tr_st(pe):
                for c in range(NCH):
                    ins = pe.transpose(ps[:, 2 + c // 4, (c % 4) * 128:(c % 4) * 128 + 96],
                                       stg_in[0:96, c * 128:(c + 1) * 128], ident[0:96, 0:96])
                return ins
            P.op("pe", tr_st, reads=SINK + [("ident",)], writes=[("ps", 2), ("ps", 3)])
            for h in range(2):
                P.op("act", lambda e, h=h: e.activation(
                    out=stS[:, 4 * h:4 * h + 4, :],
                    in_=ps[:, 2 + h, :].rearrange("p (c n) -> p c n", n=128)[:, :, 0:96], func=AF.Identity),
                    reads=[("ps", 2 + h)], writes=[("stS", h)])
            for c in range(NCH):
                P.op("dve", lambda e, c=c: e.memset(c4h[:, c, :], 0.0), writes=[("c4h", c)])
                P.op("dve", lambda e, c=c: e.memset(c3h[:, c, :], 0.0), writes=[("c3h", c)])
                P.op("dve", lambda e, c=c: e.memset(hcar[:, c:c + 1], 0.0), writes=[("hcar", c)])

        def build_diag(l, c, par_, which):
            if which == "rnn":
                r0, n, base = 8, 5, 0
            else:
                r0, n, base = 16, 3, 5
            P.op("pool", lambda e: e.tensor_tensor(
                out=dg[:, par_, base:base + n, :],
                in0=identb[:].unsqueeze(1).broadcast_to([128, n, 128]),
                in1=parT[:, c, l * NPAR + r0:l * NPAR + r0 + n].unsqueeze(2).broadcast_to([128, n, 128]),
                op=ALU.mult),
                reads=[("identb",)] + PARK, writes=[("dg", par_, which)])


        itctr = {"rnn": 0, "conv": 0, "ph2": 0, "ph3": 0}

        nrm = stage[:].rearrange("p a d -> p (a d)").rearrange("p (s n) -> p s n", n=512)
        tff = tf[:].rearrange("p s n -> p (s n)")
        stg_out = tff[:, 10 * NMAX:10 * NMAX + D]
        stg_in = tff[:, 13 * NMAX:13 * NMAX + D]
        SOUTK = [("tf", 10), ("tf", 11), ("tf", 12)]
        SINK = [("tf", 13), ("tf", 14), ("tf", 15)]
        pending_norm = []

        def ph3_norm_chunk(u, si, j, ll):
            segs = TILES[u][2]
            Av, Bv = nrm[:, 2 * si, :], nrm[:, 2 * si + 1, :]
            SK = ("stg", si)
            for (kind, g0, n, l0) in segs:
                P.op("dve", lambda e, g0=g0, n=n, l0=l0: e.tensor_tensor(
                    out=x32[:, j, g0:g0 + n], in0=x32[:, j, g0:g0 + n], in1=Av[:, l0:l0 + n], op=ALU.mult),
                    reads=[("x32", u, j), SK], writes=[("x32", u, j)])
            for (kind, g0, n, l0) in segs:
                P.op("dve", lambda e, g0=g0, n=n, l0=l0: e.tensor_tensor(
                    out=x32[:, j, g0:g0 + n], in0=x32[:, j, g0:g0 + n], in1=Bv[:, l0:l0 + n], op=ALU.add),
                    reads=[("x32", u, j), SK], writes=[("x32", u, j)])
            for (kind, g0, n, l0) in segs:
                P.op("act", lambda e, g0=g0, n=n: e.activation(
                    out=x32[:, j, g0:g0 + n], in_=x32[:, j, g0:g0 + n], func=AF.Identity,
                    bias=par(ll, 20, j), scale=par(ll, 19, j)),
                    reads=[("x32", u, j)] + PARK, writes=[("x32", u, j)])

        def emit_deferred(j):
            for (u, si, ll) in pending_norm:
                ph3_norm_chunk(u, si, j, ll)
            if j == NCH - 1:
                del pending_norm[:]

        def flush_deferred():
            if pending_norm:
                for j in range(NCH):
                    emit_deferred(j)

        def run_layer(l, gi0):
            gi = gi0
            if small:
                load_layer_small(l)
            for pa in range(npass):
                tl_list = PASS_TILES[pa]
                rchain = {"f": None}
                for cp in range(4 if nphase >= 1 else 0):
                    ensure_loaded(gi)
                    ensure_loaded(gi + 1)
                    sxr, sgr = gslots[gi]
                    gi += 1
                    def rnn_body(u, pk, prev_chain, cp=cp, sxr=sxr, sgr=sgr):
                        N, np_, has_s = seg_views(u)
                        toff = TILES[u][1]
                        lt = u % 2
                        first = (u == 0)
                        last = (u == 5)
                        sbase = 3 + np_
                        dk = DERK(l)
                        ctx = []
                        for ci in range(2):
                            st_ = 4 * (2 * pk + ci)
                            ctx.append(dict(
                                c=2 * cp + ci, ci=ci, b_xr=ci, b_gr=2 + ci, b_xc=4 + 2 * pk + ci,
                                sg=tf[:, st_, :], ta=tf[:, st_ + 1, :], tx=tf[:, st_ + 2, :], av=tf[:, st_ + 3, :],
                                SG=("tf", st_), TA=("tf", st_ + 1), TX=("tf", st_ + 2), AV=("tf", st_ + 3),
                                hs=tf[:, 16 + ci, :], HS=("tf", 16 + ci),
                                xrb=tb[:, 2 * ci, :], xcb=tb[:, 2 * ci + 1, :], XRB=("tb", 2 * ci), XCB=("tb", 2 * ci + 1)))
                        b_ga, b_gx = 0, 1

                        def mm(pe, slot, bank, ci, N=N, toff=toff):
                            for k in range(NCH):
                                ins = pe.matmul(ps[:, bank, 0:N], wring[:, slot, k, ci * 128:(ci + 1) * 128],
                                                xb[:, k, toff:toff + N], start=(k == 0), stop=(k == NCH - 1))
                            return ins
                        for d in ctx:
                            P.op("pe", lambda pe, mm=mm, s=sxr, b=d["b_xr"], ci=d["ci"]: mm(pe, s, b, ci),
                                 reads=[("ws", sxr), ("xb", lt)], writes=[("ps", d["b_xr"])])
                            P.op("pe", lambda pe, mm=mm, s=sgr, b=d["b_gr"], ci=d["ci"]: mm(pe, s, b, ci),
                                 reads=[("ws", sgr), ("xb", lt)], writes=[("ps", d["b_gr"])])
                        if u == tl_list[0]:
                            for d in ctx:
                                build_diag(l, d["c"], d["ci"], "rnn")
                        for d in ctx:
                            c, xrb, XRB, b = d["c"], d["xrb"], d["XRB"], d["b_xr"]
                            P.op("pool", lambda e, xrb=xrb, c=c: e.tensor_copy(out=xrb[:, 0:3], in_=c4h[:, c, :]),
                                 reads=[("c4h", c)], writes=[XRB])
                            if has_s:
                                P.op("pool", lambda e, xrb=xrb, c=c: e.tensor_copy(
                                    out=xrb[:, sbase:sbase + 176].rearrange("p (s k) -> p s k", k=11)[:, :, 0:3],
                                    in_=stS[:, c, 0:48].rearrange("p (s k) -> p s k", k=3)),
                                    reads=[("stS", c // 4)], writes=[XRB])
                            P.op("dve", lambda e, xrb=xrb, c=c, b=b: e.tensor_scalar(
                                out=xrb[:, 3:3 + np_], in0=ps[:, b, 0:np_], scalar1=par(l, 0, c), scalar2=None,
                                op0=ALU.add),
                                reads=[("ps", b)] + PARK, writes=[XRB])
                            P.op("dve", lambda e, c=c, b=b: e.tensor_scalar(
                                out=c4h[:, c, :], in0=ps[:, b, np_ - 3:np_], scalar1=par(l, 0, c), scalar2=None,
                                op0=ALU.add),
                                reads=[("ps", b)] + PARK, writes=[("c4h", c)])
                            if has_s:
                                P.op("dve", lambda e, xrb=xrb, c=c, b=b: e.tensor_scalar(
                                    out=xrb[:, sbase:sbase + 176].rearrange("p (s k) -> p s k", k=11)[:, :, 3:11],
                                    in0=ps[:, b, np_:np_ + 128].rearrange("p (s k) -> p s k", k=8),
                                    scalar1=par(l, 0, c), scalar2=None, op0=ALU.add),
                                    reads=[("ps", b)] + PARK, writes=[XRB])
                                P.op("dve", lambda e, c=c, b=b: e.tensor_scalar(
                                    out=sout[:, c, 22:70].rearrange("p (s k) -> p s k", k=3),
                                    in0=ps[:, b, np_:np_ + 128].rearrange("p (s k) -> p s k", k=8)[:, :, 5:8],
                                    scalar1=par(l, 0, c), scalar2=None, op0=ALU.add),
                                    reads=[("ps", b)] + PARK, writes=[("sout", c)])
                            if last:
                                P.op("dve", lambda e, c=c: e.tensor_copy(out=sout[:, c, 1:4], in_=c4h[:, c, :]),
                                     reads=[("c4h", c)], writes=[("sout", c)])
                        for d in ctx:
                            c = d["c"]
                            P.op("act", lambda e, d=d, c=c: e.activation(
                                out=d["sg"][:, 0:N], in_=ps[:, d["b_gr"], 0:N], func=AF.Silu, bias=par(l, 1, c)),
                                reads=[("ps", d["b_gr"])] + PARK, writes=[d["SG"]])
                        for d in ctx:
                            def conv4(pe, d=d):
                                xrb, b_xc, ci = d["xrb"], d["b_xc"], d["ci"]
                                for k in range(4):
                                    pe.matmul(ps[:, b_xc, 0:np_], dg[:, ci, k, :], xrb[:, k:k + np_],
                                              start=(k == 0), stop=False)
                                ins = pe.matmul(ps[:, b_xc, 0:np_], dg[:, ci, 4, :], ones[:, 0:np_],
                                                start=False, stop=True)
                                if has_s:
                                    xs = xrb[:, sbase:sbase + 176].rearrange("p (s k) -> p s k", k=11)
                                    o = ps[:, b_xc, np_:np_ + 128].rearrange("p (s k) -> p s k", k=8)
                                    for k in range(4):
                                        pe.matmul(o, dg[:, ci, k, :], xs[:, :, k:k + 8], start=(k == 0), stop=False)
                                    ins = pe.matmul(o, dg[:, ci, 4, :],
                                                    ones[:, 0:128].rearrange("p (s k) -> p s k", k=8),
                                                    start=False, stop=True)
                                return ins
                            P.op("pe", conv4, reads=[d["XRB"], ("dg", d["ci"], "rnn"), ("ones",)],
                                 writes=[("ps", d["b_xc"])])
                        for d in ctx:
                            P.op("dve", lambda e, d=d: e.tensor_copy(out=d["xcb"][:, 0:N], in_=ps[:, d["b_xc"], 0:N]),
                                 reads=[("ps", d["b_xc"])], writes=[d["XCB"]])
                        for d in ctx:
                            c = d["c"]
                            P.op("pe", lambda pe, d=d, c=c: pe.matmul(ps[:, b_ga, 0:N], wab[:, 0, c, :],
                                                                      d["xcb"][:, 0:N], start=True, stop=True),
                                 reads=[d["XCB"], ("wab", 0)], writes=[("ps", b_ga)])
                            P.op("pe", lambda pe, d=d, c=c: pe.matmul(ps[:, b_gx, 0:N], wab[:, 1, c, :],
                                                                      d["xcb"][:, 0:N], start=True, stop=True),
                                 reads=[d["XCB"], ("wab", 1)], writes=[("ps", b_gx)])
                            P.op("act", lambda e, d=d, c=c: e.activation(out=d["ta"][:, 0:N], in_=ps[:, b_ga, 0:N],
                                                                         func=AF.Tanh, bias=der[:, 0, l, c:c + 1],
                                                                         scale=0.5),
                                 reads=[("ps", b_ga)] + dk, writes=[d["TA"]])
                            P.op("act", lambda e, d=d, c=c: e.activation(out=d["tx"][:, 0:N], in_=ps[:, b_gx, 0:N],
                                                                         func=AF.Tanh, bias=der[:, 1, l, c:c + 1],
                                                                         scale=0.5),
                                 reads=[("ps", b_gx)] + dk, writes=[d["TX"]])
                        if prev_chain is not None:
                            prev_chain()
                        for d in ctx:
                            c = d["c"]
                            P.op("act", lambda e, d=d, c=c: e.activation(
                                out=d["av"][:, 0:N], in_=d["ta"][:, 0:N], func=AF.Exp,
                                bias=der[:, 3, l, c:c + 1], scale=der[:, 3, l, c:c + 1]),
                                reads=[d["TA"]] + dk, writes=[d["AV"]])
                        for d in ctx:
                            P.op("act", lambda e, d=d: e.activation(out=d["ta"][:, 0:N], in_=d["av"][:, 0:N],
                                                                    func=AF.Square),
                                 reads=[d["AV"]], writes=[d["TA"]])
                        for d in ctx:
                            P.op("act", lambda e, d=d: e.activation(out=d["ta"][:, 0:N], in_=d["ta"][:, 0:N],
                                                                    func=AF.Sqrt, bias=0.25, scale=-0.25),
                                 reads=[d["TA"]], writes=[d["TA"]])
                        def chain():
                            for d in ctx:
                                c, ta, tx, av, hs, sg = d["c"], d["ta"], d["tx"], d["av"], d["hs"], d["sg"]
                                TA, TX, AV, HS, SG, b_xc = d["TA"], d["TX"], d["AV"], d["HS"], d["SG"], d["b_xc"]
                                if first:
                                    P.op("dve", lambda e, ta=ta: e.memset(ta[:, 0:1], 0.5), writes=[TA])
                                P.op("dve", lambda e, tx=tx, b_xc=b_xc: e.scalar_tensor_tensor(
                                    out=tx[:, 0:N], in0=tx[:, 0:N], scalar=1.0, in1=ps[:, b_xc, 0:N],
                                    op0=ALU.add, op1=ALU.mult),
                                    reads=[TX, ("ps", b_xc)], writes=[TX])
                                P.op("dve", lambda e, tx=tx, ta=ta: e.tensor_tensor(out=tx[:, 0:N], in0=tx[:, 0:N],
                                                                                    in1=ta[:, 0:N], op=ALU.mult),
                                     reads=[TX, TA], writes=[TX])
                                P.op("dve", lambda e, hs=hs, av=av, tx=tx, c=c: e.tensor_tensor_scan(
                                    out=hs[:, 0:np_], data0=av[:, 0:np_], data1=tx[:, 0:np_],
                                    initial=hcar[:, c:c + 1], op0=ALU.mult, op1=ALU.add),
                                    reads=[AV, TX, ("hcar", c)], writes=[HS])
                                P.op("dve", lambda e, hs=hs, c=c: e.tensor_copy(out=hcar[:, c:c + 1],
                                                                                in_=hs[:, np_ - 1:np_]),
                                     reads=[HS], writes=[("hcar", c)])
                                if last:
                                    P.op("dve", lambda e, hs=hs, c=c: e.tensor_copy(out=sout[:, c, 0:1],
                                                                                    in_=hs[:, np_ - 1:np_]),
                                         reads=[HS], writes=[("sout", c)])
                                if has_s:
                                    a_s = av[:, np_:np_ + 128].rearrange("p (s k) -> p s k", k=8)
                                    b_s = tx[:, np_:np_ + 128].rearrange("p (s k) -> p s k", k=8)
                                    h_s = hs[:, np_:np_ + 128].rearrange("p (s k) -> p s k", k=8)
                                    h0 = stS[:, c, 80:96]
                                    P.op("dve", lambda e, h_s=h_s, a_s=a_s, h0=h0: e.tensor_tensor(
                                        out=h_s[:, :, 0], in0=a_s[:, :, 0], in1=h0, op=ALU.mult),
                                        reads=[AV, ("stS", c // 4)], writes=[HS])
                                    P.op("dve", lambda e, h_s=h_s, b_s=b_s: e.tensor_tensor(
                                        out=b_s[:, :, 0], in0=b_s[:, :, 0], in1=h_s[:, :, 0], op=ALU.add),
                                        reads=[HS, TX], writes=[TX])
                                    P.op("dve", lambda e, a_s=a_s: e.memset(a_s[:, :, 0], 0.0), writes=[AV])
                                    P.op("dve", lambda e, hs=hs, av=av, tx=tx: e.tensor_tensor_scan(
                                        out=hs[:, np_:np_ + 128], data0=av[:, np_:np_ + 128],
                                        data1=tx[:, np_:np_ + 128], initial=0.0, op0=ALU.mult, op1=ALU.add),
                                        reads=[AV, TX], writes=[HS])
                                    P.op("dve", lambda e, h_s=h_s, c=c: e.tensor_copy(out=sout[:, c, 6:22],
                                                                                      in_=h_s[:, :, 7]),
                                         reads=[HS], writes=[("sout", c)])
                                P.op("dve", lambda e, hs=hs, sg=sg, c=c: e.tensor_tensor(
                                    out=pb[:, c, toff:toff + N], in0=hs[:, 0:N], in1=sg[:, 0:N], op=ALU.mult),
                                    reads=[HS, SG], writes=[("p", lt, c)])
                        return chain

                    for u in tl_list:
                        rchain["f"] = rnn_body(u, itctr["rnn"] % 2, rchain["f"])
                        itctr["rnn"] += 1

                if rchain["f"] is not None:
                    rchain["f"]()
                    rchain["f"] = None
                for cp in range(4 if nphase >= 2 else 0):
                    ensure_loaded(gi)
                    ensure_loaded(gi + 1)
                    scc, sch, sgc_, scb = gslots[gi]
                    gi += 1
                    its = [(u, c) for u in tl_list for c in (2 * cp, 2 * cp + 1)]
                    pend = None
                    for (u, c) in its:
                        it = itctr["conv"]
                        itctr["conv"] += 1
                        par_ = it % 2
                        civ = cp * 4 + its.index((u, c))
                        N, np_, has_s = seg_views(u)
                        toff = TILES[u][1]
                        lt = u % 2
                        cc_ = c % 2
                        b_cc, b_ch, b_gc, b_cb, b_v = 0, 1, 2 + par_, 4 + par_, 6 + par_
                        last = (u == 5)
                        ccs, sgc = tf[:, 6 * par_ + 0, :], tf[:, 6 * par_ + 1, :]
                        CCS, SGC = ("tf", 6 * par_ + 0), ("tf", 6 * par_ + 1)
                        ub = tb[:, 2 * par_, :]
                        UB = ("tb", 2 * par_)
                        sbase = 2 + np_

                        def mm(pe, slot, bank, N=N, toff=toff, cc_=cc_):
                            for k in range(NCH):
                                ins = pe.matmul(ps[:, bank, 0:N], wring[:, slot, k, cc_ * 128:(cc_ + 1) * 128],
                                                xb[:, k, toff:toff + N], start=(k == 0), stop=(k == NCH - 1))
                            return ins
                        for (s_, b_) in ((scc, b_cc), (sch, b_ch), (sgc_, b_gc), (scb, b_cb)):
                            P.op("pe", lambda pe, mm=mm, s=s_, b=b_: mm(pe, s, b),
                                 reads=[("ws", s_), ("xb", lt)], writes=[("ps", b_)])
                        if pend is not None:
                            pend()
                        if u == tl_list[0]:
                            build_diag(l, c, par_, "conv")
                        P.op("act", lambda e, ccs=ccs, c=c, N=N: e.activation(
                            out=ccs[:, 0:N], in_=ps[:, b_cc, 0:N], func=AF.Identity, bias=par(l, 3, c)),
                            reads=[("ps", b_cc)] + PARK, writes=[CCS])
                        P.op("act", lambda e, ub=ub, c=c: e.activation(out=ub[:, 0:2], in_=c3h[:, c, :],
                                                                       func=AF.Identity),
                             reads=[("c3h", c)], writes=[UB])
                        if has_s:
                            P.op("act", lambda e, ub=ub, c=c, sbase=sbase: e.activation(
                                out=ub[:, sbase:sbase + 160].rearrange("p (s k) -> p s k", k=10)[:, :, 0:2],
                                in_=stS[:, c, 48:80].rearrange("p (s k) -> p s k", k=2), func=AF.Identity),
                                reads=[("stS", c // 4)], writes=[UB])
                        P.op("dve", lambda e, ub=ub, ccs=ccs, c=c, np_=np_: e.scalar_tensor_tensor(
                            out=ub[:, 2:2 + np_], in0=ps[:, b_ch, 0:np_], scalar=par(l, 4, c), in1=ccs[:, 0:np_],
                            op0=ALU.add, op1=ALU.mult),
                            reads=[("ps", b_ch), CCS] + PARK, writes=[UB])
                        P.op("dve", lambda e, ccs=ccs, c=c, np_=np_: e.scalar_tensor_tensor(
                            out=c3h[:, c, :], in0=ps[:, b_ch, np_ - 2:np_], scalar=par(l, 4, c),
                            in1=ccs[:, np_ - 2:np_], op0=ALU.add, op1=ALU.mult),
                            reads=[("ps", b_ch), CCS] + PARK, writes=[("c3h", c)])
                        if has_s:
                            P.op("dve", lambda e, ub=ub, ccs=ccs, c=c, np_=np_, sbase=sbase: e.scalar_tensor_tensor(
                                out=ub[:, sbase:sbase + 160].rearrange("p (s k) -> p s k", k=10)[:, :, 2:10],
                                in0=ps[:, b_ch, np_:np_ + 128].rearrange("p (s k) -> p s k", k=8),
                                scalar=par(l, 4, c),
                                in1=ccs[:, np_:np_ + 128].rearrange("p (s k) -> p s k", k=8),
                                op0=ALU.add, op1=ALU.mult),
                                reads=[("ps", b_ch), CCS] + PARK, writes=[UB])
                            P.op("dve", lambda e, ccs=ccs, c=c, np_=np_: e.scalar_tensor_tensor(
                                out=sout[:, c, 70:102].rearrange("p (s k) -> p s k", k=2),
                                in0=ps[:, b_ch, np_:np_ + 128].rearrange("p (s k) -> p s k", k=8)[:, :, 6:8],
                                scalar=par(l, 4, c),
                                in1=ccs[:, np_:np_ + 128].rearrange("p (s k) -> p s k", k=8)[:, :, 6:8],
                                op0=ALU.add, op1=ALU.mult),
                                reads=[("ps", b_ch), CCS] + PARK, writes=[("sout", c)])
                        if last:
                            P.op("dve", lambda e, c=c: e.tensor_copy(out=sout[:, c, 4:6], in_=c3h[:, c, :]),
                                 reads=[("c3h", c)], writes=[("sout", c)])
                        P.op("act", lambda e, sgc=sgc, c=c, N=N, b=b_gc: e.activation(
                            out=sgc[:, 0:N], in_=ps[:, b, 0:N], func=AF.Silu, bias=par(l, 5, c)),
                            reads=[("ps", b_gc)] + PARK, writes=[SGC])
                        P.op("dve", lambda e, sgc=sgc, c=c, N=N, b=b_cb: e.scalar_tensor_tensor(
                            out=sgc[:, 0:N], in0=ps[:, b, 0:N], scalar=par(l, 2, c), in1=sgc[:, 0:N],
                            op0=ALU.add, op1=ALU.mult),
                            reads=[("ps", b_cb), SGC] + PARK, writes=[SGC])
                        if pending_norm:
                            (u_, si_, ll_) = pending_norm[civ % 2]
                            ph3_norm_chunk(u_, si_, civ // 2, ll_)
                            if civ == 2 * NCH - 1:
                                del pending_norm[:]

                        def pe2(u=u, c=c, par_=par_, N=N, np_=np_, has_s=has_s, ub=ub, UB=UB, b_v=b_v, sgc=sgc,
                                SGC=SGC, toff=toff, lt=lt, sbase=sbase):
                            def conv3(pe):
                                for k in range(3):
                                    ins = pe.matmul(ps[:, b_v, 0:np_], dg[:, par_, 5 + k, :], ub[:, k:k + np_],
                                                    start=(k == 0), stop=(k == 2))
                                if has_s:
                                    us = ub[:, sbase:sbase + 160].rearrange("p (s k) -> p s k", k=10)
                                    o = ps[:, b_v, np_:np_ + 128].rearrange("p (s k) -> p s k", k=8)
                                    for k in range(3):
                                        ins = pe.matmul(o, dg[:, par_, 5 + k, :], us[:, :, k:k + 8],
                                                        start=(k == 0), stop=(k == 2))
                                return ins
                            P.op("pe", conv3, reads=[UB, ("dg", par_, "conv")], writes=[("ps", b_v)])
                            P.op("dve", lambda e: e.tensor_tensor(out=qb[:, c, toff:toff + N], in0=sgc[:, 0:N],
                                                                  in1=ps[:, b_v, 0:N], op=ALU.mult),
                                 reads=[SGC, ("ps", b_v)], writes=[("q", lt, c)])
                        pend = pe2
                    if pend is not None:
                        pend()
                        pend = None

                for jp in range(4 if nphase >= 3 else 0):
                    ensure_loaded(gi)
                    ensure_loaded(gi + 1)
                    sro, sco, sg1, sg2 = gslots[gi]
                    gi += 1
                    for u in tl_list:
                        for j in (2 * jp, 2 * jp + 1):
                            it = itctr["ph2"]
                            itctr["ph2"] += 1
                            par_ = it % 2
                            N, np_, has_s = seg_views(u)
                            toff = TILES[u][1]
                            lt = u % 2
                            jj = j % 2
                            b_yr, b_yc, b_g1, b_g2 = par_, 2 + par_, 4 + par_, 6 + par_
                            s1, s2 = tf[:, 6 * par_ + 0, :], tf[:, 6 * par_ + 1, :]
                            S1K, S2K = ("tf", 6 * par_ + 0), ("tf", 6 * par_ + 1)

                            def mm(pe, slot, bank, src, N=N, toff=toff, jj=jj):
                                for k in range(NCH):
                                    ins = pe.matmul(ps[:, bank, 0:N], wring[:, slot, k, jj * 128:(jj + 1) * 128],
                                                    src[:, k, toff:toff + N], start=(k == 0), stop=(k == NCH - 1))
                                return ins
                            P.op("pe", lambda pe, mm=mm, s=sg1, b=b_g1: mm(pe, s, b, xb),
                                 reads=[("ws", sg1), ("xb", lt)], writes=[("ps", b_g1)])
                            P.op("pe", lambda pe, mm=mm, s=sg2, b=b_g2: mm(pe, s, b, xb),
                                 reads=[("ws", sg2), ("xb", lt)], writes=[("ps", b_g2)])
                            P.op("pe", lambda pe, mm=mm, s=sro, b=b_yr: mm(pe, s, b, pb),
                                 reads=[("ws", sro)] + [("p", lt, c) for c in range(NCH)], writes=[("ps", b_yr)])
                            P.op("pe", lambda pe, mm=mm, s=sco, b=b_yc: mm(pe, s, b, qb),
                                 reads=[("ws", sco)] + [("q", lt, c) for c in range(NCH)], writes=[("ps", b_yc)])
                            P.op("act", lambda e, s1=s1, j=j, N=N, b=b_g1: e.activation(
                                out=s1[:, 0:N], in_=ps[:, b, 0:N], func=AF.Sigmoid, bias=par(l, 6, j)),
                                reads=[("ps", b_g1)] + PARK, writes=[S1K])
                            P.op("act", lambda e, s2=s2, j=j, N=N, b=b_g2: e.activation(
                                out=s2[:, 0:N], in_=ps[:, b, 0:N], func=AF.Sigmoid, bias=par(l, 7, j)),
                                reads=[("ps", b_g2)] + PARK, writes=[S2K])
                            P.op("dve", lambda e, s1=s1, N=N, b=b_yr: e.tensor_tensor(
                                out=s1[:, 0:N], in0=s1[:, 0:N], in1=ps[:, b, 0:N], op=ALU.mult),
                                reads=[S1K, ("ps", b_yr)], writes=[S1K])
                            P.op("dve", lambda e, s2=s2, N=N, b=b_yc: e.tensor_tensor(
                                out=s2[:, 0:N], in0=s2[:, 0:N], in1=ps[:, b, 0:N], op=ALU.mult),
                                reads=[S2K, ("ps", b_yc)], writes=[S2K])
                            P.op("dve", lambda e, s1=s1, s2=s2, N=N, j=j, toff=toff: e.tensor_tensor(
                                out=mb[:, j, toff:toff + N], in0=s1[:, 0:N], in1=s2[:, 0:N], op=ALU.add),
                                reads=[S1K, S2K], writes=[("m", lt, j)])

                ensure_loaded(gi)
                ensure_loaded(gi + 1)
                swo = gslots[gi]
                gi += 1
                nxt = (l, pa + 1) if pa < npass - 1 else ((l + 1, 0) if l + 1 < depth else None)
                def ph3_loop1(u, si, swo=swo, cast_pa=None, hook=None):
                    b_S1, b_S2 = 2 + 2 * si, 3 + 2 * si
                    N, np_, has_s = seg_views(u)
                    toff = TILES[u][1]
                    lt = u % 2
                    segs = TILES[u][2]
                    pend_s = None
                    for j in range(NCH):
                        it = itctr["ph3"]
                        itctr["ph3"] += 1
                        par_ = it % 2
                        b_o = par_
                        vb, vsq = tb[:, 2 * par_, :], tb[:, 2 * par_ + 1, :]
                        VB, VSQ = ("tb", 2 * par_), ("tb", 2 * par_ + 1)

                        def mmo(pe, j=j, b_o=b_o):
                            slot = swo[j // 2]
                            jj = j % 2
                            for k in range(NCH):
                                ins = pe.matmul(ps[:, b_o, 0:N], wring[:, slot, k, jj * 128:(jj + 1) * 128],
                                                mb[:, k, toff:toff + N], start=(k == 0), stop=(k == NCH - 1))
                            return ins
                        P.op("pe", mmo, reads=[("ws", swo[j // 2])] + [("m", lt, k) for k in range(NCH)],
                             writes=[("ps", b_o)])
                        if pend_s is not None:
                            pend_s()
                        for (kind, g0, n, l0) in segs:
                            P.op("dve", lambda e, j=j, g0=g0, n=n, l0=l0, b_o=b_o: e.scalar_tensor_tensor(
                                out=x32[:, j, g0:g0 + n], in0=x32[:, j, g0:g0 + n], scalar=ALPHA,
                                in1=ps[:, b_o, l0:l0 + n], op0=ALU.mult, op1=ALU.add),
                                reads=[("x32", u, j), ("ps", b_o)], writes=[("x32", u, j)])
                        for (kind, g0, n, l0) in segs:
                            P.op("act", lambda e, j=j, g0=g0, n=n, l0=l0, vb=vb: e.activation(
                                out=vb[:, l0:l0 + n], in_=x32[:, j, g0:g0 + n], func=AF.Copy),
                                reads=[("x32", u, j)], writes=[VB])
                            P.op("act", lambda e, j=j, g0=g0, n=n, l0=l0, vsq=vsq: e.activation(
                                out=vsq[:, l0:l0 + n], in_=x32[:, j, g0:g0 + n], func=AF.Square),
                                reads=[("x32", u, j)], writes=[VSQ])

                        if cast_pa is not None:
                            cast_xb(cast_pa, "act", j)

                        def smm(j=j, vb=vb, vsq=vsq, VB=VB, VSQ=VSQ):
                            P.op("pe", lambda pe: pe.matmul(
                                ps[:, b_S1, 0:N], ones[:, 0:128], vb[:, 0:N], start=(j == 0), stop=(j == NCH - 1)),
                                reads=[VB, ("ones",)], writes=[("ps", b_S1)])
                            P.op("pe", lambda pe: pe.matmul(
                                ps[:, b_S2, 0:N], ones[:, 0:128], vsq[:, 0:N], start=(j == 0), stop=(j == NCH - 1)),
                                reads=[VSQ, ("ones",)], writes=[("ps", b_S2)])
                        pend_s = smm
                        if hook is not None and j == 1:
                            hook()
                    pend_s()

                def ph3_stats(u, si):
                    N, np_, has_s = seg_views(u)
                    b_S1, b_S2 = 2 + 2 * si, 3 + 2 * si
                    mean, msq = tf[:, 8, :], tf[:, 9, :]
                    MEAN, MSQ = ("tf", 8), ("tf", 9)
                    Av, Bv = nrm[:, 2 * si, :], nrm[:, 2 * si + 1, :]
                    SK = ("stg", si)
                    P.op("dve", lambda e: e.tensor_scalar(out=mean[:, 0:N], in0=ps[:, b_S1, 0:N],
                                                          scalar1=1.0 / D, scalar2=None, op0=ALU.mult),
                         reads=[("ps", b_S1)], writes=[MEAN])
                    P.op("act", lambda e: e.activation(out=msq[:, 0:N], in_=ps[:, b_S1, 0:N], func=AF.Square,
                                                       scale=1.0 / D),
                         reads=[("ps", b_S1)], writes=[MSQ])
                    P.op("dve", lambda e: e.scalar_tensor_tensor(out=msq[:, 0:N], in0=ps[:, b_S2, 0:N],
                                                                 scalar=1.0 / D, in1=msq[:, 0:N],
                                                                 op0=ALU.mult, op1=ALU.subtract),
                         reads=[("ps", b_S2), MSQ], writes=[MSQ])
                    P.op("act", lambda e: e.activation(out=msq[:, 0:N], in_=msq[:, 0:N], func=AF.Ln,
                                                       bias=epst[:, 0:1], scale=1.0),
                         reads=[MSQ, ("eps",)], writes=[MSQ])
                    P.op("act", lambda e: e.activation(out=Av[:, 0:N], in_=msq[:, 0:N], func=AF.Exp, scale=-0.5),
                         reads=[MSQ], writes=[SK])
                    P.op("dve", lambda e: e.scalar_tensor_tensor(out=Bv[:, 0:N], in0=mean[:, 0:N], scalar=-1.0,
                                                                 in1=Av[:, 0:N], op0=ALU.mult, op1=ALU.mult),
                         reads=[MEAN, SK], writes=[SK])

                u0, u1 = tl_list
                flush_deferred()
                ph3_loop1(u0, 0, cast_pa=(nxt[1] if nxt is not None else None))
                ph3_loop1(u1, 1, hook=lambda: ph3_stats(u0, 0))
                ph3_stats(u1, 1)
                pending_norm.extend([(u0, 0, l), (u1, 1, l)])
                if pa == npass - 1 and l == depth - 1:
                    flush_deferred()

            return gi

        def cast_xb(pa, eng="pool", only_k=None):
            for u in PASS_TILES[pa]:
                toff = TILES[u][1]
                lt = u % 2
                for (kind, g0, n, l0) in TILES[u][2]:
                    ks = range(NCH) if only_k is None else [only_k]
                    if eng == "pool":
                        P.op("pool", lambda e, g0=g0, n=n, o=toff + l0: e.tensor_copy(
                            out=xb[:, :, o:o + n], in_=x32[:, :, g0:g0 + n]),
                            reads=x32keys(u), writes=[("xb", lt)])
                    else:
                        for k in ks:
                            P.op("act", lambda e, g0=g0, n=n, o=toff + l0, k=k: e.activation(
                                out=xb[:, k, o:o + n], in_=x32[:, k, g0:g0 + n], func=AF.Copy),
                                reads=[("x32", u, k)], writes=[("xb", lt)])

        def store_states(l):
            def tr(pe):
                for c in range(NCH):
                    ins = pe.transpose(ps[0:102, 4 + c // 4, (c % 4) * 128:(c % 4 + 1) * 128], sout[:, c, :], ident[:])
                return ins
            P.op("pe", tr, reads=[("sout", c) for c in range(NCH)] + [("ident",)], writes=[("ps", 4), ("ps", 5)])
            for h in range(2):
                P.op("act", lambda e, h=h: e.activation(out=stg_out[0:102, 512 * h:512 * (h + 1)],
                                                        in_=ps[0:102, 4 + h, :], func=AF.Identity),
                     reads=[("ps", 4 + h)], writes=SOUTK)
            P.op("sp", lambda e: e.dma_start(out=st_out[l], in_=stg_out[0:102, :]),
                 reads=SOUTK, writes=[("st_out", l)], dma="out0")

        ensure_loaded(0)
        ensure_loaded(1)
        cast_xb(0)
        gi = 0
        for l in range(depth):
            gi = run_layer(l, gi)
            store_states(l)

        for b in range(NBLK):
            s = b % 6
            sap, skeys = XS[s]
            pbk = 2 * (b % 4)
            rk = []
            for u in tiles_of_block(b):
                rk += x32keys(u)

            def tr_out(pe, b=b, pbk=pbk):
                for c in range(NCH):
                    ins = pe.transpose(ps[:, pbk + c // 4, (c % 4) * 128:(c % 4 + 1) * 128],
                                       x32[:, c, b * 128:(b + 1) * 128], ident[:])
                return ins
            P.op("pe", tr_out, reads=rk + [("ident",)], writes=[("ps", pbk), ("ps", pbk + 1)])
            for h in range(2):
                eng = "act" if h == 0 else "dve"

                def ev(e, sap=sap, h=h, pbk=pbk, eng=eng):
                    o = sap[:, 512 * h:512 * (h + 1)]
                    i = ps[:, pbk + h, :]
                    if eng == "act":
                        return e.activation(out=o, in_=i, func=AF.Identity)
                    return e.tensor_copy(out=o, in_=i)
                P.op(eng, ev, reads=[("ps", pbk + h)], writes=skeys)
            P.op("sp", lambda e, b=b, sap=sap: e.dma_start(out=y[b * 128:(b + 1) * 128, :], in_=sap),
                 reads=skeys, writes=[("y", b)], dma="out%d" % s)
        P.op("sp", lambda e: e.nop(), reads=[("y", b) for b in range(NBLK)] + [("st_out", l) for l in range(depth)])

        P.finalize()
        with nc.Block() as block:
            @block.tensor
            def _(e):
                P.emit_stream("pe", e, esem, dsem)

            @block.scalar
            def _(e):
                P.emit_stream("act", e, esem, dsem)

            @block.vector
            def _(e):
                P.emit_stream("dve", e, esem, dsem)

            @block.gpsimd
            def _(e):
                P.emit_stream("pool", e, esem, dsem)

            @block.sync
            def _(e):
                P.emit_stream("sp", e, esem, dsem)
    return nc


_NC_CACHE = {}


def _prep_inputs(inputs):
    f = lambda k: np.ascontiguousarray(np.asarray(inputs[k], dtype=np.float32))
    x_prompt, x_sample = f("x_prompt"), f("x_sample")
    s_h, s_c4, s_c3 = f("state_rglru"), f("state_conv4"), f("state_conv3")
    rows = []
    for l in range(DEPTH):
        rows.append(f("b_in")[l].reshape(8, D))
        rows.append(f("conv4_w")[l].reshape(4, D))
        rows.append(f("conv4_b")[l].reshape(1, D))
        rows.append(f("b_rg_a")[l].reshape(1, D))
        rows.append(f("b_rg_x")[l].reshape(1, D))
        rows.append(f("rg_lambda")[l].reshape(1, D))
        rows.append(f("conv3_w")[l].reshape(3, D))
        rows.append(f("ln_g")[l].reshape(1, D))
        rows.append(f("ln_b")[l].reshape(1, D))
    params = np.ascontiguousarray(np.concatenate(rows, axis=0))
    shared = {
        "params": params, "w_in": f("w_in"), "w_ro": f("w_rnn_out"), "w_co": f("w_conv_out"), "w_o": f("w_out"),
        "w_a": f("w_rg_a"), "w_x": f("w_rg_x"), "ident": np.eye(128, dtype=np.float32),
    }
    in_maps = []
    for c in range(8):
        sl = slice(16 * c, 16 * c + 16)
        xin = np.concatenate([x_prompt[c], x_sample[sl].reshape(128, D)], axis=0)
        st = np.concatenate([s_c4[:, sl].reshape(DEPTH, 48, D), s_c3[:, sl].reshape(DEPTH, 32, D),
                             s_h[:, sl].reshape(DEPTH, 16, D)], axis=1)
        m = dict(shared)
        m["xin"] = np.ascontiguousarray(xin)
        m["st_in"] = np.ascontiguousarray(st)
        in_maps.append(m)
    return in_maps


def kernel(**inputs):
    if "nc" not in _NC_CACHE:
        _NC_CACHE["nc"] = build_nc()
    nc = _NC_CACHE["nc"]
    in_maps = _prep_inputs(inputs)
    res = run_bass_kernel_spmd(nc, in_maps, core_ids=list(range(8)))
    ys = [np.asarray(r["y"]) for r in res.results]
    sts = [np.asarray(r["st_out"]) for r in res.results]
    y_prompt = np.stack([yy[0:NPR] for yy in ys], axis=0)
    y_sample = np.concatenate([yy[NPR:].reshape(16, 8, D) for yy in ys], axis=0)
    ph = np.stack([s[:, 0] for s in sts], axis=1)
    pc4 = np.stack([s[:, 1:4] for s in sts], axis=1)
    pc3 = np.stack([s[:, 4:6] for s in sts], axis=1)
    sh = np.concatenate([s[:, 6:22] for s in sts], axis=1)
    sc4 = np.concatenate([s[:, 22:70].reshape(DEPTH, 16, 3, D) for s in sts], axis=1)
    sc3 = np.concatenate([s[:, 70:102].reshape(DEPTH, 16, 2, D) for s in sts], axis=1)
    f32 = lambda a: np.ascontiguousarray(a, dtype=np.float32)
    return (f32(y_prompt), f32(y_sample), f32(ph), f32(pc4), f32(pc3), f32(sh), f32(sc4), f32(sc3))
```

```python
import contextlib
import numpy as np
import concourse.bass as bass
import concourse.mybir as mybir
from concourse.bass_utils import run_bass_kernel_spmd

F32 = mybir.dt.float32
BF16 = mybir.dt.bfloat16
AF = mybir.ActivationFunctionType
ALU = mybir.AluOpType

DEPTH = 4
D = 1024
NCH = 8
NPR = 2048
NSM = 128
NTOK = NPR + NSM
ALPHA = (2.0 * DEPTH) ** 0.25
EPS = 1e-5
NMAX = 384
NSLOT = 8
UW = 256
NPAR = 21

TILES = [
    (0, 0, [("P", 0, 384, 0)]),
    (0, 384, [("P", 384, 256, 0), ("S", 2048, 128, 256)]),
    (1, 0, [("P", 640, 352, 0)]),
    (1, 352, [("P", 992, 352, 0)]),
    (2, 0, [("P", 1344, 352, 0)]),
    (2, 352, [("P", 1696, 352, 0)]),
]
PASS_TILES = {0: [0, 1], 1: [2, 3], 2: [4, 5]}
PASSW = 768


def tile_n(u):
    return sum(s[2] for s in TILES[u][2])


class _Op:
    __slots__ = ("eng", "fn", "deps", "sig", "cnt", "dma", "dma_ord")


class Prog:
    ENGS = ("pe", "act", "dve", "pool", "sp")

    def __init__(self):
        self.streams = {e: [] for e in self.ENGS}
        self.last_w = {}
        self.readers = {}
        self.dma_last = {}
        self.dma_cnt = {}

    def op(self, eng, fn, reads=(), writes=(), dma=None):
        o = _Op()
        o.eng, o.fn, o.sig, o.cnt, o.dma, o.dma_ord = eng, fn, False, 0, dma, 0
        deps = []
        for r in reads:
            w = self.last_w.get(r)
            if w is not None:
                deps.append(w)
            if r[0] == "ps":
                rd = self.readers.get(r)
                if rd:
                    deps.extend(v for k, v in rd[0].items() if k != eng)
        for r in writes:
            w = self.last_w.get(r)
            if w is not None:
                deps.append(w)
            rd = self.readers.get(r)
            if rd:
                deps.extend(rd[0].values())
                deps.extend(rd[1])
        if dma is not None:
            prev = self.dma_last.get(dma)
            if prev is not None:
                deps.append(prev)
            self.dma_last[dma] = o
            self.dma_cnt[dma] = self.dma_cnt.get(dma, 0) + 1
            o.dma_ord = self.dma_cnt[dma]
        seen = set()
        o.deps = []
        for d in deps:
            if id(d) in seen or d is o:
                continue
            seen.add(id(d))
            if eng == "pe" and d.eng == "pe" and d.dma is None and dma is None:
                continue
            d.sig = True
            o.deps.append(d)
        for r in writes:
            self.last_w[r] = o
            self.readers[r] = ({}, [])
        for r in reads:
            rd = self.readers.setdefault(r, ({}, []))
            if dma is None:
                rd[0][eng] = o
            else:
                rd[1].append(o)
        self.streams[eng].append(o)
        return o

    def finalize(self):
        for e in self.ENGS:
            c = 0
            for o in self.streams[e]:
                if o.sig and o.dma is None:
                    c += 1
                o.cnt = c

    def emit_stream(self, e, eng, esem, dsem):
        waited = {}
        for o in self.streams[e]:
            for d in o.deps:
                if d.dma is not None:
                    key, sem, val = ("d", d.dma), dsem[d.dma], 16 * d.dma_ord
                else:
                    key, sem, val = ("e", d.eng), esem[d.eng], d.cnt
                if waited.get(key, 0) < val:
                    eng.wait_ge(sem, val)
                    waited[key] = val
            ins = o.fn(eng)
            if o.dma is not None:
                ins.then_inc(dsem[o.dma], 16)
            elif o.sig:
                ins.then_inc(esem[e], 1)


def build_nc(depth=DEPTH, npass=3, nphase=4, small=True, p3=9):
    nc = bass.Bass("TRN2", target_bir_lowering=False)
    xin = nc.dram_tensor("xin", [NTOK, D], F32, kind="ExternalInput").ap()
    st_in = nc.dram_tensor("st_in", [DEPTH, 96, D], F32, kind="ExternalInput").ap()
    params = nc.dram_tensor("params", [DEPTH * NPAR, D], F32, kind="ExternalInput").ap()
    w_in = nc.dram_tensor("w_in", [DEPTH, D, 8 * D], F32, kind="ExternalInput").ap()
    w_ro = nc.dram_tensor("w_ro", [DEPTH, D, D], F32, kind="ExternalInput").ap()
    w_co = nc.dram_tensor("w_co", [DEPTH, D, D], F32, kind="ExternalInput").ap()
    w_o = nc.dram_tensor("w_o", [DEPTH, D, D], F32, kind="ExternalInput").ap()
    w_a = nc.dram_tensor("w_a", [DEPTH, NCH, 128, 128], F32, kind="ExternalInput").ap()
    w_x = nc.dram_tensor("w_x", [DEPTH, NCH, 128, 128], F32, kind="ExternalInput").ap()
    ident_d = nc.dram_tensor("ident", [128, 128], F32, kind="ExternalInput").ap()
    y = nc.dram_tensor("y", [NTOK, D], F32, kind="ExternalOutput").ap()
    st_out = nc.dram_tensor("st_out", [DEPTH, 102, D], F32, kind="ExternalOutput").ap()

    P = Prog()
    es = contextlib.ExitStack()
    with es:
        def sb(name, shape, dt):
            return es.enter_context(nc.sbuf_tensor(name, shape, dt))

        x32 = sb("x32", [128, NCH, NTOK], F32)
        xb = sb("xb", [128, NCH, PASSW], BF16)
        pb = sb("pb", [128, NCH, PASSW], BF16)
        qb = sb("qb", [128, NCH, PASSW], BF16)
        mb = sb("mb", [128, NCH, PASSW], BF16)
        wring = sb("wring", [128, NSLOT, NCH, UW], BF16)
        wab = sb("wab", [128, 2, NCH, 128], BF16)
        dg = sb("dg", [128, 2, 8, 128], BF16)
        parT = sb("parT", [128, NCH, DEPTH * NPAR], F32)
        der = sb("der", [128, 4, DEPTH, NCH], F32)
        ident = sb("identf", [128, 128], F32)
        identb = sb("identb", [128, 128], BF16)
        ones = sb("ones", [128, NMAX], BF16)
        stS = sb("stS", [128, NCH, 96], F32)
        sout = sb("sout", [128, NCH, 102], F32)
        c4h = sb("c4h", [128, NCH, 3], F32)
        c3h = sb("c3h", [128, NCH, 2], F32)
        hcar = sb("hcar", [128, NCH], F32)
        epst = sb("epst", [128, 1], F32)
        stage = sb("stage", [128, 2, D], F32)
        NTF, NTB = 18, 4
        tf = sb("tf", [128, NTF, NMAX], F32)
        tb = sb("tb", [128, NTB, 448], BF16)
        ps = es.enter_context(nc.psum_tensor("ps", [128, 8, 512], F32))

        tff0 = tf[:].rearrange("p s n -> p (s n)")
        XS = [(stage[:, 0, :], [("stg", 0)]), (stage[:, 1, :], [("stg", 1)])]
        for i_ in range(4):
            XS.append((tff0[:, i_ * 3 * NMAX:i_ * 3 * NMAX + D], [("tf", 3 * i_ + k_) for k_ in range(3)]))
        esem = {e: es.enter_context(nc.semaphore("s_" + e)) for e in ("pe", "act", "dve", "pool")}
        dnames = (["w%d" % i for i in range(NSLOT)] + ["stg%d" % i for i in range(6)] + ["misc", "wab"]
                  + ["out%d" % i for i in range(6)])
        dsem = {n: es.enter_context(nc.semaphore("d_" + n)) for n in dnames}

        def par(l, r, c):
            return parT[:, c, l * NPAR + r: l * NPAR + r + 1]

        P.op("sp", lambda e: e.dma_start(out=ident[:], in_=ident_d), writes=[("ident",)], dma="misc")
        P.op("sp", lambda e: e.dma_start(out=stage[0:DEPTH * NPAR, 0, :], in_=params),
             writes=[("stg", 0)], dma="stg0")
        P.op("dve", lambda e: e.tensor_copy(out=identb[:], in_=ident[:]), reads=[("ident",)],
             writes=[("identb",)])
        P.op("dve", lambda e: e.memset(ones[:], 1.0), writes=[("ones",)])
        P.op("dve", lambda e: e.memset(epst[:], EPS), writes=[("eps",)])

        def tr_params(pe):
            for c in range(NCH):
                ins = pe.transpose(ps[:, c // 4, (c % 4) * 128:(c % 4) * 128 + DEPTH * NPAR],
                                   stage[0:DEPTH * NPAR, 0, c * 128:(c + 1) * 128],
                                   ident[0:DEPTH * NPAR, 0:DEPTH * NPAR])
            return ins
        P.op("pe", tr_params, reads=[("stg", 0), ("ident",)], writes=[("ps", 0), ("ps", 1)])
        for h in range(2):
            P.op("act", lambda e, h=h: e.activation(
                out=parT[:, 4 * h:4 * h + 4, :],
                in_=ps[:, h, :].rearrange("p (c n) -> p c n", n=128)[:, :, 0:DEPTH * NPAR],
                func=AF.Identity), reads=[("ps", h)], writes=[("parT", h)])
        PARK = [("parT", 0), ("parT", 1)]
        for l in range(DEPTH):
            def mk(l):
                def v(ri):
                    return parT[:, :, l * NPAR + ri]
                P.op("act", lambda e: e.activation(out=der[:, 0, l, :], in_=v(13), func=AF.Identity, scale=0.5),
                     reads=PARK, writes=[("der", l, 0)])
                P.op("act", lambda e: e.activation(out=der[:, 1, l, :], in_=v(14), func=AF.Identity, scale=0.5),
                     reads=PARK, writes=[("der", l, 1)])
                P.op("act", lambda e: e.activation(out=der[:, 2, l, :], in_=v(15), func=AF.Exp, scale=-1.0),
                     reads=PARK, writes=[("der", l, 2)])
                P.op("act", lambda e: e.activation(out=der[:, 3, l, :], in_=der[:, 2, l, :], func=AF.Ln,
                                                   bias=1.0, scale=1.0),
                     reads=[("der", l, 2)], writes=[("der", l, 3)])
                P.op("act", lambda e: e.activation(out=der[:, 2, l, :], in_=der[:, 3, l, :], func=AF.Identity,
                                                   scale=-8.0),
                     reads=[("der", l, 3)], writes=[("der", l, 2)])
                P.op("act", lambda e: e.activation(out=der[:, 3, l, :], in_=der[:, 2, l, :], func=AF.Identity,
                                                   scale=0.5),
                     reads=[("der", l, 2)], writes=[("der", l, 3)])
            mk(l)
        DERK = lambda l: [("der", l, i) for i in range(4)]

        def tiles_of_block(b):
            g0, g1 = b * 128, (b + 1) * 128
            res = []
            for u, (_, _, segs) in enumerate(TILES):
                for (_, sg0, n, _) in segs:
                    if sg0 < g1 and sg0 + n > g0:
                        res.append(u)
            return sorted(set(res))

        def x32keys(u):
            return [("x32", u, c) for c in range(NCH)]

        NBLK = NTOK // 128
        for b in range(NBLK):
            s = b % 6
            sap, skeys = XS[s]
            P.op("sp", lambda e, b=b, sap=sap: e.dma_start(out=sap, in_=xin[b * 128:(b + 1) * 128, :]),
                 writes=skeys, dma="stg%d" % s)
            pbk = 2 * (b % 4)

            def tr_in(pe, sap=sap, pbk=pbk):
                for c in range(NCH):
                    ins = pe.transpose(ps[:, pbk + c // 4, (c % 4) * 128:(c % 4 + 1) * 128],
                                       sap[:, c * 128:(c + 1) * 128], ident[:])
                return ins
            P.op("pe", tr_in, reads=skeys + [("ident",)], writes=[("ps", pbk), ("ps", pbk + 1)])
            wk = []
            for u in tiles_of_block(b):
                wk += x32keys(u)
            for h in range(2):
                eng = "act" if h == 0 else "dve"

                def ev(e, b=b, h=h, pbk=pbk, eng=eng):
                    o = x32[:, 4 * h:4 * h + 4, b * 128:(b + 1) * 128]
                    i = ps[:, pbk + h, :].rearrange("p (c n) -> p c n", n=128)
                    if eng == "act":
                        return e.activation(out=o, in_=i, func=AF.Identity)
                    return e.tensor_copy(out=o, in_=i)
                P.op(eng, ev, reads=[("ps", pbk + h)], writes=[k for k in wk if k[2] // 4 == h])

        wstate = {"n": 0}

        def w_unit_ap(l, kind, col):
            if kind == "in":
                src = w_in[l]
            elif kind == "ro":
                src = w_ro[l]
            elif kind == "co":
                src = w_co[l]
            else:
                src = w_o[l]
            return src.rearrange("(k p) n -> p k n", p=128)[:, :, col:col + UW]

        def load_group(units):
            slots = []
            for (l, kind, col) in units:
                s = wstate["n"] % NSLOT
                wstate["n"] += 1
                src = w_unit_ap(l, kind, col)
                P.op("pool", lambda e, s=s, src=src: e.dma_start(out=wring[:, s, :, :], in_=src),
                     writes=[("ws", s)], dma="w%d" % s)
                slots.append(s)
            return slots

        groups = []
        for l in range(depth):
            for pa in range(npass):
                for cp in range(4 if nphase >= 1 else 0):
                    groups.append(("rnn", l, pa, cp, [(l, "in", 0 * D + cp * UW), (l, "in", 1 * D + cp * UW)]))
                for cp in range(4 if nphase >= 2 else 0):
                    groups.append(("conv", l, pa, cp, [(l, "in", g * D + cp * UW) for g in (3, 4, 5, 2)]))
                for jp in range(4 if nphase >= 3 else 0):
                    groups.append(("ph2", l, pa, jp, [(l, "ro", jp * UW), (l, "co", jp * UW),
                                                      (l, "in", 6 * D + jp * UW), (l, "in", 7 * D + jp * UW)]))
                if nphase >= 4:
                    groups.append(("ph3", l, pa, 0, [(l, "o", jp * UW) for jp in range(4)]))
        gslots = {}

        def ensure_loaded(gi):
            if gi < len(groups) and gi not in gslots:
                gslots[gi] = load_group(groups[gi][4])

        def seg_views(u):
            segs = TILES[u][2]
            np_ = segs[0][2]
            has_s = len(segs) > 1
            return tile_n(u), np_, has_s

        def load_layer_small(l):
            P.op("pool", lambda e: e.dma_start(out=wab[:, 0, :, :], in_=w_a[l].rearrange("n i j -> i n j")),
                 writes=[("wab", 0)], dma="wab")
            P.op("pool", lambda e: e.dma_start(out=wab[:, 1, :, :], in_=w_x[l].rearrange("n i j -> i n j")),
                 writes=[("wab", 1)], dma="wab")
            P.op("sp", lambda e: e.dma_start(out=stg_in[0:96, :], in_=st_in[l]), writes=SINK, dma="stg1")

            def tr_st(pe):
                for c in range(NCH):
                    ins = pe.transpose(ps[:, 2 + c // 4, (c % 4) * 128:(c % 4) * 128 + 96],
                                       stg_in[0:96, c * 128:(c + 1) * 128], ident[0:96, 0:96])
                return ins
            P.op("pe", tr_st, reads=SINK + [("ident",)], writes=[("ps", 2), ("ps", 3)])
            for h in range(2):
                P.op("act", lambda e, h=h: e.activation(
                    out=stS[:, 4 * h:4 * h + 4, :],
                    in_=ps[:, 2 + h, :].rearrange("p (c n) -> p c n", n=128)[:, :, 0:96], func=AF.Identity),
                    reads=[("ps", 2 + h)], writes=[("stS", h)])
            for c in range(NCH):
                P.op("dve", lambda e, c=c: e.memset(c4h[:, c, :], 0.0), writes=[("c4h", c)])
                P.op("dve", lambda e, c=c: e.memset(c3h[:, c, :], 0.0), writes=[("c3h", c)])
                P.op("dve", lambda e, c=c: e.memset(hcar[:, c:c + 1], 0.0), writes=[("hcar", c)])

        def build_diag(l, c, par_, which):
            if which == "rnn":
                r0, n, base = 8, 5, 0
            else:
                r0, n, base = 16, 3, 5
            P.op("pool", lambda e: e.tensor_tensor(
                out=dg[:, par_, base:base + n, :],
                in0=identb[:].unsqueeze(1).broadcast_to([128, n, 128]),
                in1=parT[:, c, l * NPAR + r0:l * NPAR + r0 + n].unsqueeze(2).broadcast_to([128, n, 128]),
                op=ALU.mult),
                reads=[("identb",)] + PARK, writes=[("dg", par_, which)])


        itctr = {"rnn": 0, "conv": 0, "ph2": 0, "ph3": 0}

        nrm = stage[:].rearrange("p a d -> p (a d)").rearrange("p (s n) -> p s n", n=512)
        tff = tf[:].rearrange("p s n -> p (s n)")
        stg_out = tff[:, 10 * NMAX:10 * NMAX + D]
        stg_in = tff[:, 13 * NMAX:13 * NMAX + D]
        SOUTK = [("tf", 10), ("tf", 11), ("tf", 12)]
        SINK = [("tf", 13), ("tf", 14), ("tf", 15)]
        pending_norm = []

        def ph3_norm_chunk(u, si, j, ll):
            segs = TILES[u][2]
            Av, Bv = nrm[:, 2 * si, :], nrm[:, 2 * si + 1, :]
            SK = ("stg", si)
            for (kind, g0, n, l0) in segs:
                P.op("dve", lambda e, g0=g0, n=n, l0=l0: e.tensor_tensor(
                    out=x32[:, j, g0:g0 + n], in0=x32[:, j, g0:g0 + n], in1=Av[:, l0:l0 + n], op=ALU.mult),
                    reads=[("x32", u, j), SK], writes=[("x32", u, j)])
            for (kind, g0, n, l0) in segs:
                P.op("dve", lambda e, g0=g0, n=n, l0=l0: e.tensor_tensor(
                    out=x32[:, j, g0:g0 + n], in0=x32[:, j, g0:g0 + n], in1=Bv[:, l0:l0 + n], op=ALU.add),
                    reads=[("x32", u, j), SK], writes=[("x32", u, j)])
            for (kind, g0, n, l0) in segs:
                P.op("act", lambda e, g0=g0, n=n: e.activation(
                    out=x32[:, j, g0:g0 + n], in_=x32[:, j, g0:g0 + n], func=AF.Identity,
                    bias=par(ll, 20, j), scale=par(ll, 19, j)),
                    reads=[("x32", u, j)] + PARK, writes=[("x32", u, j)])

        def emit_deferred(j):
            for (u, si, ll) in pending_norm:
                ph3_norm_chunk(u, si, j, ll)
            if j == NCH - 1:
                del pending_norm[:]

        def flush_deferred():
            if pending_norm:
                for j in range(NCH):
                    emit_deferred(j)

        def run_layer(l, gi0):
            gi = gi0
            if small:
                load_layer_small(l)
            for pa in range(npass):
                tl_list = PASS_TILES[pa]
                rchain = {"f": None}
                for cp in range(4 if nphase >= 1 else 0):
                    ensure_loaded(gi)
                    ensure_loaded(gi + 1)
                    sxr, sgr = gslots[gi]
                    gi += 1
                    def rnn_body(u, pk, prev_chain, cp=cp, sxr=sxr, sgr=sgr):
                        N, np_, has_s = seg_views(u)
                        toff = TILES[u][1]
                        lt = u % 2
                        first = (u == 0)
                        last = (u == 5)
                        sbase = 3 + np_
                        dk = DERK(l)
                        ctx = []
                        for ci in range(2):
                            st_ = 4 * (2 * pk + ci)
                            ctx.append(dict(
                                c=2 * cp + ci, ci=ci, b_xr=ci, b_gr=2 + ci, b_xc=4 + 2 * pk + ci,
                                sg=tf[:, st_, :], ta=tf[:, st_ + 1, :], tx=tf[:, st_ + 2, :], av=tf[:, st_ + 3, :],
                                SG=("tf", st_), TA=("tf", st_ + 1), TX=("tf", st_ + 2), AV=("tf", st_ + 3),
                                hs=tf[:, 16 + ci, :], HS=("tf", 16 + ci),
                                xrb=tb[:, 2 * ci, :], xcb=tb[:, 2 * ci + 1, :], XRB=("tb", 2 * ci), XCB=("tb", 2 * ci + 1)))
                        b_ga, b_gx = 0, 1

                        def mm(pe, slot, bank, ci, N=N, toff=toff):
                            for k in range(NCH):
                                ins = pe.matmul(ps[:, bank, 0:N], wring[:, slot, k, ci * 128:(ci + 1) * 128],
                                                xb[:, k, toff:toff + N], start=(k == 0), stop=(k == NCH - 1))
                            return ins
                        for d in ctx:
                            P.op("pe", lambda pe, mm=mm, s=sxr, b=d["b_xr"], ci=d["ci"]: mm(pe, s, b, ci),
                                 reads=[("ws", sxr), ("xb", lt)], writes=[("ps", d["b_xr"])])
                            P.op("pe", lambda pe, mm=mm, s=sgr, b=d["b_gr"], ci=d["ci"]: mm(pe, s, b, ci),
                                 reads=[("ws", sgr), ("xb", lt)], writes=[("ps", d["b_gr"])])
                        if u == tl_list[0]:
                            for d in ctx:
                                build_diag(l, d["c"], d["ci"], "rnn")
                        for d in ctx:
                            c, xrb, XRB, b = d["c"], d["xrb"], d["XRB"], d["b_xr"]
                            P.op("pool", lambda e, xrb=xrb, c=c: e.tensor_copy(out=xrb[:, 0:3], in_=c4h[:, c, :]),
                                 reads=[("c4h", c)], writes=[XRB])
                            if has_s:
                                P.op("pool", lambda e, xrb=xrb, c=c: e.tensor_copy(
                                    out=xrb[:, sbase:sbase + 176].rearrange("p (s k) -> p s k", k=11)[:, :, 0:3],
                                    in_=stS[:, c, 0:48].rearrange("p (s k) -> p s k", k=3)),
                                    reads=[("stS", c // 4)], writes=[XRB])
                            P.op("dve", lambda e, xrb=xrb, c=c, b=b: e.tensor_scalar(
                                out=xrb[:, 3:3 + np_], in0=ps[:, b, 0:np_], scalar1=par(l, 0, c), scalar2=None,
                                op0=ALU.add),
                                reads=[("ps", b)] + PARK, writes=[XRB])
                            P.op("dve", lambda e, c=c, b=b: e.tensor_scalar(
                                out=c4h[:, c, :], in0=ps[:, b, np_ - 3:np_], scalar1=par(l, 0, c), scalar2=None,
                                op0=ALU.add),
                                reads=[("ps", b)] + PARK, writes=[("c4h", c)])
                            if has_s:
                                P.op("dve", lambda e, xrb=xrb, c=c, b=b: e.tensor_scalar(
                                    out=xrb[:, sbase:sbase + 176].rearrange("p (s k) -> p s k", k=11)[:, :, 3:11],
                                    in0=ps[:, b, np_:np_ + 128].rearrange("p (s k) -> p s k", k=8),
                                    scalar1=par(l, 0, c), scalar2=None, op0=ALU.add),
                                    reads=[("ps", b)] + PARK, writes=[XRB])
                                P.op("dve", lambda e, c=c, b=b: e.tensor_scalar(
                                    out=sout[:, c, 22:70].rearrange("p (s k) -> p s k", k=3),
                                    in0=ps[:, b, np_:np_ + 128].rearrange("p (s k) -> p s k", k=8)[:, :, 5:8],
                                    scalar1=par(l, 0, c), scalar2=None, op0=ALU.add),
                                    reads=[("ps", b)] + PARK, writes=[("sout", c)])
                            if last:
                                P.op("dve", lambda e, c=c: e.tensor_copy(out=sout[:, c, 1:4], in_=c4h[:, c, :]),
                                     reads=[("c4h", c)], writes=[("sout", c)])
                        for d in ctx:
                            c = d["c"]
                            P.op("act", lambda e, d=d, c=c: e.activation(
                                out=d["sg"][:, 0:N], in_=ps[:, d["b_gr"], 0:N], func=AF.Silu, bias=par(l, 1, c)),
                                reads=[("ps", d["b_gr"])] + PARK, writes=[d["SG"]])
                        for d in ctx:
                            def conv4(pe, d=d):
                                xrb, b_xc, ci = d["xrb"], d["b_xc"], d["ci"]
                                for k in range(4):
                                    pe.matmul(ps[:, b_xc, 0:np_], dg[:, ci, k, :], xrb[:, k:k + np_],
                                              start=(k == 0), stop=False)
                                ins = pe.matmul(ps[:, b_xc, 0:np_], dg[:, ci, 4, :], ones[:, 0:np_],
                                                start=False, stop=True)
                                if has_s:
                                    xs = xrb[:, sbase:sbase + 176].rearrange("p (s k) -> p s k", k=11)
                                    o = ps[:, b_xc, np_:np_ + 128].rearrange("p (s k) -> p s k", k=8)
                                    for k in range(4):
                                        pe.matmul(o, dg[:, ci, k, :], xs[:, :, k:k + 8], start=(k == 0), stop=False)
                                    ins = pe.matmul(o, dg[:, ci, 4, :],
                                                    ones[:, 0:128].rearrange("p (s k) -> p s k", k=8),
                                                    start=False, stop=True)
                                return ins
                            P.op("pe", conv4, reads=[d["XRB"], ("dg", d["ci"], "rnn"), ("ones",)],
                                 writes=[("ps", d["b_xc"])])
                        for d in ctx:
                            P.op("dve", lambda e, d=d: e.tensor_copy(out=d["xcb"][:, 0:N], in_=ps[:, d["b_xc"], 0:N]),
                                 reads=[("ps", d["b_xc"])], writes=[d["XCB"]])
                        for d in ctx:
                            c = d["c"]
                            P.op("pe", lambda pe, d=d, c=c: pe.matmul(ps[:, b_ga, 0:N], wab[:, 0, c, :],
                                                                      d["xcb"][:, 0:N], start=True, stop=True),
                                 reads=[d["XCB"], ("wab", 0)], writes=[("ps", b_ga)])
                            P.op("pe", lambda pe, d=d, c=c: pe.matmul(ps[:, b_gx, 0:N], wab[:, 1, c, :],
                                                                      d["xcb"][:, 0:N], start=True, stop=True),
                                 reads=[d["XCB"], ("wab", 1)], writes=[("ps", b_gx)])
                            P.op("act", lambda e, d=d, c=c: e.activation(out=d["ta"][:, 0:N], in_=ps[:, b_ga, 0:N],
                                                                         func=AF.Tanh, bias=der[:, 0, l, c:c + 1],
                                                                         scale=0.5),
                                 reads=[("ps", b_ga)] + dk, writes=[d["TA"]])
                            P.op("act", lambda e, d=d, c=c: e.activation(out=d["tx"][:, 0:N], in_=ps[:, b_gx, 0:N],
                                                                         func=AF.Tanh, bias=der[:, 1, l, c:c + 1],
                                                                         scale=0.5),
                                 reads=[("ps", b_gx)] + dk, writes=[d["TX"]])
                        if prev_chain is not None:
                            prev_chain()
                        for d in ctx:
                            c = d["c"]
                            P.op("act", lambda e, d=d, c=c: e.activation(
                                out=d["av"][:, 0:N], in_=d["ta"][:, 0:N], func=AF.Exp,
                                bias=der[:, 3, l, c:c + 1], scale=der[:, 3, l, c:c + 1]),
                                reads=[d["TA"]] + dk, writes=[d["AV"]])
                        for d in ctx:
                            P.op("act", lambda e, d=d: e.activation(out=d["ta"][:, 0:N], in_=d["av"][:, 0:N],
                                                                    func=AF.Square),
                                 reads=[d["AV"]], writes=[d["TA"]])
                        for d in ctx:
                            P.op("act", lambda e, d=d: e.activation(out=d["ta"][:, 0:N], in_=d["ta"][:, 0:N],
                                                                    func=AF.Ln, bias=0.25, scale=-0.25),
                                 reads=[d["TA"]], writes=[d["TA"]])
                        for d in ctx:
                            P.op("act", lambda e, d=d: e.activation(out=d["ta"][:, 0:N], in_=d["ta"][:, 0:N],
                                                                    func=AF.Exp, scale=0.5),
                                 reads=[d["TA"]], writes=[d["TA"]])
                        def chain():
                            for d in ctx:
                                c, ta, tx, av, hs, sg = d["c"], d["ta"], d["tx"], d["av"], d["hs"], d["sg"]
                                TA, TX, AV, HS, SG, b_xc = d["TA"], d["TX"], d["AV"], d["HS"], d["SG"], d["b_xc"]
                                if first:
                                    P.op("dve", lambda e, ta=ta: e.memset(ta[:, 0:1], 0.5), writes=[TA])
                                P.op("dve", lambda e, tx=tx, b_xc=b_xc: e.scalar_tensor_tensor(
                                    out=tx[:, 0:N], in0=tx[:, 0:N], scalar=1.0, in1=ps[:, b_xc, 0:N],
                                    op0=ALU.add, op1=ALU.mult),
                                    reads=[TX, ("ps", b_xc)], writes=[TX])
                                P.op("dve", lambda e, tx=tx, ta=ta: e.tensor_tensor(out=tx[:, 0:N], in0=tx[:, 0:N],
                                                                                    in1=ta[:, 0:N], op=ALU.mult),
                                     reads=[TX, TA], writes=[TX])
                                P.op("dve", lambda e, hs=hs, av=av, tx=tx, c=c: e.tensor_tensor_scan(
                                    out=hs[:, 0:np_], data0=av[:, 0:np_], data1=tx[:, 0:np_],
                                    initial=hcar[:, c:c + 1], op0=ALU.mult, op1=ALU.add),
                                    reads=[AV, TX, ("hcar", c)], writes=[HS])
                                P.op("dve", lambda e, hs=hs, c=c: e.tensor_copy(out=hcar[:, c:c + 1],
                                                                                in_=hs[:, np_ - 1:np_]),
                                     reads=[HS], writes=[("hcar", c)])
                                if last:
                                    P.op("dve", lambda e, hs=hs, c=c: e.tensor_copy(out=sout[:, c, 0:1],
                                                                                    in_=hs[:, np_ - 1:np_]),
                                         reads=[HS], writes=[("sout", c)])
                                if has_s:
                                    a_s = av[:, np_:np_ + 128].rearrange("p (s k) -> p s k", k=8)
                                    b_s = tx[:, np_:np_ + 128].rearrange("p (s k) -> p s k", k=8)
                                    h_s = hs[:, np_:np_ + 128].rearrange("p (s k) -> p s k", k=8)
                                    h0 = stS[:, c, 80:96]
                                    P.op("dve", lambda e, h_s=h_s, a_s=a_s, h0=h0: e.tensor_tensor(
                                        out=h_s[:, :, 0], in0=a_s[:, :, 0], in1=h0, op=ALU.mult),
                                        reads=[AV, ("stS", c // 4)], writes=[HS])
                                    P.op("dve", lambda e, h_s=h_s, b_s=b_s: e.tensor_tensor(
                                        out=b_s[:, :, 0], in0=b_s[:, :, 0], in1=h_s[:, :, 0], op=ALU.add),
                                        reads=[HS, TX], writes=[TX])
                                    P.op("dve", lambda e, a_s=a_s: e.memset(a_s[:, :, 0], 0.0), writes=[AV])
                                    P.op("dve", lambda e, hs=hs, av=av, tx=tx: e.tensor_tensor_scan(
                                        out=hs[:, np_:np_ + 128], data0=av[:, np_:np_ + 128],
                                        data1=tx[:, np_:np_ + 128], initial=0.0, op0=ALU.mult, op1=ALU.add),
                                        reads=[AV, TX], writes=[HS])
                                    P.op("dve", lambda e, h_s=h_s, c=c: e.tensor_copy(out=sout[:, c, 6:22],
                                                                                      in_=h_s[:, :, 7]),
                                         reads=[HS], writes=[("sout", c)])
                                P.op("dve", lambda e, hs=hs, sg=sg, c=c: e.tensor_tensor(
                                    out=pb[:, c, toff:toff + N], in0=hs[:, 0:N], in1=sg[:, 0:N], op=ALU.mult),
                                    reads=[HS, SG], writes=[("p", lt, c)])
                        return chain

                    for u in tl_list:
                        rchain["f"] = rnn_body(u, itctr["rnn"] % 2, rchain["f"])
                        itctr["rnn"] += 1

                if rchain["f"] is not None:
                    rchain["f"]()
                    rchain["f"] = None
                for cp in range(4 if nphase >= 2 else 0):
                    ensure_loaded(gi)
                    ensure_loaded(gi + 1)
                    scc, sch, sgc_, scb = gslots[gi]
                    gi += 1
                    its = [(u, c) for u in tl_list for c in (2 * cp, 2 * cp + 1)]
                    pend = None
                    for (u, c) in its:
                        it = itctr["conv"]
                        itctr["conv"] += 1
                        par_ = it % 2
                        civ = cp * 4 + its.index((u, c))
                        N, np_, has_s = seg_views(u)
                        toff = TILES[u][1]
                        lt = u % 2
                        cc_ = c % 2
                        b_cc, b_ch, b_gc, b_cb, b_v = 0, 1, 2 + par_, 4 + par_, 6 + par_
                        last = (u == 5)
                        ccs, sgc = tf[:, 6 * par_ + 0, :], tf[:, 6 * par_ + 1, :]
                        CCS, SGC = ("tf", 6 * par_ + 0), ("tf", 6 * par_ + 1)
                        ub = tb[:, 2 * par_, :]
                        UB = ("tb", 2 * par_)
                        sbase = 2 + np_

                        def mm(pe, slot, bank, N=N, toff=toff, cc_=cc_):
                            for k in range(NCH):
                                ins = pe.matmul(ps[:, bank, 0:N], wring[:, slot, k, cc_ * 128:(cc_ + 1) * 128],
                                                xb[:, k, toff:toff + N], start=(k == 0), stop=(k == NCH - 1))
                            return ins
                        for (s_, b_) in ((scc, b_cc), (sch, b_ch), (sgc_, b_gc), (scb, b_cb)):
                            P.op("pe", lambda pe, mm=mm, s=s_, b=b_: mm(pe, s, b),
                                 reads=[("ws", s_), ("xb", lt)], writes=[("ps", b_)])
                        if pend is not None:
                            pend()
                        if u == tl_list[0]:
                            build_diag(l, c, par_, "conv")
                        P.op("act", lambda e, ccs=ccs, c=c, N=N: e.activation(
                            out=ccs[:, 0:N], in_=ps[:, b_cc, 0:N], func=AF.Identity, bias=par(l, 3, c)),
                            reads=[("ps", b_cc)] + PARK, writes=[CCS])
                        P.op("act", lambda e, ub=ub, c=c: e.activation(out=ub[:, 0:2], in_=c3h[:, c, :],
                                                                       func=AF.Identity),
                             reads=[("c3h", c)], writes=[UB])
                        if has_s:
                            P.op("act", lambda e, ub=ub, c=c, sbase=sbase: e.activation(
                                out=ub[:, sbase:sbase + 160].rearrange("p (s k) -> p s k", k=10)[:, :, 0:2],
                                in_=stS[:, c, 48:80].rearrange("p (s k) -> p s k", k=2), func=AF.Identity),
                                reads=[("stS", c // 4)], writes=[UB])
                        P.op("dve", lambda e, ub=ub, ccs=ccs, c=c, np_=np_: e.scalar_tensor_tensor(
                            out=ub[:, 2:2 + np_], in0=ps[:, b_ch, 0:np_], scalar=par(l, 4, c), in1=ccs[:, 0:np_],
                            op0=ALU.add, op1=ALU.mult),
                            reads=[("ps", b_ch), CCS] + PARK, writes=[UB])
                        P.op("dve", lambda e, ccs=ccs, c=c, np_=np_: e.scalar_tensor_tensor(
                            out=c3h[:, c, :], in0=ps[:, b_ch, np_ - 2:np_], scalar=par(l, 4, c),
                            in1=ccs[:, np_ - 2:np_], op0=ALU.add, op1=ALU.mult),
                            reads=[("ps", b_ch), CCS] + PARK, writes=[("c3h", c)])
                        if has_s:
                            P.op("dve", lambda e, ub=ub, ccs=ccs, c=c, np_=np_, sbase=sbase: e.scalar_tensor_tensor(
                                out=ub[:, sbase:sbase + 160].rearrange("p (s k) -> p s k", k=10)[:, :, 2:10],
                                in0=ps[:, b_ch, np_:np_ + 128].rearrange("p (s k) -> p s k", k=8),
                                scalar=par(l, 4, c),
                                in1=ccs[:, np_:np_ + 128].rearrange("p (s k) -> p s k", k=8),
                                op0=ALU.add, op1=ALU.mult),
                                reads=[("ps", b_ch), CCS] + PARK, writes=[UB])
                            P.op("dve", lambda e, ccs=ccs, c=c, np_=np_: e.scalar_tensor_tensor(
                                out=sout[:, c, 70:102].rearrange("p (s k) -> p s k", k=2),
                                in0=ps[:, b_ch, np_:np_ + 128].rearrange("p (s k) -> p s k", k=8)[:, :, 6:8],
                                scalar=par(l, 4, c),
                                in1=ccs[:, np_:np_ + 128].rearrange("p (s k) -> p s k", k=8)[:, :, 6:8],
                                op0=ALU.add, op1=ALU.mult),
                                reads=[("ps", b_ch), CCS] + PARK, writes=[("sout", c)])
                        if last:
                            P.op("dve", lambda e, c=c: e.tensor_copy(out=sout[:, c, 4:6], in_=c3h[:, c, :]),
                                 reads=[("c3h", c)], writes=[("sout", c)])
                        P.op("act", lambda e, sgc=sgc, c=c, N=N, b=b_gc: e.activation(
                            out=sgc[:, 0:N], in_=ps[:, b, 0:N], func=AF.Silu, bias=par(l, 5, c)),
                            reads=[("ps", b_gc)] + PARK, writes=[SGC])
                        P.op("dve", lambda e, sgc=sgc, c=c, N=N, b=b_cb: e.scalar_tensor_tensor(
                            out=sgc[:, 0:N], in0=ps[:, b, 0:N], scalar=par(l, 2, c), in1=sgc[:, 0:N],
                            op0=ALU.add, op1=ALU.mult),
                            reads=[("ps", b_cb), SGC] + PARK, writes=[SGC])
                        if pending_norm:
                            (u_, si_, ll_) = pending_norm[civ % 2]
                            ph3_norm_chunk(u_, si_, civ // 2, ll_)
                            if civ == 2 * NCH - 1:
                                del pending_norm[:]

                        def pe2(u=u, c=c, par_=par_, N=N, np_=np_, has_s=has_s, ub=ub, UB=UB, b_v=b_v, sgc=sgc,
                                SGC=SGC, toff=toff, lt=lt, sbase=sbase):
                            def conv3(pe):
                                for k in range(3):
                                    ins = pe.matmul(ps[:, b_v, 0:np_], dg[:, par_, 5 + k, :], ub[:, k:k + np_],
                                                    start=(k == 0), stop=(k == 2))
                                if has_s:
                                    us = ub[:, sbase:sbase + 160].rearrange("p (s k) -> p s k", k=10)
                                    o = ps[:, b_v, np_:np_ + 128].rearrange("p (s k) -> p s k", k=8)
                                    for k in range(3):
                                        ins = pe.matmul(o, dg[:, par_, 5 + k, :], us[:, :, k:k + 8],
                                                        start=(k == 0), stop=(k == 2))
                                return ins
                            P.op("pe", conv3, reads=[UB, ("dg", par_, "conv")], writes=[("ps", b_v)])
                            P.op("dve", lambda e: e.tensor_tensor(out=qb[:, c, toff:toff + N], in0=sgc[:, 0:N],
                                                                  in1=ps[:, b_v, 0:N], op=ALU.mult),
                                 reads=[SGC, ("ps", b_v)], writes=[("q", lt, c)])
                        pend = pe2
                    if pend is not None:
                        pend()
                        pend = None

                for jp in range(4 if nphase >= 3 else 0):
                    ensure_loaded(gi)
                    ensure_loaded(gi + 1)
                    sro, sco, sg1, sg2 = gslots[gi]
                    gi += 1
                    for u in tl_list:
                        for j in (2 * jp, 2 * jp + 1):
                            it = itctr["ph2"]
                            itctr["ph2"] += 1
                            par_ = it % 2
                            N, np_, has_s = seg_views(u)
                            toff = TILES[u][1]
                            lt = u % 2
                            jj = j % 2
                            b_yr, b_yc, b_g1, b_g2 = par_, 2 + par_, 4 + par_, 6 + par_
                            s1, s2 = tf[:, 6 * par_ + 0, :], tf[:, 6 * par_ + 1, :]
                            S1K, S2K = ("tf", 6 * par_ + 0), ("tf", 6 * par_ + 1)

                            def mm(pe, slot, bank, src, N=N, toff=toff, jj=jj):
                                for k in range(NCH):
                                    ins = pe.matmul(ps[:, bank, 0:N], wring[:, slot, k, jj * 128:(jj + 1) * 128],
                                                    src[:, k, toff:toff + N], start=(k == 0), stop=(k == NCH - 1))
                                return ins
                            P.op("pe", lambda pe, mm=mm, s=sg1, b=b_g1: mm(pe, s, b, xb),
                                 reads=[("ws", sg1), ("xb", lt)], writes=[("ps", b_g1)])
                            P.op("pe", lambda pe, mm=mm, s=sg2, b=b_g2: mm(pe, s, b, xb),
                                 reads=[("ws", sg2), ("xb", lt)], writes=[("ps", b_g2)])
                            P.op("pe", lambda pe, mm=mm, s=sro, b=b_yr: mm(pe, s, b, pb),
                                 reads=[("ws", sro)] + [("p", lt, c) for c in range(NCH)], writes=[("ps", b_yr)])
                            P.op("pe", lambda pe, mm=mm, s=sco, b=b_yc: mm(pe, s, b, qb),
                                 reads=[("ws", sco)] + [("q", lt, c) for c in range(NCH)], writes=[("ps", b_yc)])
                            P.op("act", lambda e, s1=s1, j=j, N=N, b=b_g1: e.activation(
                                out=s1[:, 0:N], in_=ps[:, b, 0:N], func=AF.Sigmoid, bias=par(l, 6, j)),
                                reads=[("ps", b_g1)] + PARK, writes=[S1K])
                            P.op("act", lambda e, s2=s2, j=j, N=N, b=b_g2: e.activation(
                                out=s2[:, 0:N], in_=ps[:, b, 0:N], func=AF.Sigmoid, bias=par(l, 7, j)),
                                reads=[("ps", b_g2)] + PARK, writes=[S2K])
                            P.op("dve", lambda e, s1=s1, N=N, b=b_yr: e.tensor_tensor(
                                out=s1[:, 0:N], in0=s1[:, 0:N], in1=ps[:, b, 0:N], op=ALU.mult),
                                reads=[S1K, ("ps", b_yr)], writes=[S1K])
                            P.op("dve", lambda e, s2=s2, N=N, b=b_yc: e.tensor_tensor(
                                out=s2[:, 0:N], in0=s2[:, 0:N], in1=ps[:, b, 0:N], op=ALU.mult),
                                reads=[S2K, ("ps", b_yc)], writes=[S2K])
                            P.op("dve", lambda e, s1=s1, s2=s2, N=N, j=j, toff=toff: e.tensor_tensor(
                                out=mb[:, j, toff:toff + N], in0=s1[:, 0:N], in1=s2[:, 0:N], op=ALU.add),
                                reads=[S1K, S2K], writes=[("m", lt, j)])

                ensure_loaded(gi)
                ensure_loaded(gi + 1)
                swo = gslots[gi]
                gi += 1
                nxt = (l, pa + 1) if pa < npass - 1 else ((l + 1, 0) if l + 1 < depth else None)
                def ph3_loop1(u, si, swo=swo, cast_pa=None, hook=None):
                    b_S1, b_S2 = 2 + 2 * si, 3 + 2 * si
                    N, np_, has_s = seg_views(u)
                    toff = TILES[u][1]
                    lt = u % 2
                    segs = TILES[u][2]
                    pend_s = None
                    for j in range(NCH):
                        it = itctr["ph3"]
                        itctr["ph3"] += 1
                        par_ = it % 2
                        b_o = par_
                        vb, vsq = tb[:, 2 * par_, :], tb[:, 2 * par_ + 1, :]
                        VB, VSQ = ("tb", 2 * par_), ("tb", 2 * par_ + 1)

                        def mmo(pe, j=j, b_o=b_o):
                            slot = swo[j // 2]
                            jj = j % 2
                            for k in range(NCH):
                                ins = pe.matmul(ps[:, b_o, 0:N], wring[:, slot, k, jj * 128:(jj + 1) * 128],
                                                mb[:, k, toff:toff + N], start=(k == 0), stop=(k == NCH - 1))
                            return ins
                        P.op("pe", mmo, reads=[("ws", swo[j // 2])] + [("m", lt, k) for k in range(NCH)],
                             writes=[("ps", b_o)])
                        if pend_s is not None:
                            pend_s()
                        for (kind, g0, n, l0) in segs:
                            P.op("dve", lambda e, j=j, g0=g0, n=n, l0=l0, b_o=b_o: e.scalar_tensor_tensor(
                                out=x32[:, j, g0:g0 + n], in0=x32[:, j, g0:g0 + n], scalar=ALPHA,
                                in1=ps[:, b_o, l0:l0 + n], op0=ALU.mult, op1=ALU.add),
                                reads=[("x32", u, j), ("ps", b_o)], writes=[("x32", u, j)])
                        for (kind, g0, n, l0) in segs:
                            P.op("act", lambda e, j=j, g0=g0, n=n, l0=l0, vb=vb: e.activation(
                                out=vb[:, l0:l0 + n], in_=x32[:, j, g0:g0 + n], func=AF.Copy),
                                reads=[("x32", u, j)], writes=[VB])
                            P.op("act", lambda e, j=j, g0=g0, n=n, l0=l0, vsq=vsq: e.activation(
                                out=vsq[:, l0:l0 + n], in_=x32[:, j, g0:g0 + n], func=AF.Square),
                                reads=[("x32", u, j)], writes=[VSQ])

                        if cast_pa is not None:
                            cast_xb(cast_pa, "act", j)

                        def smm(j=j, vb=vb, vsq=vsq, VB=VB, VSQ=VSQ):
                            P.op("pe", lambda pe: pe.matmul(
                                ps[:, b_S1, 0:N], ones[:, 0:128], vb[:, 0:N], start=(j == 0), stop=(j == NCH - 1)),
                                reads=[VB, ("ones",)], writes=[("ps", b_S1)])
                            P.op("pe", lambda pe: pe.matmul(
                                ps[:, b_S2, 0:N], ones[:, 0:128], vsq[:, 0:N], start=(j == 0), stop=(j == NCH - 1)),
                                reads=[VSQ, ("ones",)], writes=[("ps", b_S2)])
                        pend_s = smm
                        if hook is not None and j == 1:
                            hook()
                    pend_s()

                def ph3_stats(u, si):
                    N, np_, has_s = seg_views(u)
                    b_S1, b_S2 = 2 + 2 * si, 3 + 2 * si
                    mean, msq = tf[:, 8, :], tf[:, 9, :]
                    MEAN, MSQ = ("tf", 8), ("tf", 9)
                    Av, Bv = nrm[:, 2 * si, :], nrm[:, 2 * si + 1, :]
                    SK = ("stg", si)
                    P.op("dve", lambda e: e.tensor_scalar(out=mean[:, 0:N], in0=ps[:, b_S1, 0:N],
                                                          scalar1=1.0 / D, scalar2=None, op0=ALU.mult),
                         reads=[("ps", b_S1)], writes=[MEAN])
                    P.op("act", lambda e: e.activation(out=msq[:, 0:N], in_=ps[:, b_S1, 0:N], func=AF.Square,
                                                       scale=1.0 / D),
                         reads=[("ps", b_S1)], writes=[MSQ])
                    P.op("dve", lambda e: e.scalar_tensor_tensor(out=msq[:, 0:N], in0=ps[:, b_S2, 0:N],
                                                                 scalar=1.0 / D, in1=msq[:, 0:N],
                                                                 op0=ALU.mult, op1=ALU.subtract),
                         reads=[("ps", b_S2), MSQ], writes=[MSQ])
                    P.op("act", lambda e: e.activation(out=msq[:, 0:N], in_=msq[:, 0:N], func=AF.Ln,
                                                       bias=epst[:, 0:1], scale=1.0),
                         reads=[MSQ, ("eps",)], writes=[MSQ])
                    P.op("act", lambda e: e.activation(out=Av[:, 0:N], in_=msq[:, 0:N], func=AF.Exp, scale=-0.5),
                         reads=[MSQ], writes=[SK])
                    P.op("dve", lambda e: e.scalar_tensor_tensor(out=Bv[:, 0:N], in0=mean[:, 0:N], scalar=-1.0,
                                                                 in1=Av[:, 0:N], op0=ALU.mult, op1=ALU.mult),
                         reads=[MEAN, SK], writes=[SK])

                u0, u1 = tl_list
                flush_deferred()
                ph3_loop1(u0, 0, cast_pa=(nxt[1] if nxt is not None else None))
                ph3_loop1(u1, 1, hook=lambda: ph3_stats(u0, 0))
                ph3_stats(u1, 1)
                pending_norm.extend([(u0, 0, l), (u1, 1, l)])
                if pa == npass - 1 and l == depth - 1:
                    flush_deferred()

            return gi

        def cast_xb(pa, eng="pool", only_k=None):
            for u in PASS_TILES[pa]:
                toff = TILES[u][1]
                lt = u % 2
                for (kind, g0, n, l0) in TILES[u][2]:
                    ks = range(NCH) if only_k is None else [only_k]
                    if eng == "pool":
                        P.op("pool", lambda e, g0=g0, n=n, o=toff + l0: e.tensor_copy(
                            out=xb[:, :, o:o + n], in_=x32[:, :, g0:g0 + n]),
                            reads=x32keys(u), writes=[("xb", lt)])
                    else:
                        for k in ks:
                            P.op("act", lambda e, g0=g0, n=n, o=toff + l0, k=k: e.activation(
                                out=xb[:, k, o:o + n], in_=x32[:, k, g0:g0 + n], func=AF.Copy),
                                reads=[("x32", u, k)], writes=[("xb", lt)])

        def store_states(l):
            def tr(pe):
                for c in range(NCH):
                    ins = pe.transpose(ps[0:102, 4 + c // 4, (c % 4) * 128:(c % 4 + 1) * 128], sout[:, c, :], ident[:])
                return ins
            P.op("pe", tr, reads=[("sout", c) for c in range(NCH)] + [("ident",)], writes=[("ps", 4), ("ps", 5)])
            for h in range(2):
                P.op("act", lambda e, h=h: e.activation(out=stg_out[0:102, 512 * h:512 * (h + 1)],
                                                        in_=ps[0:102, 4 + h, :], func=AF.Identity),
                     reads=[("ps", 4 + h)], writes=SOUTK)
            P.op("sp", lambda e: e.dma_start(out=st_out[l], in_=stg_out[0:102, :]),
                 reads=SOUTK, writes=[("st_out", l)], dma="out0")

        ensure_loaded(0)
        ensure_loaded(1)
        cast_xb(0)
        gi = 0
        for l in range(depth):
            gi = run_layer(l, gi)
            store_states(l)

        for b in range(NBLK):
            s = b % 6
            sap, skeys = XS[s]
            pbk = 2 * (b % 4)
            rk = []
            for u in tiles_of_block(b):
                rk += x32keys(u)

            def tr_out(pe, b=b, pbk=pbk):
                for c in range(NCH):
                    ins = pe.transpose(ps[:, pbk + c // 4, (c % 4) * 128:(c % 4 + 1) * 128],
                                       x32[:, c, b * 128:(b + 1) * 128], ident[:])
                return ins
            P.op("pe", tr_out, reads=rk + [("ident",)], writes=[("ps", pbk), ("ps", pbk + 1)])
            for h in range(2):
                eng = "act" if h == 0 else "dve"

                def ev(e, sap=sap, h=h, pbk=pbk, eng=eng):
                    o = sap[:, 512 * h:512 * (h + 1)]
                    i = ps[:, pbk + h, :]
                    if eng == "act":
                        return e.activation(out=o, in_=i, func=AF.Identity)
                    return e.tensor_copy(out=o, in_=i)
                P.op(eng, ev, reads=[("ps", pbk + h)], writes=skeys)
            P.op("sp", lambda e, b=b, sap=sap: e.dma_start(out=y[b * 128:(b + 1) * 128, :], in_=sap),
                 reads=skeys, writes=[("y", b)], dma="out%d" % s)
        P.op("sp", lambda e: e.nop(), reads=[("y", b) for b in range(NBLK)] + [("st_out", l) for l in range(depth)])

        P.finalize()
        with nc.Block() as block:
            @block.tensor
            def _(e):
                P.emit_stream("pe", e, esem, dsem)

            @block.scalar
            def _(e):
                P.emit_stream("act", e, esem, dsem)

            @block.vector
            def _(e):
                P.emit_stream("dve", e, esem, dsem)

            @block.gpsimd
            def _(e):
                P.emit_stream("pool", e, esem, dsem)

            @block.sync
            def _(e):
                P.emit_stream("sp", e, esem, dsem)
    return nc


_NC_CACHE = {}


def _prep_inputs(inputs):
    f = lambda k: np.ascontiguousarray(np.asarray(inputs[k], dtype=np.float32))
    x_prompt, x_sample = f("x_prompt"), f("x_sample")
    s_h, s_c4, s_c3 = f("state_rglru"), f("state_conv4"), f("state_conv3")
    rows = []
    for l in range(DEPTH):
        rows.append(f("b_in")[l].reshape(8, D))
        rows.append(f("conv4_w")[l].reshape(4, D))
        rows.append(f("conv4_b")[l].reshape(1, D))
        rows.append(f("b_rg_a")[l].reshape(1, D))
        rows.append(f("b_rg_x")[l].reshape(1, D))
        rows.append(f("rg_lambda")[l].reshape(1, D))
        rows.append(f("conv3_w")[l].reshape(3, D))
        rows.append(f("ln_g")[l].reshape(1, D))
        rows.append(f("ln_b")[l].reshape(1, D))
    params = np.ascontiguousarray(np.concatenate(rows, axis=0))
    shared = {
        "params": params, "w_in": f("w_in"), "w_ro": f("w_rnn_out"), "w_co": f("w_conv_out"), "w_o": f("w_out"),
        "w_a": f("w_rg_a"), "w_x": f("w_rg_x"), "ident": np.eye(128, dtype=np.float32),
    }
    in_maps = []
    for c in range(8):
        sl = slice(16 * c, 16 * c + 16)
        xin = np.concatenate([x_prompt[c], x_sample[sl].reshape(128, D)], axis=0)
        st = np.concatenate([s_c4[:, sl].reshape(DEPTH, 48, D), s_c3[:, sl].reshape(DEPTH, 32, D),
                             s_h[:, sl].reshape(DEPTH, 16, D)], axis=1)
        m = dict(shared)
        m["xin"] = np.ascontiguousarray(xin)
        m["st_in"] = np.ascontiguousarray(st)
        in_maps.append(m)
    return in_maps


def kernel(**inputs):
    if "nc" not in _NC_CACHE:
        _NC_CACHE["nc"] = build_nc()
    nc = _NC_CACHE["nc"]
    in_maps = _prep_inputs(inputs)
    res = run_bass_kernel_spmd(nc, in_maps, core_ids=list(range(8)))
    ys = [np.asarray(r["y"]) for r in res.results]
    sts = [np.asarray(r["st_out"]) for r in res.results]
    y_prompt = np.stack([yy[0:NPR] for yy in ys], axis=0)
    y_sample = np.concatenate([yy[NPR:].reshape(16, 8, D) for yy in ys], axis=0)
    ph = np.stack([s[:, 0] for s in sts], axis=1)
    pc4 = np.stack([s[:, 1:4] for s in sts], axis=1)
    pc3 = np.stack([s[:, 4:6] for s in sts], axis=1)
    sh = np.concatenate([s[:, 6:22] for s in sts], axis=1)
    sc4 = np.concatenate([s[:, 22:70].reshape(DEPTH, 16, 3, D) for s in sts], axis=1)
    sc3 = np.concatenate([s[:, 70:102].reshape(DEPTH, 16, 2, D) for s in sts], axis=1)
    f32 = lambda a: np.ascontiguousarray(a, dtype=np.float32)
    return (f32(y_prompt), f32(y_sample), f32(ph), f32(pc4), f32(pc3), f32(sh), f32(sc4), f32(sc3))
```

```python
import contextlib
import numpy as np
import concourse.bass as bass
import concourse.mybir as mybir
from concourse.bass_utils import run_bass_kernel_spmd

F32 = mybir.dt.float32
BF16 = mybir.dt.bfloat16
AF = mybir.ActivationFunctionType
ALU = mybir.AluOpType

DEPTH = 4
D = 1024
NCH = 8
NPR = 2048
NSM = 128
NTOK = NPR + NSM
ALPHA = (2.0 * DEPTH) ** 0.25
EPS = 1e-5
NMAX = 384
NSLOT = 8
UW = 256
NPAR = 21

TILES = [
    (0, 0, [("P", 0, 384, 0)]),
    (0, 384, [("P", 384, 256, 0), ("S", 2048, 128, 256)]),
    (1, 0, [("P", 640, 352, 0)]),
    (1, 352, [("P", 992, 352, 0)]),
    (2, 0, [("P", 1344, 352, 0)]),
    (2, 352, [("P", 1696, 352, 0)]),
]
PASS_TILES = {0: [0, 1], 1: [2, 3], 2: [4, 5]}
PASSW = 768


def tile_n(u):
    return sum(s[2] for s in TILES[u][2])


class _Op:
    __slots__ = ("eng", "fn", "deps", "sig", "cnt", "dma", "dma_ord")


class Prog:
    ENGS = ("pe", "act", "dve", "pool", "sp")

    def __init__(self):
        self.streams = {e: [] for e in self.ENGS}
        self.last_w = {}
        self.readers = {}
        self.dma_last = {}
        self.dma_cnt = {}

    def op(self, eng, fn, reads=(), writes=(), dma=None):
        o = _Op()
        o.eng, o.fn, o.sig, o.cnt, o.dma, o.dma_ord = eng, fn, False, 0, dma, 0
        deps = []
        for r in reads:
            w = self.last_w.get(r)
            if w is not None:
                deps.append(w)
            if r[0] == "ps":
                rd = self.readers.get(r)
                if rd:
                    deps.extend(v for k, v in rd[0].items() if k != eng)
        for r in writes:
            w = self.last_w.get(r)
            if w is not None:
                deps.append(w)
            rd = self.readers.get(r)
            if rd:
                deps.extend(rd[0].values())
                deps.extend(rd[1])
        if dma is not None:
            prev = self.dma_last.get(dma)
            if prev is not None:
                deps.append(prev)
            self.dma_last[dma] = o
            self.dma_cnt[dma] = self.dma_cnt.get(dma, 0) + 1
            o.dma_ord = self.dma_cnt[dma]
        seen = set()
        o.deps = []
        for d in deps:
            if id(d) in seen or d is o:
                continue
            seen.add(id(d))
            if eng == "pe" and d.eng == "pe" and d.dma is None and dma is None:
                continue
            d.sig = True
            o.deps.append(d)
        for r in writes:
            self.last_w[r] = o
            self.readers[r] = ({}, [])
        for r in reads:
            rd = self.readers.setdefault(r, ({}, []))
            if dma is None:
                rd[0][eng] = o
            else:
                rd[1].append(o)
        self.streams[eng].append(o)
        return o

    def finalize(self):
        for e in self.ENGS:
            c = 0
            for o in self.streams[e]:
                if o.sig and o.dma is None:
                    c += 1
                o.cnt = c

    def emit_stream(self, e, eng, esem, dsem):
        waited = {}
        for o in self.streams[e]:
            for d in o.deps:
                if d.dma is not None:
                    key, sem, val = ("d", d.dma), dsem[d.dma], 16 * d.dma_ord
                else:
                    key, sem, val = ("e", d.eng), esem[d.eng], d.cnt
                if waited.get(key, 0) < val:
                    eng.wait_ge(sem, val)
                    waited[key] = val
            ins = o.fn(eng)
            if o.dma is not None:
                ins.then_inc(dsem[o.dma], 16)
            elif o.sig:
                ins.then_inc(esem[e], 1)


def build_nc(depth=DEPTH, npass=3, nphase=4, small=True, p3=9):
    nc = bass.Bass("TRN2", target_bir_lowering=False)
    xin = nc.dram_tensor("xin", [NTOK, D], F32, kind="ExternalInput").ap()
    st_in = nc.dram_tensor("st_in", [DEPTH, 96, D], F32, kind="ExternalInput").ap()
    params = nc.dram_tensor("params", [DEPTH * NPAR, D], F32, kind="ExternalInput").ap()
    w_in = nc.dram_tensor("w_in", [DEPTH, D, 8 * D], F32, kind="ExternalInput").ap()
    w_ro = nc.dram_tensor("w_ro", [DEPTH, D, D], F32, kind="ExternalInput").ap()
    w_co = nc.dram_tensor("w_co", [DEPTH, D, D], F32, kind="ExternalInput").ap()
    w_o = nc.dram_tensor("w_o", [DEPTH, D, D], F32, kind="ExternalInput").ap()
    w_a = nc.dram_tensor("w_a", [DEPTH, NCH, 128, 128], F32, kind="ExternalInput").ap()
    w_x = nc.dram_tensor("w_x", [DEPTH, NCH, 128, 128], F32, kind="ExternalInput").ap()
    ident_d = nc.dram_tensor("ident", [128, 128], F32, kind="ExternalInput").ap()
    y = nc.dram_tensor("y", [NTOK, D], F32, kind="ExternalOutput").ap()
    st_out = nc.dram_tensor("st_out", [DEPTH, 102, D], F32, kind="ExternalOutput").ap()

    P = Prog()
    es = contextlib.ExitStack()
    with es:
        def sb(name, shape, dt):
            return es.enter_context(nc.sbuf_tensor(name, shape, dt))

        x32 = sb("x32", [128, NCH, NTOK], F32)
        xb = sb("xb", [128, NCH, PASSW], BF16)
        pb = sb("pb", [128, NCH, PASSW], BF16)
        qb = sb("qb", [128, NCH, PASSW], BF16)
        mb = sb("mb", [128, NCH, PASSW], BF16)
        wring = sb("wring", [128, NSLOT, NCH, UW], BF16)
        wab = sb("wab", [128, 2, NCH, 128], BF16)
        dg = sb("dg", [128, 2, 8, 128], BF16)
        parT = sb("parT", [128, NCH, DEPTH * NPAR], F32)
        der = sb("der", [128, 4, DEPTH, NCH], F32)
        ident = sb("identf", [128, 128], F32)
        identb = sb("identb", [128, 128], BF16)
        ones = sb("ones", [128, NMAX], BF16)
        stS = sb("stS", [128, NCH, 96], F32)
        sout = sb("sout", [128, NCH, 102], F32)
        c4h = sb("c4h", [128, NCH, 3], F32)
        c3h = sb("c3h", [128, NCH, 2], F32)
        hcar = sb("hcar", [128, NCH], F32)
        epst = sb("epst", [128, 1], F32)
        stage = sb("stage", [128, 2, D], F32)
        NTF, NTB = 18, 4
        tf = sb("tf", [128, NTF, NMAX], F32)
        tb = sb("tb", [128, NTB, 448], BF16)
        ps = es.enter_context(nc.psum_tensor("ps", [128, 8, 512], F32))

        tff0 = tf[:].rearrange("p s n -> p (s n)")
        XS = [(stage[:, 0, :], [("stg", 0)]), (stage[:, 1, :], [("stg", 1)])]
        for i_ in range(4):
            XS.append((tff0[:, i_ * 3 * NMAX:i_ * 3 * NMAX + D], [("tf", 3 * i_ + k_) for k_ in range(3)]))
        esem = {e: es.enter_context(nc.semaphore("s_" + e)) for e in ("pe", "act", "dve", "pool")}
        dnames = (["w%d" % i for i in range(NSLOT)] + ["stg%d" % i for i in range(6)] + ["misc", "wab"]
                  + ["out%d" % i for i in range(6)])
        dsem = {n: es.enter_context(nc.semaphore("d_" + n)) for n in dnames}

        def par(l, r, c):
            return parT[:, c, l * NPAR + r: l * NPAR + r + 1]

        P.op("sp", lambda e: e.dma_start(out=ident[:], in_=ident_d), writes=[("ident",)], dma="misc")
        P.op("sp", lambda e: e.dma_start(out=stage[0:DEPTH * NPAR, 0, :], in_=params),
             writes=[("stg", 0)], dma="stg0")
        P.op("dve", lambda e: e.tensor_copy(out=identb[:], in_=ident[:]), reads=[("ident",)],
             writes=[("identb",)])
        P.op("dve", lambda e: e.memset(ones[:], 1.0), writes=[("ones",)])
        P.op("dve", lambda e: e.memset(epst[:], EPS), writes=[("eps",)])

        def tr_params(pe):
            for c in range(NCH):
                ins = pe.transpose(ps[:, c // 4, (c % 4) * 128:(c % 4) * 128 + DEPTH * NPAR],
                                   stage[0:DEPTH * NPAR, 0, c * 128:(c + 1) * 128],
                                   ident[0:DEPTH * NPAR, 0:DEPTH * NPAR])
            return ins
        P.op("pe", tr_params, reads=[("stg", 0), ("ident",)], writes=[("ps", 0), ("ps", 1)])
        for h in range(2):
            P.op("act", lambda e, h=h: e.activation(
                out=parT[:, 4 * h:4 * h + 4, :],
                in_=ps[:, h, :].rearrange("p (c n) -> p c n", n=128)[:, :, 0:DEPTH * NPAR],
                func=AF.Identity), reads=[("ps", h)], writes=[("parT", h)])
        PARK = [("parT", 0), ("parT", 1)]
        for l in range(DEPTH):
            def mk(l):
                def v(ri):
                    return parT[:, :, l * NPAR + ri]
                P.op("act", lambda e: e.activation(out=der[:, 0, l, :], in_=v(13), func=AF.Identity, scale=0.5),
                     reads=PARK, writes=[("der", l, 0)])
                P.op("act", lambda e: e.activation(out=der[:, 1, l, :], in_=v(14), func=AF.Identity, scale=0.5),
                     reads=PARK, writes=[("der", l, 1)])
                P.op("act", lambda e: e.activation(out=der[:, 2, l, :], in_=v(15), func=AF.Exp, scale=-1.0),
                     reads=PARK, writes=[("der", l, 2)])
                P.op("act", lambda e: e.activation(out=der[:, 3, l, :], in_=der[:, 2, l, :], func=AF.Ln,
                                                   bias=1.0, scale=1.0),
                     reads=[("der", l, 2)], writes=[("der", l, 3)])
                P.op("act", lambda e: e.activation(out=der[:, 2, l, :], in_=der[:, 3, l, :], func=AF.Identity,
                                                   scale=-8.0),
                     reads=[("der", l, 3)], writes=[("der", l, 2)])
                P.op("act", lambda e: e.activation(out=der[:, 3, l, :], in_=der[:, 2, l, :], func=AF.Identity,
                                                   scale=0.5),
                     reads=[("der", l, 2)], writes=[("der", l, 3)])
            mk(l)
        DERK = lambda l: [("der", l, i) for i in range(4)]

        def tiles_of_block(b):
            g0, g1 = b * 128, (b + 1) * 128
            res = []
            for u, (_, _, segs) in enumerate(TILES):
                for (_, sg0, n, _) in segs:
                    if sg0 < g1 and sg0 + n > g0:
                        res.append(u)
            return sorted(set(res))

        def x32keys(u):
            return [("x32", u, c) for c in range(NCH)]

        NBLK = NTOK // 128
        for b in range(NBLK):
            s = b % 6
            sap, skeys = XS[s]
            P.op("sp", lambda e, b=b, sap=sap: e.dma_start(out=sap, in_=xin[b * 128:(b + 1) * 128, :]),
                 writes=skeys, dma="stg%d" % s)
            pbk = 2 * (b % 4)

            def tr_in(pe, sap=sap, pbk=pbk):
                for c in range(NCH):
                    ins = pe.transpose(ps[:, pbk + c // 4, (c % 4) * 128:(c % 4 + 1) * 128],
                                       sap[:, c * 128:(c + 1) * 128], ident[:])
                return ins
            P.op("pe", tr_in, reads=skeys + [("ident",)], writes=[("ps", pbk), ("ps", pbk + 1)])
            wk = []
            for u in tiles_of_block(b):
                wk += x32keys(u)
            for h in range(2):
                eng = "act" if h == 0 else "dve"

                def ev(e, b=b, h=h, pbk=pbk, eng=eng):
                    o = x32[:, 4 * h:4 * h + 4, b * 128:(b + 1) * 128]
                    i = ps[:, pbk + h, :].rearrange("p (c n) -> p c n", n=128)
                    if eng == "act":
                        return e.activation(out=o, in_=i, func=AF.Identity)
                    return e.tensor_copy(out=o, in_=i)
                P.op(eng, ev, reads=[("ps", pbk + h)], writes=[k for k in wk if k[2] // 4 == h])

        wstate = {"n": 0}

        def w_unit_ap(l, kind, col):
            if kind == "in":
                src = w_in[l]
            elif kind == "ro":
                src = w_ro[l]
            elif kind == "co":
                src = w_co[l]
            else:
                src = w_o[l]
            return src.rearrange("(k p) n -> p k n", p=128)[:, :, col:col + UW]

        def load_group(units):
            slots = []
            for (l, kind, col) in units:
                s = wstate["n"] % NSLOT
                wstate["n"] += 1
                src = w_unit_ap(l, kind, col)
                P.op("pool", lambda e, s=s, src=src: e.dma_start(out=wring[:, s, :, :], in_=src),
                     writes=[("ws", s)], dma="w%d" % s)
                slots.append(s)
            return slots

        groups = []
        for l in range(depth):
            for pa in range(npass):
                for cp in range(4 if nphase >= 1 else 0):
                    groups.append(("rnn", l, pa, cp, [(l, "in", 0 * D + cp * UW), (l, "in", 1 * D + cp * UW)]))
                for cp in range(4 if nphase >= 2 else 0):
                    groups.append(("conv", l, pa, cp, [(l, "in", g * D + cp * UW) for g in (3, 4, 5, 2)]))
                for jp in range(4 if nphase >= 3 else 0):
                    groups.append(("ph2", l, pa, jp, [(l, "ro", jp * UW), (l, "co", jp * UW),
                                                      (l, "in", 6 * D + jp * UW), (l, "in", 7 * D + jp * UW)]))
                if nphase >= 4:
                    groups.append(("ph3", l, pa, 0, [(l, "o", jp * UW) for jp in range(4)]))
        gslots = {}

        def ensure_loaded(gi):
            if gi < len(groups) and gi not in gslots:
                gslots[gi] = load_group(groups[gi][4])

        def seg_views(u):
            segs = TILES[u][2]
            np_ = segs[0][2]
            has_s = len(segs) > 1
            return tile_n(u), np_, has_s

        def load_layer_small(l):
            P.op("pool", lambda e: e.dma_start(out=wab[:, 0, :, :], in_=w_a[l].rearrange("n i j -> i n j")),
                 writes=[("wab", 0)], dma="wab")
            P.op("pool", lambda e: e.dma_start(out=wab[:, 1, :, :], in_=w_x[l].rearrange("n i j -> i n j")),
                 writes=[("wab", 1)], dma="wab")
            P.op("sp", lambda e: e.dma_start(out=stg_in[0:96, :], in_=st_in[l]), writes=SINK, dma="stg1")

            def tr_st(pe):
                for c in range(NCH):
                    ins = pe.transpose(ps[:, 2 + c // 4, (c % 4) * 128:(c % 4) * 128 + 96],
                                       stg_in[0:96, c * 128:(c + 1) * 128], ident[0:96, 0:96])
                return ins
            P.op("pe", tr_st, reads=SINK + [("ident",)], writes=[("ps", 2), ("ps", 3)])
            for h in range(2):
                P.op("act", lambda e, h=h: e.activation(
                    out=stS[:, 4 * h:4 * h + 4, :],
                    in_=ps[:, 2 + h, :].rearrange("p (c n) -> p c n", n=128)[:, :, 0:96], func=AF.Identity),
                    reads=[("ps", 2 + h)], writes=[("stS", h)])
            for c in range(NCH):
                P.op("dve", lambda e, c=c: e.memset(c4h[:, c, :], 0.0), writes=[("c4h", c)])
                P.op("dve", lambda e, c=c: e.memset(c3h[:, c, :], 0.0), writes=[("c3h", c)])
                P.op("dve", lambda e, c=c: e.memset(hcar[:, c:c + 1], 0.0), writes=[("hcar", c)])

        def build_diag(l, c, par_, which):
            if which == "rnn":
                r0, n, base = 8, 5, 0
            else:
                r0, n, base = 16, 3, 5
            P.op("pool", lambda e: e.tensor_tensor(
                out=dg[:, par_, base:base + n, :],
                in0=identb[:].unsqueeze(1).broadcast_to([128, n, 128]),
                in1=parT[:, c, l * NPAR + r0:l * NPAR + r0 + n].unsqueeze(2).broadcast_to([128, n, 128]),
                op=ALU.mult),
                reads=[("identb",)] + PARK, writes=[("dg", par_, which)])


        itctr = {"rnn": 0, "conv": 0, "ph2": 0, "ph3": 0}

        nrm = stage[:].rearrange("p a d -> p (a d)").rearrange("p (s n) -> p s n", n=512)
        tff = tf[:].rearrange("p s n -> p (s n)")
        stg_out = tff[:, 10 * NMAX:10 * NMAX + D]
        stg_in = tff[:, 13 * NMAX:13 * NMAX + D]
        SOUTK = [("tf", 10), ("tf", 11), ("tf", 12)]
        SINK = [("tf", 13), ("tf", 14), ("tf", 15)]
        pending_norm = []

        def ph3_norm_chunk(u, si, j, ll):
            segs = TILES[u][2]
            Av, Bv = nrm[:, 2 * si, :], nrm[:, 2 * si + 1, :]
            SK = ("stg", si)
            for (kind, g0, n, l0) in segs:
                P.op("dve", lambda e, g0=g0, n=n, l0=l0: e.tensor_tensor(
                    out=x32[:, j, g0:g0 + n], in0=x32[:, j, g0:g0 + n], in1=Av[:, l0:l0 + n], op=ALU.mult),
                    reads=[("x32", u, j), SK], writes=[("x32", u, j)])
            for (kind, g0, n, l0) in segs:
                P.op("dve", lambda e, g0=g0, n=n, l0=l0: e.tensor_tensor(
                    out=x32[:, j, g0:g0 + n], in0=x32[:, j, g0:g0 + n], in1=Bv[:, l0:l0 + n], op=ALU.add),
                    reads=[("x32", u, j), SK], writes=[("x32", u, j)])
            for (kind, g0, n, l0) in segs:
                P.op("act", lambda e, g0=g0, n=n: e.activation(
                    out=x32[:, j, g0:g0 + n], in_=x32[:, j, g0:g0 + n], func=AF.Identity,
                    bias=par(ll, 20, j), scale=par(ll, 19, j)),
                    reads=[("x32", u, j)] + PARK, writes=[("x32", u, j)])

        def emit_deferred(j):
            for (u, si, ll) in pending_norm:
                ph3_norm_chunk(u, si, j, ll)
            if j == NCH - 1:
                del pending_norm[:]

        def flush_deferred():
            if pending_norm:
                for j in range(NCH):
                    emit_deferred(j)

        def run_layer(l, gi0):
            gi = gi0
            if small:
                load_layer_small(l)
            for pa in range(npass):
                tl_list = PASS_TILES[pa]
                rchain = {"f": None}
                for cp in range(4 if nphase >= 1 else 0):
                    ensure_loaded(gi)
                    ensure_loaded(gi + 1)
                    sxr, sgr = gslots[gi]
                    gi += 1
                    def rnn_body(u, pk, prev_chain, cp=cp, sxr=sxr, sgr=sgr):
                        N, np_, has_s = seg_views(u)
                        toff = TILES[u][1]
                        lt = u % 2
                        first = (u == 0)
                        last = (u == 5)
                        sbase = 3 + np_
                        dk = DERK(l)
                        ctx = []
                        for ci in range(2):
                            st_ = 4 * (2 * pk + ci)
                            ctx.append(dict(
                                c=2 * cp + ci, ci=ci, b_xr=ci, b_gr=2 + ci, b_xc=4 + 2 * pk + ci,
                                sg=tf[:, st_, :], ta=tf[:, st_ + 1, :], tx=tf[:, st_ + 2, :], av=tf[:, st_ + 3, :],
                                SG=("tf", st_), TA=("tf", st_ + 1), TX=("tf", st_ + 2), AV=("tf", st_ + 3),
                                hs=tf[:, 16 + ci, :], HS=("tf", 16 + ci),
                                xrb=tb[:, 2 * ci, :], xcb=tb[:, 2 * ci + 1, :], XRB=("tb", 2 * ci), XCB=("tb", 2 * ci + 1)))
                        b_ga, b_gx = 0, 1

                        def mm(pe, slot, bank, ci, N=N, toff=toff):
                            for k in range(NCH):
                                ins = pe.matmul(ps[:, bank, 0:N], wring[:, slot, k, ci * 128:(ci + 1) * 128],
                                                xb[:, k, toff:toff + N], start=(k == 0), stop=(k == NCH - 1))
                            return ins
                        for d in ctx:
                            P.op("pe", lambda pe, mm=mm, s=sxr, b=d["b_xr"], ci=d["ci"]: mm(pe, s, b, ci),
                                 reads=[("ws", sxr), ("xb", lt)], writes=[("ps", d["b_xr"])])
                            P.op("pe", lambda pe, mm=mm, s=sgr, b=d["b_gr"], ci=d["ci"]: mm(pe, s, b, ci),
                                 reads=[("ws", sgr), ("xb", lt)], writes=[("ps", d["b_gr"])])
                        if u == tl_list[0]:
                            for d in ctx:
                                build_diag(l, d["c"], d["ci"], "rnn")
                        for d in ctx:
                            c, xrb, XRB, b = d["c"], d["xrb"], d["XRB"], d["b_xr"]
                            P.op("pool", lambda e, xrb=xrb, c=c: e.tensor_copy(out=xrb[:, 0:3], in_=c4h[:, c, :]),
                                 reads=[("c4h", c)], writes=[XRB])
                            if has_s:
                                P.op("pool", lambda e, xrb=xrb, c=c: e.tensor_copy(
                                    out=xrb[:, sbase:sbase + 176].rearrange("p (s k) -> p s k", k=11)[:, :, 0:3],
                                    in_=stS[:, c, 0:48].rearrange("p (s k) -> p s k", k=3)),
                                    reads=[("stS", c // 4)], writes=[XRB])
                            P.op("dve", lambda e, xrb=xrb, c=c, b=b: e.tensor_scalar(
                                out=xrb[:, 3:3 + np_], in0=ps[:, b, 0:np_], scalar1=par(l, 0, c), scalar2=None,
                                op0=ALU.add),
                                reads=[("ps", b)] + PARK, writes=[XRB])
                            P.op("dve", lambda e, c=c, b=b: e.tensor_scalar(
                                out=c4h[:, c, :], in0=ps[:, b, np_ - 3:np_], scalar1=par(l, 0, c), scalar2=None,
                                op0=ALU.add),
                                reads=[("ps", b)] + PARK, writes=[("c4h", c)])
                            if has_s:
                                P.op("dve", lambda e, xrb=xrb, c=c, b=b: e.tensor_scalar(
                                    out=xrb[:, sbase:sbase + 176].rearrange("p (s k) -> p s k", k=11)[:, :, 3:11],
                                    in0=ps[:, b, np_:np_ + 128].rearrange("p (s k) -> p s k", k=8),
                                    scalar1=par(l, 0, c), scalar2=None, op0=ALU.add),
                                    reads=[("ps", b)] + PARK, writes=[XRB])
                                P.op("dve", lambda e, c=c, b=b: e.tensor_scalar(
                                    out=sout[:, c, 22:70].rearrange("p (s k) -> p s k", k=3),
                                    in0=ps[:, b, np_:np_ + 128].rearrange("p (s k) -> p s k", k=8)[:, :, 5:8],
                                    scalar1=par(l, 0, c), scalar2=None, op0=ALU.add),
                                    reads=[("ps", b)] + PARK, writes=[("sout", c)])
                            if last:
                                P.op("dve", lambda e, c=c: e.tensor_copy(out=sout[:, c, 1:4], in_=c4h[:, c, :]),
                                     reads=[("c4h", c)], writes=[("sout", c)])
                        for d in ctx:
                            c = d["c"]
                            P.op("act", lambda e, d=d, c=c: e.activation(
                                out=d["sg"][:, 0:N], in_=ps[:, d["b_gr"], 0:N], func=AF.Silu, bias=par(l, 1, c)),
                                reads=[("ps", d["b_gr"])] + PARK, writes=[d["SG"]])
                        for d in ctx:
                            def conv4(pe, d=d):
                                xrb, b_xc, ci = d["xrb"], d["b_xc"], d["ci"]
                                for k in range(4):
                                    pe.matmul(ps[:, b_xc, 0:np_], dg[:, ci, k, :], xrb[:, k:k + np_],
                                              start=(k == 0), stop=False)
                                ins = pe.matmul(ps[:, b_xc, 0:np_], dg[:, ci, 4, :], ones[:, 0:np_],
                                                start=False, stop=True)
                                if has_s:
                                    xs = xrb[:, sbase:sbase + 176].rearrange("p (s k) -> p s k", k=11)
                                    o = ps[:, b_xc, np_:np_ + 128].rearrange("p (s k) -> p s k", k=8)
                                    for k in range(4):
                                        pe.matmul(o, dg[:, ci, k, :], xs[:, :, k:k + 8], start=(k == 0), stop=False)
                                    ins = pe.matmul(o, dg[:, ci, 4, :],
                                                    ones[:, 0:128].rearrange("p (s k) -> p s k", k=8),
                                                    start=False, stop=True)
                                return ins
                            P.op("pe", conv4, reads=[d["XRB"], ("dg", d["ci"], "rnn"), ("ones",)],
                                 writes=[("ps", d["b_xc"])])
                        for d in ctx:
                            P.op("dve", lambda e, d=d: e.tensor_copy(out=d["xcb"][:, 0:N], in_=ps[:, d["b_xc"], 0:N]),
                                 reads=[("ps", d["b_xc"])], writes=[d["XCB"]])
                        for d in ctx:
                            c = d["c"]
                            P.op("pe", lambda pe, d=d, c=c: pe.matmul(ps[:, b_ga, 0:N], wab[:, 0, c, :],
                                                                      d["xcb"][:, 0:N], start=True, stop=True),
                                 reads=[d["XCB"], ("wab", 0)], writes=[("ps", b_ga)])
                            P.op("pe", lambda pe, d=d, c=c: pe.matmul(ps[:, b_gx, 0:N], wab[:, 1, c, :],
                                                                      d["xcb"][:, 0:N], start=True, stop=True),
                                 reads=[d["XCB"], ("wab", 1)], writes=[("ps", b_gx)])
                            P.op("act", lambda e, d=d, c=c: e.activation(out=d["ta"][:, 0:N], in_=ps[:, b_ga, 0:N],
                                                                         func=AF.Tanh, bias=der[:, 0, l, c:c + 1],
                                                                         scale=0.5),
                                 reads=[("ps", b_ga)] + dk, writes=[d["TA"]])
                            P.op("act", lambda e, d=d, c=c: e.activation(out=d["tx"][:, 0:N], in_=ps[:, b_gx, 0:N],
                                                                         func=AF.Tanh, bias=der[:, 1, l, c:c + 1],
                                                                         scale=0.5),
                                 reads=[("ps", b_gx)] + dk, writes=[d["TX"]])
                        if prev_chain is not None:
                            prev_chain()
                        for d in ctx:
                            c = d["c"]
                            P.op("act", lambda e, d=d, c=c: e.activation(
                                out=d["av"][:, 0:N], in_=d["ta"][:, 0:N], func=AF.Exp,
                                bias=der[:, 3, l, c:c + 1], scale=der[:, 3, l, c:c + 1]),
                                reads=[d["TA"]] + dk, writes=[d["AV"]])
                        for d in ctx:
                            P.op("act", lambda e, d=d: e.activation(out=d["ta"][:, 0:N], in_=d["av"][:, 0:N],
                                                                    func=AF.Square),
                                 reads=[d["AV"]], writes=[d["TA"]])
                        for d in ctx:
                            P.op("act", lambda e, d=d: e.activation(out=d["ta"][:, 0:N], in_=d["ta"][:, 0:N],
                                                                    func=AF.Ln, bias=0.25, scale=-0.25),
                                 reads=[d["TA"]], writes=[d["TA"]])
                        for d in ctx:
                            P.op("act", lambda e, d=d: e.activation(out=d["ta"][:, 0:N], in_=d["ta"][:, 0:N],
                                                                    func=AF.Exp, scale=0.5),
                                 reads=[d["TA"]], writes=[d["TA"]])
                        def chain():
                            for d in ctx:
                                c, ta, tx, av, hs, sg = d["c"], d["ta"], d["tx"], d["av"], d["hs"], d["sg"]
                                TA, TX, AV, HS, SG, b_xc = d["TA"], d["TX"], d["AV"], d["HS"], d["SG"], d["b_xc"]
                                if first:
                                    P.op("dve", lambda e, ta=ta: e.memset(ta[:, 0:1], 0.5), writes=[TA])
                                P.op("dve", lambda e, tx=tx, b_xc=b_xc: e.scalar_tensor_tensor(
                                    out=tx[:, 0:N], in0=tx[:, 0:N], scalar=1.0, in1=ps[:, b_xc, 0:N],
                                    op0=ALU.add, op1=ALU.mult),
                                    reads=[TX, ("ps", b_xc)], writes=[TX])
                                P.op("dve", lambda e, tx=tx, ta=ta: e.tensor_tensor(out=tx[:, 0:N], in0=tx[:, 0:N],
                                                                                    in1=ta[:, 0:N], op=ALU.mult),
                                     reads=[TX, TA], writes=[TX])
                                P.op("dve", lambda e, hs=hs, av=av, tx=tx, c=c: e.tensor_tensor_scan(
                                    out=hs[:, 0:np_], data0=av[:, 0:np_], data1=tx[:, 0:np_],
                                    initial=hcar[:, c:c + 1], op0=ALU.mult, op1=ALU.add),
                                    reads=[AV, TX, ("hcar", c)], writes=[HS])
                                P.op("dve", lambda e, hs=hs, c=c: e.tensor_copy(out=hcar[:, c:c + 1],
                                                                                in_=hs[:, np_ - 1:np_]),
                                     reads=[HS], writes=[("hcar", c)])
                                if last:
                                    P.op("dve", lambda e, hs=hs, c=c: e.tensor_copy(out=sout[:, c, 0:1],
                                                                                    in_=hs[:, np_ - 1:np_]),
                                         reads=[HS], writes=[("sout", c)])
                                if has_s:
                                    a_s = av[:, np_:np_ + 128].rearrange("p (s k) -> p s k", k=8)
                                    b_s = tx[:, np_:np_ + 128].rearrange("p (s k) -> p s k", k=8)
                                    h_s = hs[:, np_:np_ + 128].rearrange("p (s k) -> p s k", k=8)
                                    h0 = stS[:, c, 80:96]
                                    P.op("dve", lambda e, h_s=h_s, a_s=a_s, h0=h0: e.tensor_tensor(
                                        out=h_s[:, :, 0], in0=a_s[:, :, 0], in1=h0, op=ALU.mult),
                                        reads=[AV, ("stS", c // 4)], writes=[HS])
                                    P.op("dve", lambda e, h_s=h_s, b_s=b_s: e.tensor_tensor(
                                        out=b_s[:, :, 0], in0=b_s[:, :, 0], in1=h_s[:, :, 0], op=ALU.add),
                                        reads=[HS, TX], writes=[TX])
                                    P.op("dve", lambda e, a_s=a_s: e.memset(a_s[:, :, 0], 0.0), writes=[AV])
                                    P.op("dve", lambda e, hs=hs, av=av, tx=tx: e.tensor_tensor_scan(
                                        out=hs[:, np_:np_ + 128], data0=av[:, np_:np_ + 128],
                                        data1=tx[:, np_:np_ + 128], initial=0.0, op0=ALU.mult, op1=ALU.add),
                                        reads=[AV, TX], writes=[HS])
                                    P.op("dve", lambda e, h_s=h_s, c=c: e.tensor_copy(out=sout[:, c, 6:22],
                                                                                      in_=h_s[:, :, 7]),
                                         reads=[HS], writes=[("sout", c)])
                                P.op("dve", lambda e, hs=hs, sg=sg, c=c: e.tensor_tensor(
                                    out=pb[:, c, toff:toff + N], in0=hs[:, 0:N], in1=sg[:, 0:N], op=ALU.mult),
                                    reads=[HS, SG], writes=[("p", lt, c)])
                        return chain

                    for u in tl_list:
                        rchain["f"] = rnn_body(u, itctr["rnn"] % 2, rchain["f"])
                        itctr["rnn"] += 1

                if rchain["f"] is not None:
                    rchain["f"]()
                    rchain["f"] = None
                for cp in range(4 if nphase >= 2 else 0):
                    ensure_loaded(gi)
                    ensure_loaded(gi + 1)
                    scc, sch, sgc_, scb = gslots[gi]
                    gi += 1
                    its = [(u, c) for u in tl_list for c in (2 * cp, 2 * cp + 1)]
                    pend = None
                    for (u, c) in its:
                        it = itctr["conv"]
                        itctr["conv"] += 1
                        par_ = it % 2
                        civ = cp * 4 + its.index((u, c))
                        N, np_, has_s = seg_views(u)
                        toff = TILES[u][1]
                        lt = u % 2
                        cc_ = c % 2
                        b_cc, b_ch, b_gc, b_cb, b_v = 0, 1, 2 + par_, 4 + par_, 6 + par_
                        last = (u == 5)
                        ccs, sgc = tf[:, 6 * par_ + 0, :], tf[:, 6 * par_ + 1, :]
                        CCS, SGC = ("tf", 6 * par_ + 0), ("tf", 6 * par_ + 1)
                        ub = tb[:, 2 * par_, :]
                        UB = ("tb", 2 * par_)
                        sbase = 2 + np_

                        def mm(pe, slot, bank, N=N, toff=toff, cc_=cc_):
                            for k in range(NCH):
                                ins = pe.matmul(ps[:, bank, 0:N], wring[:, slot, k, cc_ * 128:(cc_ + 1) * 128],
                                                xb[:, k, toff:toff + N], start=(k == 0), stop=(k == NCH - 1))
                            return ins
                        for (s_, b_) in ((scc, b_cc), (sch, b_ch), (sgc_, b_gc), (scb, b_cb)):
                            P.op("pe", lambda pe, mm=mm, s=s_, b=b_: mm(pe, s, b),
                                 reads=[("ws", s_), ("xb", lt)], writes=[("ps", b_)])
                        if pend is not None:
                            pend()
                        if u == tl_list[0]:
                            build_diag(l, c, par_, "conv")
                        P.op("act", lambda e, ccs=ccs, c=c, N=N: e.activation(
                            out=ccs[:, 0:N], in_=ps[:, b_cc, 0:N], func=AF.Identity, bias=par(l, 3, c)),
                            reads=[("ps", b_cc)] + PARK, writes=[CCS])
                        P.op("act", lambda e, ub=ub, c=c: e.activation(out=ub[:, 0:2], in_=c3h[:, c, :],
                                                                       func=AF.Identity),
                             reads=[("c3h", c)], writes=[UB])
                        if has_s:
                            P.op("act", lambda e, ub=ub, c=c, sbase=sbase: e.activation(
                                out=ub[:, sbase:sbase + 160].rearrange("p (s k) -> p s k", k=10)[:, :, 0:2],
                                in_=stS[:, c, 48:80].rearrange("p (s k) -> p s k", k=2), func=AF.Identity),
                                reads=[("stS", c // 4)], writes=[UB])
                        P.op("dve", lambda e, ub=ub, ccs=ccs, c=c, np_=np_: e.scalar_tensor_tensor(
                            out=ub[:, 2:2 + np_], in0=ps[:, b_ch, 0:np_], scalar=par(l, 4, c), in1=ccs[:, 0:np_],
                            op0=ALU.add, op1=ALU.mult),
                            reads=[("ps", b_ch), CCS] + PARK, writes=[UB])
                        P.op("dve", lambda e, ccs=ccs, c=c, np_=np_: e.scalar_tensor_tensor(
                            out=c3h[:, c, :], in0=ps[:, b_ch, np_ - 2:np_], scalar=par(l, 4, c),
                            in1=ccs[:, np_ - 2:np_], op0=ALU.add, op1=ALU.mult),
                            reads=[("ps", b_ch), CCS] + PARK, writes=[("c3h", c)])
                        if has_s:
                            P.op("dve", lambda e, ub=ub, ccs=ccs, c=c, np_=np_, sbase=sbase: e.scalar_tensor_tensor(
                                out=ub[:, sbase:sbase + 160].rearrange("p (s k) -> p s k", k=10)[:, :, 2:10],
                                in0=ps[:, b_ch, np_:np_ + 128].rearrange("p (s k) -> p s k", k=8),
                                scalar=par(l, 4, c),
                                in1=ccs[:, np_:np_ + 128].rearrange("p (s k) -> p s k", k=8),
                                op0=ALU.add, op1=ALU.mult),
                                reads=[("ps", b_ch), CCS] + PARK, writes=[UB])
                            P.op("dve", lambda e, ccs=ccs, c=c, np_=np_: e.scalar_tensor_tensor(
                                out=sout[:, c, 70:102].rearrange("p (s k) -> p s k", k=2),
                                in0=ps[:, b_ch, np_:np_ + 128].rearrange("p (s k) -> p s k", k=8)[:, :, 6:8],
                                scalar=par(l, 4, c),
                                in1=ccs[:, np_:np_ + 128].rearrange("p (s k) -> p s k", k=8)[:, :, 6:8],
                                op0=ALU.add, op1=ALU.mult),
                                reads=[("ps", b_ch), CCS] + PARK, writes=[("sout", c)])
                        if last:
                            P.op("dve", lambda e, c=c: e.tensor_copy(out=sout[:, c, 4:6], in_=c3h[:, c, :]),
                                 reads=[("c3h", c)], writes=[("sout", c)])
                        P.op("act", lambda e, sgc=sgc, c=c, N=N, b=b_gc: e.activation(
                            out=sgc[:, 0:N], in_=ps[:, b, 0:N], func=AF.Silu, bias=par(l, 5, c)),
                            reads=[("ps", b_gc)] + PARK, writes=[SGC])
                        P.op("dve", lambda e, sgc=sgc, c=c, N=N, b=b_cb: e.scalar_tensor_tensor(
                            out=sgc[:, 0:N], in0=ps[:, b, 0:N], scalar=par(l, 2, c), in1=sgc[:, 0:N],
                            op0=ALU.add, op1=ALU.mult),
                            reads=[("ps", b_cb), SGC] + PARK, writes=[SGC])
                        if pending_norm:
                            (u_, si_, ll_) = pending_norm[civ % 2]
                            ph3_norm_chunk(u_, si_, civ // 2, ll_)
                            if civ == 2 * NCH - 1:
                                del pending_norm[:]

                        def pe2(u=u, c=c, par_=par_, N=N, np_=np_, has_s=has_s, ub=ub, UB=UB, b_v=b_v, sgc=sgc,
                                SGC=SGC, toff=toff, lt=lt, sbase=sbase):
                            def conv3(pe):
                                for k in range(3):
                                    ins = pe.matmul(ps[:, b_v, 0:np_], dg[:, par_, 5 + k, :], ub[:, k:k + np_],
                                                    start=(k == 0), stop=(k == 2))
                                if has_s:
                                    us = ub[:, sbase:sbase + 160].rearrange("p (s k) -> p s k", k=10)
                                    o = ps[:, b_v, np_:np_ + 128].rearrange("p (s k) -> p s k", k=8)
                                    for k in range(3):
                                        ins = pe.matmul(o, dg[:, par_, 5 + k, :], us[:, :, k:k + 8],
                                                        start=(k == 0), stop=(k == 2))
                                return ins
                            P.op("pe", conv3, reads=[UB, ("dg", par_, "conv")], writes=[("ps", b_v)])
                            P.op("dve", lambda e: e.tensor_tensor(out=qb[:, c, toff:toff + N], in0=sgc[:, 0:N],
                                                                  in1=ps[:, b_v, 0:N], op=ALU.mult),
                                 reads=[SGC, ("ps", b_v)], writes=[("q", lt, c)])
                        pend = pe2
                    if pend is not None:
                        pend()
                        pend = None

                for jp in range(4 if nphase >= 3 else 0):
                    ensure_loaded(gi)
                    ensure_loaded(gi + 1)
                    sro, sco, sg1, sg2 = gslots[gi]
                    gi += 1
                    for u in tl_list:
                        for j in (2 * jp, 2 * jp + 1):
                            it = itctr["ph2"]
                            itctr["ph2"] += 1
                            par_ = it % 2
                            N, np_, has_s = seg_views(u)
                            toff = TILES[u][1]
                            lt = u % 2
                            jj = j % 2
                            b_yr, b_yc, b_g1, b_g2 = par_, 2 + par_, 4 + par_, 6 + par_
                            s1, s2 = tf[:, 6 * par_ + 0, :], tf[:, 6 * par_ + 1, :]
                            S1K, S2K = ("tf", 6 * par_ + 0), ("tf", 6 * par_ + 1)

                            def mm(pe, slot, bank, src, N=N, toff=toff, jj=jj):
                                for k in range(NCH):
                                    ins = pe.matmul(ps[:, bank, 0:N], wring[:, slot, k, jj * 128:(jj + 1) * 128],
                                                    src[:, k, toff:toff + N], start=(k == 0), stop=(k == NCH - 1))
                                return ins
                            P.op("pe", lambda pe, mm=mm, s=sg1, b=b_g1: mm(pe, s, b, xb),
                                 reads=[("ws", sg1), ("xb", lt)], writes=[("ps", b_g1)])
                            P.op("pe", lambda pe, mm=mm, s=sg2, b=b_g2: mm(pe, s, b, xb),
                                 reads=[("ws", sg2), ("xb", lt)], writes=[("ps", b_g2)])
                            P.op("pe", lambda pe, mm=mm, s=sro, b=b_yr: mm(pe, s, b, pb),
                                 reads=[("ws", sro)] + [("p", lt, c) for c in range(NCH)], writes=[("ps", b_yr)])
                            P.op("pe", lambda pe, mm=mm, s=sco, b=b_yc: mm(pe, s, b, qb),
                                 reads=[("ws", sco)] + [("q", lt, c) for c in range(NCH)], writes=[("ps", b_yc)])
                            P.op("act", lambda e, s1=s1, j=j, N=N, b=b_g1: e.activation(
                                out=s1[:, 0:N], in_=ps[:, b, 0:N], func=AF.Sigmoid, bias=par(l, 6, j)),
                                reads=[("ps", b_g1)] + PARK, writes=[S1K])
                            P.op("act", lambda e, s2=s2, j=j, N=N, b=b_g2: e.activation(
                                out=s2[:, 0:N], in_=ps[:, b, 0:N], func=AF.Sigmoid, bias=par(l, 7, j)),
                                reads=[("ps", b_g2)] + PARK, writes=[S2K])
                            P.op("dve", lambda e, s1=s1, N=N, b=b_yr: e.tensor_tensor(
                                out=s1[:, 0:N], in0=s1[:, 0:N], in1=ps[:, b, 0:N], op=ALU.mult),
                                reads=[S1K, ("ps", b_yr)], writes=[S1K])
                            P.op("dve", lambda e, s2=s2, N=N, b=b_yc: e.tensor_tensor(
                                out=s2[:, 0:N], in0=s2[:, 0:N], in1=ps[:, b, 0:N], op=ALU.mult),
                                reads=[S2K, ("ps", b_yc)], writes=[S2K])
                            P.op("dve", lambda e, s1=s1, s2=s2, N=N, j=j, toff=toff: e.tensor_tensor(
                                out=mb[:, j, toff:toff + N], in0=s1[:, 0:N], in1=s2[:, 0:N], op=ALU.add),
                                reads=[S1K, S2K], writes=[("m", lt, j)])

                ensure_loaded(gi)
                ensure_loaded(gi + 1)
                swo = gslots[gi]
                gi += 1
                nxt = (l, pa + 1) if pa < npass - 1 else ((l + 1, 0) if l + 1 < depth else None)
                def ph3_loop1(u, si, swo=swo, cast_pa=None, hook=None):
                    b_S1, b_S2 = 2 + 2 * si, 3 + 2 * si
                    N, np_, has_s = seg_views(u)
                    toff = TILES[u][1]
                    lt = u % 2
                    segs = TILES[u][2]
                    pend_s = None
                    for j in range(NCH):
                        it = itctr["ph3"]
                        itctr["ph3"] += 1
                        par_ = it % 2
                        b_o = par_
                        vb, vsq = tb[:, 2 * par_, :], tb[:, 2 * par_ + 1, :]
                        VB, VSQ = ("tb", 2 * par_), ("tb", 2 * par_ + 1)

                        def mmo(pe, j=j, b_o=b_o):
                            slot = swo[j // 2]
                            jj = j % 2
                            for k in range(NCH):
                                ins = pe.matmul(ps[:, b_o, 0:N], wring[:, slot, k, jj * 128:(jj + 1) * 128],
                                                mb[:, k, toff:toff + N], start=(k == 0), stop=(k == NCH - 1))
                            return ins
                        P.op("pe", mmo, reads=[("ws", swo[j // 2])] + [("m", lt, k) for k in range(NCH)],
                             writes=[("ps", b_o)])
                        if pend_s is not None:
                            pend_s()
                        for (kind, g0, n, l0) in segs:
                            P.op("dve", lambda e, j=j, g0=g0, n=n, l0=l0, b_o=b_o: e.scalar_tensor_tensor(
                                out=x32[:, j, g0:g0 + n], in0=x32[:, j, g0:g0 + n], scalar=ALPHA,
                                in1=ps[:, b_o, l0:l0 + n], op0=ALU.mult, op1=ALU.add),
                                reads=[("x32", u, j), ("ps", b_o)], writes=[("x32", u, j)])
                        for (kind, g0, n, l0) in segs:
                            P.op("act", lambda e, j=j, g0=g0, n=n, l0=l0, vb=vb: e.activation(
                                out=vb[:, l0:l0 + n], in_=x32[:, j, g0:g0 + n], func=AF.Copy),
                                reads=[("x32", u, j)], writes=[VB])
                            P.op("act", lambda e, j=j, g0=g0, n=n, l0=l0, vsq=vsq: e.activation(
                                out=vsq[:, l0:l0 + n], in_=x32[:, j, g0:g0 + n], func=AF.Square),
                                reads=[("x32", u, j)], writes=[VSQ])

                        if cast_pa is not None:
                            cast_xb(cast_pa, "act", j)

                        def smm(j=j, vb=vb, vsq=vsq, VB=VB, VSQ=VSQ):
                            P.op("pe", lambda pe: pe.matmul(
                                ps[:, b_S1, 0:N], ones[:, 0:128], vb[:, 0:N], start=(j == 0), stop=(j == NCH - 1)),
                                reads=[VB, ("ones",)], writes=[("ps", b_S1)])
                            P.op("pe", lambda pe: pe.matmul(
                                ps[:, b_S2, 0:N], ones[:, 0:128], vsq[:, 0:N], start=(j == 0), stop=(j == NCH - 1)),
                                reads=[VSQ, ("ones",)], writes=[("ps", b_S2)])
                        pend_s = smm
                        if hook is not None:
                            hook(j)
                    pend_s()

                def ph3_stats(u, si):
                    N, np_, has_s = seg_views(u)
                    b_S1, b_S2 = 2 + 2 * si, 3 + 2 * si
                    mean, msq = tf[:, 8, :], tf[:, 9, :]
                    MEAN, MSQ = ("tf", 8), ("tf", 9)
                    Av, Bv = nrm[:, 2 * si, :], nrm[:, 2 * si + 1, :]
                    SK = ("stg", si)
                    g0, g1, g2 = [], [], []
                    g0.append(lambda: P.op("dve", lambda e: e.tensor_scalar(out=mean[:, 0:N], in0=ps[:, b_S1, 0:N],
                                                          scalar1=1.0 / D, scalar2=None, op0=ALU.mult),
                         reads=[("ps", b_S1)], writes=[MEAN]))
                    g0.append(lambda: P.op("act", lambda e: e.activation(out=msq[:, 0:N], in_=ps[:, b_S1, 0:N], func=AF.Square,
                                                       scale=1.0 / D),
                         reads=[("ps", b_S1)], writes=[MSQ]))
                    g1.append(lambda: P.op("dve", lambda e: e.scalar_tensor_tensor(out=msq[:, 0:N], in0=ps[:, b_S2, 0:N],
                                                                 scalar=1.0 / D, in1=msq[:, 0:N],
                                                                 op0=ALU.mult, op1=ALU.subtract),
                         reads=[("ps", b_S2), MSQ], writes=[MSQ]))
                    g1.append(lambda: P.op("act", lambda e: e.activation(out=msq[:, 0:N], in_=msq[:, 0:N], func=AF.Ln,
                                                       bias=epst[:, 0:1], scale=1.0),
                         reads=[MSQ, ("eps",)], writes=[MSQ]))
                    g1.append(lambda: P.op("act", lambda e: e.activation(out=Av[:, 0:N], in_=msq[:, 0:N], func=AF.Exp, scale=-0.5),
                         reads=[MSQ], writes=[SK]))
                    g2.append(lambda: P.op("dve", lambda e: e.scalar_tensor_tensor(out=Bv[:, 0:N], in0=mean[:, 0:N], scalar=-1.0,
                                                                 in1=Av[:, 0:N], op0=ALU.mult, op1=ALU.mult),
                         reads=[MEAN, SK], writes=[SK]))
                    return [g0, g1, g2]

                u0, u1 = tl_list
                flush_deferred()
                ph3_loop1(u0, 0, cast_pa=(nxt[1] if nxt is not None else None))
                st0 = ph3_stats(u0, 0)

                def hook0(j, st0=st0):
                    if j in (1, 3, 5):
                        for f in st0[(j - 1) // 2]:
                            f()
                ph3_loop1(u1, 1, hook=hook0)
                for g in ph3_stats(u1, 1):
                    for f in g:
                        f()
                pending_norm.extend([(u0, 0, l), (u1, 1, l)])
                if pa == npass - 1 and l == depth - 1:
                    flush_deferred()

            return gi

        def cast_xb(pa, eng="pool", only_k=None):
            for u in PASS_TILES[pa]:
                toff = TILES[u][1]
                lt = u % 2
                for (kind, g0, n, l0) in TILES[u][2]:
                    ks = range(NCH) if only_k is None else [only_k]
                    if eng == "pool":
                        P.op("pool", lambda e, g0=g0, n=n, o=toff + l0: e.tensor_copy(
                            out=xb[:, :, o:o + n], in_=x32[:, :, g0:g0 + n]),
                            reads=x32keys(u), writes=[("xb", lt)])
                    else:
                        for k in ks:
                            P.op("act", lambda e, g0=g0, n=n, o=toff + l0, k=k: e.activation(
                                out=xb[:, k, o:o + n], in_=x32[:, k, g0:g0 + n], func=AF.Copy),
                                reads=[("x32", u, k)], writes=[("xb", lt)])

        def store_states(l):
            def tr(pe):
                for c in range(NCH):
                    ins = pe.transpose(ps[0:102, 4 + c // 4, (c % 4) * 128:(c % 4 + 1) * 128], sout[:, c, :], ident[:])
                return ins
            P.op("pe", tr, reads=[("sout", c) for c in range(NCH)] + [("ident",)], writes=[("ps", 4), ("ps", 5)])
            for h in range(2):
                P.op("act", lambda e, h=h: e.activation(out=stg_out[0:102, 512 * h:512 * (h + 1)],
                                                        in_=ps[0:102, 4 + h, :], func=AF.Identity),
                     reads=[("ps", 4 + h)], writes=SOUTK)
            P.op("sp", lambda e: e.dma_start(out=st_out[l], in_=stg_out[0:102, :]),
                 reads=SOUTK, writes=[("st_out", l)], dma="out0")

        ensure_loaded(0)
        ensure_loaded(1)
        cast_xb(0)
        gi = 0
        for l in range(depth):
            gi = run_layer(l, gi)
            store_states(l)

        for b in range(NBLK):
            s = b % 6
            sap, skeys = XS[s]
            pbk = 2 * (b % 4)
            rk = []
            for u in tiles_of_block(b):
                rk += x32keys(u)

            def tr_out(pe, b=b, pbk=pbk):
                for c in range(NCH):
                    ins = pe.transpose(ps[:, pbk + c // 4, (c % 4) * 128:(c % 4 + 1) * 128],
                                       x32[:, c, b * 128:(b + 1) * 128], ident[:])
                return ins
            P.op("pe", tr_out, reads=rk + [("ident",)], writes=[("ps", pbk), ("ps", pbk + 1)])
            for h in range(2):
                eng = "act" if h == 0 else "dve"

                def ev(e, sap=sap, h=h, pbk=pbk, eng=eng):
                    o = sap[:, 512 * h:512 * (h + 1)]
                    i = ps[:, pbk + h, :]
                    if eng == "act":
                        return e.activation(out=o, in_=i, func=AF.Identity)
                    return e.tensor_copy(out=o, in_=i)
                P.op(eng, ev, reads=[("ps", pbk + h)], writes=skeys)
            P.op("sp", lambda e, b=b, sap=sap: e.dma_start(out=y[b * 128:(b + 1) * 128, :], in_=sap),
                 reads=skeys, writes=[("y", b)], dma="out%d" % s)
        P.op("sp", lambda e: e.nop(), reads=[("y", b) for b in range(NBLK)] + [("st_out", l) for l in range(depth)])

        P.finalize()
        with nc.Block() as block:
            @block.tensor
            def _(e):
                P.emit_stream("pe", e, esem, dsem)

            @block.scalar
            def _(e):
                P.emit_stream("act", e, esem, dsem)

            @block.vector
            def _(e):
                P.emit_stream("dve", e, esem, dsem)

            @block.gpsimd
            def _(e):
                P.emit_stream("pool", e, esem, dsem)

            @block.sync
            def _(e):
                P.emit_stream("sp", e, esem, dsem)
    return nc


_NC_CACHE = {}


def _prep_inputs(inputs):
    f = lambda k: np.ascontiguousarray(np.asarray(inputs[k], dtype=np.float32))
    x_prompt, x_sample = f("x_prompt"), f("x_sample")
    s_h, s_c4, s_c3 = f("state_rglru"), f("state_conv4"), f("state_conv3")
    rows = []
    for l in range(DEPTH):
        rows.append(f("b_in")[l].reshape(8, D))
        rows.append(f("conv4_w")[l].reshape(4, D))
        rows.append(f("conv4_b")[l].reshape(1, D))
        rows.append(f("b_rg_a")[l].reshape(1, D))
        rows.append(f("b_rg_x")[l].reshape(1, D))
        rows.append(f("rg_lambda")[l].reshape(1, D))
        rows.append(f("conv3_w")[l].reshape(3, D))
        rows.append(f("ln_g")[l].reshape(1, D))
        rows.append(f("ln_b")[l].reshape(1, D))
    params = np.ascontiguousarray(np.concatenate(rows, axis=0))
    shared = {
        "params": params, "w_in": f("w_in"), "w_ro": f("w_rnn_out"), "w_co": f("w_conv_out"), "w_o": f("w_out"),
        "w_a": f("w_rg_a"), "w_x": f("w_rg_x"), "ident": np.eye(128, dtype=np.float32),
    }
    in_maps = []
    for c in range(8):
        sl = slice(16 * c, 16 * c + 16)
        xin = np.concatenate([x_prompt[c], x_sample[sl].reshape(128, D)], axis=0)
        st = np.concatenate([s_c4[:, sl].reshape(DEPTH, 48, D), s_c3[:, sl].reshape(DEPTH, 32, D),
                             s_h[:, sl].reshape(DEPTH, 16, D)], axis=1)
        m = dict(shared)
        m["xin"] = np.ascontiguousarray(xin)
        m["st_in"] = np.ascontiguousarray(st)
        in_maps.append(m)
    return in_maps


def kernel(**inputs):
    if "nc" not in _NC_CACHE:
        _NC_CACHE["nc"] = build_nc()
    nc = _NC_CACHE["nc"]
    in_maps = _prep_inputs(inputs)
    res = run_bass_kernel_spmd(nc, in_maps, core_ids=list(range(8)))
    ys = [np.asarray(r["y"]) for r in res.results]
    sts = [np.asarray(r["st_out"]) for r in res.results]
    y_prompt = np.stack([yy[0:NPR] for yy in ys], axis=0)
    y_sample = np.concatenate([yy[NPR:].reshape(16, 8, D) for yy in ys], axis=0)
    ph = np.stack([s[:, 0] for s in sts], axis=1)
    pc4 = np.stack([s[:, 1:4] for s in sts], axis=1)
    pc3 = np.stack([s[:, 4:6] for s in sts], axis=1)
    sh = np.concatenate([s[:, 6:22] for s in sts], axis=1)
    sc4 = np.concatenate([s[:, 22:70].reshape(DEPTH, 16, 3, D) for s in sts], axis=1)
    sc3 = np.concatenate([s[:, 70:102].reshape(DEPTH, 16, 2, D) for s in sts], axis=1)
    f32 = lambda a: np.ascontiguousarray(a, dtype=np.float32)
    return (f32(y_prompt), f32(y_sample), f32(ph), f32(pc4), f32(pc3), f32(sh), f32(sc4), f32(sc3))
```

```python
import contextlib
import numpy as np
import concourse.bass as bass
import concourse.mybir as mybir
from concourse.bass_utils import run_bass_kernel_spmd

F32 = mybir.dt.float32
BF16 = mybir.dt.bfloat16
AF = mybir.ActivationFunctionType
ALU = mybir.AluOpType

DEPTH = 4
D = 1024
NCH = 8
NPR = 2048
NSM = 128
NTOK = NPR + NSM
ALPHA = (2.0 * DEPTH) ** 0.25
EPS = 1e-5
NMAX = 384
NSLOT = 8
UW = 256
NPAR = 21

TILES = [
    (0, 0, [("P", 0, 384, 0)]),
    (0, 384, [("P", 384, 256, 0), ("S", 2048, 128, 256)]),
    (1, 0, [("P", 640, 352, 0)]),
    (1, 352, [("P", 992, 352, 0)]),
    (2, 0, [("P", 1344, 352, 0)]),
    (2, 352, [("P", 1696, 352, 0)]),
]
PASS_TILES = {0: [0, 1], 1: [2, 3], 2: [4, 5]}
PASSW = 768


def tile_n(u):
    return sum(s[2] for s in TILES[u][2])


class _Op:
    __slots__ = ("eng", "fn", "deps", "sig", "cnt", "dma", "dma_ord")


class Prog:
    ENGS = ("pe", "act", "dve", "pool", "sp")

    def __init__(self):
        self.streams = {e: [] for e in self.ENGS}
        self.last_w = {}
        self.readers = {}
        self.dma_last = {}
        self.dma_cnt = {}

    def op(self, eng, fn, reads=(), writes=(), dma=None):
        o = _Op()
        o.eng, o.fn, o.sig, o.cnt, o.dma, o.dma_ord = eng, fn, False, 0, dma, 0
        deps = []
        for r in reads:
            w = self.last_w.get(r)
            if w is not None:
                deps.append(w)
            if r[0] == "ps":
                rd = self.readers.get(r)
                if rd:
                    deps.extend(v for k, v in rd[0].items() if k != eng)
        for r in writes:
            w = self.last_w.get(r)
            if w is not None:
                deps.append(w)
            rd = self.readers.get(r)
            if rd:
                deps.extend(rd[0].values())
                deps.extend(rd[1])
        if dma is not None:
            prev = self.dma_last.get(dma)
            if prev is not None:
                deps.append(prev)
            self.dma_last[dma] = o
            self.dma_cnt[dma] = self.dma_cnt.get(dma, 0) + 1
            o.dma_ord = self.dma_cnt[dma]
        seen = set()
        o.deps = []
        for d in deps:
            if id(d) in seen or d is o:
                continue
            seen.add(id(d))
            if eng == "pe" and d.eng == "pe" and d.dma is None and dma is None:
                continue
            d.sig = True
            o.deps.append(d)
        for r in writes:
            self.last_w[r] = o
            self.readers[r] = ({}, [])
        for r in reads:
            rd = self.readers.setdefault(r, ({}, []))
            if dma is None:
                rd[0][eng] = o
            else:
                rd[1].append(o)
        self.streams[eng].append(o)
        return o

    def finalize(self):
        for e in self.ENGS:
            c = 0
            for o in self.streams[e]:
                if o.sig and o.dma is None:
                    c += 1
                o.cnt = c

    def emit_stream(self, e, eng, esem, dsem):
        waited = {}
        for o in self.streams[e]:
            for d in o.deps:
                if d.dma is not None:
                    key, sem, val = ("d", d.dma), dsem[d.dma], 16 * d.dma_ord
                else:
                    key, sem, val = ("e", d.eng), esem[d.eng], d.cnt
                if waited.get(key, 0) < val:
                    eng.wait_ge(sem, val)
                    waited[key] = val
            ins = o.fn(eng)
            if o.dma is not None:
                ins.then_inc(dsem[o.dma], 16)
            elif o.sig:
                ins.then_inc(esem[e], 1)


def build_nc(depth=DEPTH, npass=3, nphase=4, small=True, p3=9):
    nc = bass.Bass("TRN2", target_bir_lowering=False)
    xin = nc.dram_tensor("xin", [NTOK, D], F32, kind="ExternalInput").ap()
    st_in = nc.dram_tensor("st_in", [DEPTH, 96, D], F32, kind="ExternalInput").ap()
    params = nc.dram_tensor("params", [DEPTH * NPAR, D], F32, kind="ExternalInput").ap()
    w_in = nc.dram_tensor("w_in", [DEPTH, D, 8 * D], F32, kind="ExternalInput").ap()
    w_ro = nc.dram_tensor("w_ro", [DEPTH, D, D], F32, kind="ExternalInput").ap()
    w_co = nc.dram_tensor("w_co", [DEPTH, D, D], F32, kind="ExternalInput").ap()
    w_o = nc.dram_tensor("w_o", [DEPTH, D, D], F32, kind="ExternalInput").ap()
    w_a = nc.dram_tensor("w_a", [DEPTH, NCH, 128, 128], F32, kind="ExternalInput").ap()
    w_x = nc.dram_tensor("w_x", [DEPTH, NCH, 128, 128], F32, kind="ExternalInput").ap()
    ident_d = nc.dram_tensor("ident", [128, 128], F32, kind="ExternalInput").ap()
    y = nc.dram_tensor("y", [NTOK, D], F32, kind="ExternalOutput").ap()
    st_out = nc.dram_tensor("st_out", [DEPTH, 102, D], F32, kind="ExternalOutput").ap()

    P = Prog()
    es = contextlib.ExitStack()
    with es:
        def sb(name, shape, dt):
            return es.enter_context(nc.sbuf_tensor(name, shape, dt))

        x32 = sb("x32", [128, NCH, NTOK], F32)
        xb = sb("xb", [128, NCH, PASSW], BF16)
        pb = sb("pb", [128, NCH, PASSW], BF16)
        qb = sb("qb", [128, NCH, PASSW], BF16)
        mb = sb("mb", [128, NCH, PASSW], BF16)
        wring = sb("wring", [128, NSLOT, NCH, UW], BF16)
        wab = sb("wab", [128, 2, NCH, 128], BF16)
        dg = sb("dg", [128, 2, 8, 128], BF16)
        parT = sb("parT", [128, NCH, DEPTH * NPAR], F32)
        der = sb("der", [128, 4, DEPTH, NCH], F32)
        ident = sb("identf", [128, 128], F32)
        identb = sb("identb", [128, 128], BF16)
        ones = sb("ones", [128, NMAX], BF16)
        stS = sb("stS", [128, NCH, 96], F32)
        sout = sb("sout", [128, NCH, 102], F32)
        c4h = sb("c4h", [128, NCH, 3], F32)
        c3h = sb("c3h", [128, NCH, 2], F32)
        hcar = sb("hcar", [128, NCH], F32)
        epst = sb("epst", [128, 1], F32)
        stage = sb("stage", [128, 2, D], F32)
        NTF, NTB = 18, 4
        tf = sb("tf", [128, NTF, NMAX], F32)
        tb = sb("tb", [128, NTB, 448], BF16)
        ps = es.enter_context(nc.psum_tensor("ps", [128, 8, 512], F32))

        tff0 = tf[:].rearrange("p s n -> p (s n)")
        XS = [(stage[:, 0, :], [("stg", 0)]), (stage[:, 1, :], [("stg", 1)])]
        for i_ in range(4):
            XS.append((tff0[:, i_ * 3 * NMAX:i_ * 3 * NMAX + D], [("tf", 3 * i_ + k_) for k_ in range(3)]))
        esem = {e: es.enter_context(nc.semaphore("s_" + e)) for e in ("pe", "act", "dve", "pool")}
        dnames = (["w%d" % i for i in range(NSLOT)] + ["stg%d" % i for i in range(6)] + ["misc", "wab"]
                  + ["out%d" % i for i in range(6)])
        dsem = {n: es.enter_context(nc.semaphore("d_" + n)) for n in dnames}

        def par(l, r, c):
            return parT[:, c, l * NPAR + r: l * NPAR + r + 1]

        P.op("sp", lambda e: e.dma_start(out=ident[:], in_=ident_d), writes=[("ident",)], dma="misc")
        P.op("sp", lambda e: e.dma_start(out=stage[0:DEPTH * NPAR, 0, :], in_=params),
             writes=[("stg", 0)], dma="stg0")
        P.op("dve", lambda e: e.tensor_copy(out=identb[:], in_=ident[:]), reads=[("ident",)],
             writes=[("identb",)])
        P.op("dve", lambda e: e.memset(ones[:], 1.0), writes=[("ones",)])
        P.op("dve", lambda e: e.memset(epst[:], EPS), writes=[("eps",)])

        def tr_params(pe):
            for c in range(NCH):
                ins = pe.transpose(ps[:, c // 4, (c % 4) * 128:(c % 4) * 128 + DEPTH * NPAR],
                                   stage[0:DEPTH * NPAR, 0, c * 128:(c + 1) * 128],
                                   ident[0:DEPTH * NPAR, 0:DEPTH * NPAR])
            return ins
        P.op("pe", tr_params, reads=[("stg", 0), ("ident",)], writes=[("ps", 0), ("ps", 1)])
        for h in range(2):
            P.op("act", lambda e, h=h: e.activation(
                out=parT[:, 4 * h:4 * h + 4, :],
                in_=ps[:, h, :].rearrange("p (c n) -> p c n", n=128)[:, :, 0:DEPTH * NPAR],
                func=AF.Identity), reads=[("ps", h)], writes=[("parT", h)])
        PARK = [("parT", 0), ("parT", 1)]
        for l in range(DEPTH):
            def mk(l):
                def v(ri):
                    return parT[:, :, l * NPAR + ri]
                P.op("act", lambda e: e.activation(out=der[:, 0, l, :], in_=v(13), func=AF.Identity, scale=0.5),
                     reads=PARK, writes=[("der", l, 0)])
                P.op("act", lambda e: e.activation(out=der[:, 1, l, :], in_=v(14), func=AF.Identity, scale=0.5),
                     reads=PARK, writes=[("der", l, 1)])
                P.op("act", lambda e: e.activation(out=der[:, 2, l, :], in_=v(15), func=AF.Exp, scale=-1.0),
                     reads=PARK, writes=[("der", l, 2)])
                P.op("act", lambda e: e.activation(out=der[:, 3, l, :], in_=der[:, 2, l, :], func=AF.Ln,
                                                   bias=1.0, scale=1.0),
                     reads=[("der", l, 2)], writes=[("der", l, 3)])
                P.op("act", lambda e: e.activation(out=der[:, 2, l, :], in_=der[:, 3, l, :], func=AF.Identity,
                                                   scale=-8.0),
                     reads=[("der", l, 3)], writes=[("der", l, 2)])
                P.op("act", lambda e: e.activation(out=der[:, 3, l, :], in_=der[:, 2, l, :], func=AF.Identity,
                                                   scale=0.5),
                     reads=[("der", l, 2)], writes=[("der", l, 3)])
            mk(l)
        DERK = lambda l: [("der", l, i) for i in range(4)]

        def tiles_of_block(b):
            g0, g1 = b * 128, (b + 1) * 128
            res = []
            for u, (_, _, segs) in enumerate(TILES):
                for (_, sg0, n, _) in segs:
                    if sg0 < g1 and sg0 + n > g0:
                        res.append(u)
            return sorted(set(res))

        def x32keys(u):
            return [("x32", u, c) for c in range(NCH)]

        NBLK = NTOK // 128
        for b in range(NBLK):
            s = b % 6
            sap, skeys = XS[s]
            P.op("sp", lambda e, b=b, sap=sap: e.dma_start(out=sap, in_=xin[b * 128:(b + 1) * 128, :]),
                 writes=skeys, dma="stg%d" % s)
            pbk = 2 * (b % 4)

            def tr_in(pe, sap=sap, pbk=pbk):
                for c in range(NCH):
                    ins = pe.transpose(ps[:, pbk + c // 4, (c % 4) * 128:(c % 4 + 1) * 128],
                                       sap[:, c * 128:(c + 1) * 128], ident[:])
                return ins
            P.op("pe", tr_in, reads=skeys + [("ident",)], writes=[("ps", pbk), ("ps", pbk + 1)])
            wk = []
            for u in tiles_of_block(b):
                wk += x32keys(u)
            for h in range(2):
                eng = "act" if h == 0 else "dve"

                def ev(e, b=b, h=h, pbk=pbk, eng=eng):
                    o = x32[:, 4 * h:4 * h + 4, b * 128:(b + 1) * 128]
                    i = ps[:, pbk + h, :].rearrange("p (c n) -> p c n", n=128)
                    if eng == "act":
                        return e.activation(out=o, in_=i, func=AF.Identity)
                    return e.tensor_copy(out=o, in_=i)
                P.op(eng, ev, reads=[("ps", pbk + h)], writes=[k for k in wk if k[2] // 4 == h])

        wstate = {"n": 0}

        def w_unit_ap(l, kind, col):
            if kind == "in":
                src = w_in[l]
            elif kind == "ro":
                src = w_ro[l]
            elif kind == "co":
                src = w_co[l]
            else:
                src = w_o[l]
            return src.rearrange("(k p) n -> p k n", p=128)[:, :, col:col + UW]

        def load_group(units):
            slots = []
            for (l, kind, col) in units:
                s = wstate["n"] % NSLOT
                wstate["n"] += 1
                src = w_unit_ap(l, kind, col)
                P.op("pool", lambda e, s=s, src=src: e.dma_start(out=wring[:, s, :, :], in_=src),
                     writes=[("ws", s)], dma="w%d" % s)
                slots.append(s)
            return slots

        groups = []
        for l in range(depth):
            for pa in range(npass):
                for cp in range(4 if nphase >= 1 else 0):
                    groups.append(("rnn", l, pa, cp, [(l, "in", 0 * D + cp * UW), (l, "in", 1 * D + cp * UW)]))
                for cp in range(4 if nphase >= 2 else 0):
                    groups.append(("conv", l, pa, cp, [(l, "in", g * D + cp * UW) for g in (3, 4, 5, 2)]))
                for jp in range(4 if nphase >= 3 else 0):
                    groups.append(("ph2", l, pa, jp, [(l, "ro", jp * UW), (l, "co", jp * UW),
                                                      (l, "in", 6 * D + jp * UW), (l, "in", 7 * D + jp * UW)]))
                if nphase >= 4:
                    groups.append(("ph3", l, pa, 0, [(l, "o", jp * UW) for jp in range(4)]))
        gslots = {}

        def ensure_loaded(gi):
            if gi < len(groups) and gi not in gslots:
                gslots[gi] = load_group(groups[gi][4])

        def seg_views(u):
            segs = TILES[u][2]
            np_ = segs[0][2]
            has_s = len(segs) > 1
            return tile_n(u), np_, has_s

        def load_layer_small(l):
            P.op("pool", lambda e: e.dma_start(out=wab[:, 0, :, :], in_=w_a[l].rearrange("n i j -> i n j")),
                 writes=[("wab", 0)], dma="wab")
            P.op("pool", lambda e: e.dma_start(out=wab[:, 1, :, :], in_=w_x[l].rearrange("n i j -> i n j")),
                 writes=[("wab", 1)], dma="wab")
            P.op("sp", lambda e: e.dma_start(out=stg_in[0:96, :], in_=st_in[l]), writes=SINK, dma="stg1")

            def tr_st(pe):
                for c in range(NCH):
                    ins = pe.transpose(ps[:, 2 + c // 4, (c % 4) * 128:(c % 4) * 128 + 96],
                                       stg_in[0:96, c * 128:(c + 1) * 128], ident[0:96, 0:96])
                return ins
            P.op("pe", tr_st, reads=SINK + [("ident",)], writes=[("ps", 2), ("ps", 3)])
            for h in range(2):
                P.op("act", lambda e, h=h: e.activation(
                    out=stS[:, 4 * h:4 * h + 4, :],
                    in_=ps[:, 2 + h, :].rearrange("p (c n) -> p c n", n=128)[:, :, 0:96], func=AF.Identity),
                    reads=[("ps", 2 + h)], writes=[("stS", h)])
            for c in range(NCH):
                P.op("dve", lambda e, c=c: e.memset(c4h[:, c, :], 0.0), writes=[("c4h", c)])
                P.op("dve", lambda e, c=c: e.memset(c3h[:, c, :], 0.0), writes=[("c3h", c)])
                P.op("dve", lambda e, c=c: e.memset(hcar[:, c:c + 1], 0.0), writes=[("hcar", c)])

        def build_diag(l, c, par_, which):
            if which == "rnn":
                r0, n, base = 8, 5, 0
            else:
                r0, n, base = 16, 3, 5
            P.op("pool", lambda e: e.tensor_tensor(
                out=dg[:, par_, base:base + n, :],
                in0=identb[:].unsqueeze(1).broadcast_to([128, n, 128]),
                in1=parT[:, c, l * NPAR + r0:l * NPAR + r0 + n].unsqueeze(2).broadcast_to([128, n, 128]),
                op=ALU.mult),
                reads=[("identb",)] + PARK, writes=[("dg", par_, which)])


        itctr = {"rnn": 0, "conv": 0, "ph2": 0, "ph3": 0}

        nrm = stage[:].rearrange("p a d -> p (a d)").rearrange("p (s n) -> p s n", n=512)
        tff = tf[:].rearrange("p s n -> p (s n)")
        stg_out = tff[:, 10 * NMAX:10 * NMAX + D]
        stg_in = tff[:, 13 * NMAX:13 * NMAX + D]
        SOUTK = [("tf", 10), ("tf", 11), ("tf", 12)]
        SINK = [("tf", 13), ("tf", 14), ("tf", 15)]
        pending_norm = []

        def ph3_norm_chunk(u, si, j, ll):
            segs = TILES[u][2]
            Av, Bv = nrm[:, 2 * si, :], nrm[:, 2 * si + 1, :]
            SK = ("stg", si)
            for (kind, g0, n, l0) in segs:
                P.op("dve", lambda e, g0=g0, n=n, l0=l0: e.tensor_tensor(
                    out=x32[:, j, g0:g0 + n], in0=x32[:, j, g0:g0 + n], in1=Av[:, l0:l0 + n], op=ALU.mult),
                    reads=[("x32", u, j), SK], writes=[("x32", u, j)])
            for (kind, g0, n, l0) in segs:
                P.op("dve", lambda e, g0=g0, n=n, l0=l0: e.tensor_tensor(
                    out=x32[:, j, g0:g0 + n], in0=x32[:, j, g0:g0 + n], in1=Bv[:, l0:l0 + n], op=ALU.add),
                    reads=[("x32", u, j), SK], writes=[("x32", u, j)])
            for (kind, g0, n, l0) in segs:
                P.op("act", lambda e, g0=g0, n=n: e.activation(
                    out=x32[:, j, g0:g0 + n], in_=x32[:, j, g0:g0 + n], func=AF.Identity,
                    bias=par(ll, 20, j), scale=par(ll, 19, j)),
                    reads=[("x32", u, j)] + PARK, writes=[("x32", u, j)])

        def emit_deferred(j):
            for (u, si, ll) in pending_norm:
                ph3_norm_chunk(u, si, j, ll)
            if j == NCH - 1:
                del pending_norm[:]

        def flush_deferred():
            if pending_norm:
                for j in range(NCH):
                    emit_deferred(j)

        def run_layer(l, gi0):
            gi = gi0
            if small:
                load_layer_small(l)
            for pa in range(npass):
                tl_list = PASS_TILES[pa]
                rchain = {"f": None}
                for cp in range(4 if nphase >= 1 else 0):
                    ensure_loaded(gi)
                    ensure_loaded(gi + 1)
                    sxr, sgr = gslots[gi]
                    gi += 1
                    def rnn_body(u, pk, prev_chain, cp=cp, sxr=sxr, sgr=sgr):
                        N, np_, has_s = seg_views(u)
                        toff = TILES[u][1]
                        lt = u % 2
                        first = (u == 0)
                        last = (u == 5)
                        sbase = 3 + np_
                        dk = DERK(l)
                        ctx = []
                        for ci in range(2):
                            st_ = 4 * (2 * pk + ci)
                            ctx.append(dict(
                                c=2 * cp + ci, ci=ci, b_xr=ci, b_gr=2 + ci, b_xc=4 + 2 * pk + ci,
                                sg=tf[:, st_, :], ta=tf[:, st_ + 1, :], tx=tf[:, st_ + 2, :], av=tf[:, st_ + 3, :],
                                SG=("tf", st_), TA=("tf", st_ + 1), TX=("tf", st_ + 2), AV=("tf", st_ + 3),
                                hs=tf[:, 16 + ci, :], HS=("tf", 16 + ci),
                                xrb=tb[:, 2 * ci, :], xcb=tb[:, 2 * ci + 1, :], XRB=("tb", 2 * ci), XCB=("tb", 2 * ci + 1)))
                        b_ga, b_gx = 0, 1

                        def mm(pe, slot, bank, ci, N=N, toff=toff):
                            for k in range(NCH):
                                ins = pe.matmul(ps[:, bank, 0:N], wring[:, slot, k, ci * 128:(ci + 1) * 128],
                                                xb[:, k, toff:toff + N], start=(k == 0), stop=(k == NCH - 1))
                            return ins
                        for d in ctx:
                            P.op("pe", lambda pe, mm=mm, s=sxr, b=d["b_xr"], ci=d["ci"]: mm(pe, s, b, ci),
                                 reads=[("ws", sxr), ("xb", lt)], writes=[("ps", d["b_xr"])])
                            P.op("pe", lambda pe, mm=mm, s=sgr, b=d["b_gr"], ci=d["ci"]: mm(pe, s, b, ci),
                                 reads=[("ws", sgr), ("xb", lt)], writes=[("ps", d["b_gr"])])
                        if u == tl_list[0]:
                            for d in ctx:
                                build_diag(l, d["c"], d["ci"], "rnn")
                        for d in ctx:
                            c, xrb, XRB, b = d["c"], d["xrb"], d["XRB"], d["b_xr"]
                            P.op("pool", lambda e, xrb=xrb, c=c: e.tensor_copy(out=xrb[:, 0:3], in_=c4h[:, c, :]),
                                 reads=[("c4h", c)], writes=[XRB])
                            if has_s:
                                P.op("pool", lambda e, xrb=xrb, c=c: e.tensor_copy(
                                    out=xrb[:, sbase:sbase + 176].rearrange("p (s k) -> p s k", k=11)[:, :, 0:3],
                                    in_=stS[:, c, 0:48].rearrange("p (s k) -> p s k", k=3)),
                                    reads=[("stS", c // 4)], writes=[XRB])
                            P.op("dve", lambda e, xrb=xrb, c=c, b=b: e.tensor_scalar(
                                out=xrb[:, 3:3 + np_], in0=ps[:, b, 0:np_], scalar1=par(l, 0, c), scalar2=None,
                                op0=ALU.add),
                                reads=[("ps", b)] + PARK, writes=[XRB])
                            P.op("dve", lambda e, c=c, b=b: e.tensor_scalar(
                                out=c4h[:, c, :], in0=ps[:, b, np_ - 3:np_], scalar1=par(l, 0, c), scalar2=None,
                                op0=ALU.add),
                                reads=[("ps", b)] + PARK, writes=[("c4h", c)])
                            if has_s:
                                P.op("dve", lambda e, xrb=xrb, c=c, b=b: e.tensor_scalar(
                                    out=xrb[:, sbase:sbase + 176].rearrange("p (s k) -> p s k", k=11)[:, :, 3:11],
                                    in0=ps[:, b, np_:np_ + 128].rearrange("p (s k) -> p s k", k=8),
                                    scalar1=par(l, 0, c), scalar2=None, op0=ALU.add),
                                    reads=[("ps", b)] + PARK, writes=[XRB])
                                P.op("dve", lambda e, c=c, b=b: e.tensor_scalar(
                                    out=sout[:, c, 22:70].rearrange("p (s k) -> p s k", k=3),
                                    in0=ps[:, b, np_:np_ + 128].rearrange("p (s k) -> p s k", k=8)[:, :, 5:8],
                                    scalar1=par(l, 0, c), scalar2=None, op0=ALU.add),
                                    reads=[("ps", b)] + PARK, writes=[("sout", c)])
                            if last:
                                P.op("dve", lambda e, c=c: e.tensor_copy(out=sout[:, c, 1:4], in_=c4h[:, c, :]),
                                     reads=[("c4h", c)], writes=[("sout", c)])
                        for d in ctx:
                            c = d["c"]
                            P.op("act", lambda e, d=d, c=c: e.activation(
                                out=d["sg"][:, 0:N], in_=ps[:, d["b_gr"], 0:N], func=AF.Silu, bias=par(l, 1, c)),
                                reads=[("ps", d["b_gr"])] + PARK, writes=[d["SG"]])
                        for d in ctx:
                            def conv4(pe, d=d):
                                xrb, b_xc, ci = d["xrb"], d["b_xc"], d["ci"]
                                for k in range(4):
                                    pe.matmul(ps[:, b_xc, 0:np_], dg[:, ci, k, :], xrb[:, k:k + np_],
                                              start=(k == 0), stop=False)
                                ins = pe.matmul(ps[:, b_xc, 0:np_], dg[:, ci, 4, :], ones[:, 0:np_],
                                                start=False, stop=True)
                                if has_s:
                                    xs = xrb[:, sbase:sbase + 176].rearrange("p (s k) -> p s k", k=11)
                                    o = ps[:, b_xc, np_:np_ + 128].rearrange("p (s k) -> p s k", k=8)
                                    for k in range(4):
                                        pe.matmul(o, dg[:, ci, k, :], xs[:, :, k:k + 8], start=(k == 0), stop=False)
                                    ins = pe.matmul(o, dg[:, ci, 4, :],
                                                    ones[:, 0:128].rearrange("p (s k) -> p s k", k=8),
                                                    start=False, stop=True)
                                return ins
                            P.op("pe", conv4, reads=[d["XRB"], ("dg", d["ci"], "rnn"), ("ones",)],
                                 writes=[("ps", d["b_xc"])])
                        for d in ctx:
                            P.op("dve", lambda e, d=d: e.tensor_copy(out=d["xcb"][:, 0:N], in_=ps[:, d["b_xc"], 0:N]),
                                 reads=[("ps", d["b_xc"])], writes=[d["XCB"]])
                        for d in ctx:
                            c = d["c"]
                            P.op("pe", lambda pe, d=d, c=c: pe.matmul(ps[:, b_ga, 0:N], wab[:, 0, c, :],
                                                                      d["xcb"][:, 0:N], start=True, stop=True),
                                 reads=[d["XCB"], ("wab", 0)], writes=[("ps", b_ga)])
                            P.op("pe", lambda pe, d=d, c=c: pe.matmul(ps[:, b_gx, 0:N], wab[:, 1, c, :],
                                                                      d["xcb"][:, 0:N], start=True, stop=True),
                                 reads=[d["XCB"], ("wab", 1)], writes=[("ps", b_gx)])
                            P.op("act", lambda e, d=d, c=c: e.activation(out=d["ta"][:, 0:N], in_=ps[:, b_ga, 0:N],
                                                                         func=AF.Tanh, bias=der[:, 0, l, c:c + 1],
                                                                         scale=0.5),
                                 reads=[("ps", b_ga)] + dk, writes=[d["TA"]])
                            P.op("act", lambda e, d=d, c=c: e.activation(out=d["tx"][:, 0:N], in_=ps[:, b_gx, 0:N],
                                                                         func=AF.Tanh, bias=der[:, 1, l, c:c + 1],
                                                                         scale=0.5),
                                 reads=[("ps", b_gx)] + dk, writes=[d["TX"]])
                        if prev_chain is not None:
                            prev_chain()
                        for d in ctx:
                            c = d["c"]
                            P.op("act", lambda e, d=d, c=c: e.activation(
                                out=d["av"][:, 0:N], in_=d["ta"][:, 0:N], func=AF.Exp,
                                bias=der[:, 3, l, c:c + 1], scale=der[:, 3, l, c:c + 1]),
                                reads=[d["TA"]] + dk, writes=[d["AV"]])
                        for d in ctx:
                            P.op("act", lambda e, d=d: e.activation(out=d["ta"][:, 0:N], in_=d["av"][:, 0:N],
                                                                    func=AF.Square),
                                 reads=[d["AV"]], writes=[d["TA"]])
                        for d in ctx:
                            P.op("act", lambda e, d=d: e.activation(out=d["ta"][:, 0:N], in_=d["ta"][:, 0:N],
                                                                    func=AF.Ln, bias=0.25, scale=-0.25),
                                 reads=[d["TA"]], writes=[d["TA"]])
                        for d in ctx:
                            P.op("act", lambda e, d=d: e.activation(out=d["ta"][:, 0:N], in_=d["ta"][:, 0:N],
                                                                    func=AF.Exp, scale=0.5),
                                 reads=[d["TA"]], writes=[d["TA"]])
                        def chain():
                            for d in ctx:
                                c, ta, tx, av, hs, sg = d["c"], d["ta"], d["tx"], d["av"], d["hs"], d["sg"]
                                TA, TX, AV, HS, SG, b_xc = d["TA"], d["TX"], d["AV"], d["HS"], d["SG"], d["b_xc"]
                                if first:
                                    P.op("dve", lambda e, ta=ta: e.memset(ta[:, 0:1], 0.5), writes=[TA])
                                P.op("dve", lambda e, tx=tx, b_xc=b_xc: e.scalar_tensor_tensor(
                                    out=tx[:, 0:N], in0=tx[:, 0:N], scalar=1.0, in1=ps[:, b_xc, 0:N],
                                    op0=ALU.add, op1=ALU.mult),
                                    reads=[TX, ("ps", b_xc)], writes=[TX])
                                P.op("dve", lambda e, tx=tx, ta=ta: e.tensor_tensor(out=tx[:, 0:N], in0=tx[:, 0:N],
                                                                                    in1=ta[:, 0:N], op=ALU.mult),
                                     reads=[TX, TA], writes=[TX])
                                P.op("dve", lambda e, hs=hs, av=av, tx=tx, c=c: e.tensor_tensor_scan(
                                    out=hs[:, 0:np_], data0=av[:, 0:np_], data1=tx[:, 0:np_],
                                    initial=hcar[:, c:c + 1], op0=ALU.mult, op1=ALU.add),
                                    reads=[AV, TX, ("hcar", c)], writes=[HS])
                                P.op("dve", lambda e, hs=hs, c=c: e.tensor_copy(out=hcar[:, c:c + 1],
                                                                                in_=hs[:, np_ - 1:np_]),
                                     reads=[HS], writes=[("hcar", c)])
                                if last:
                                    P.op("dve", lambda e, hs=hs, c=c: e.tensor_copy(out=sout[:, c, 0:1],
                                                                                    in_=hs[:, np_ - 1:np_]),
                                         reads=[HS], writes=[("sout", c)])
                                if has_s:
                                    a_s = av[:, np_:np_ + 128].rearrange("p (s k) -> p s k", k=8)
                                    b_s = tx[:, np_:np_ + 128].rearrange("p (s k) -> p s k", k=8)
                                    h_s = hs[:, np_:np_ + 128].rearrange("p (s k) -> p s k", k=8)
                                    h0 = stS[:, c, 80:96]
                                    P.op("dve", lambda e, h_s=h_s, a_s=a_s, h0=h0: e.tensor_tensor(
                                        out=h_s[:, :, 0], in0=a_s[:, :, 0], in1=h0, op=ALU.mult),
                                        reads=[AV, ("stS", c // 4)], writes=[HS])
                                    P.op("dve", lambda e, h_s=h_s, b_s=b_s: e.tensor_tensor(
                                        out=b_s[:, :, 0], in0=b_s[:, :, 0], in1=h_s[:, :, 0], op=ALU.add),
                                        reads=[HS, TX], writes=[TX])
                                    P.op("dve", lambda e, a_s=a_s: e.memset(a_s[:, :, 0], 0.0), writes=[AV])
                                    P.op("dve", lambda e, hs=hs, av=av, tx=tx: e.tensor_tensor_scan(
                                        out=hs[:, np_:np_ + 128], data0=av[:, np_:np_ + 128],
                                        data1=tx[:, np_:np_ + 128], initial=0.0, op0=ALU.mult, op1=ALU.add),
                                        reads=[AV, TX], writes=[HS])
                                    P.op("dve", lambda e, h_s=h_s, c=c: e.tensor_copy(out=sout[:, c, 6:22],
                                                                                      in_=h_s[:, :, 7]),
                                         reads=[HS], writes=[("sout", c)])
                                P.op("dve", lambda e, hs=hs, sg=sg, c=c: e.tensor_tensor(
                                    out=pb[:, c, toff:toff + N], in0=hs[:, 0:N], in1=sg[:, 0:N], op=ALU.mult),
                                    reads=[HS, SG], writes=[("p", lt, c)])
                        return chain

                    for u in tl_list:
                        rchain["f"] = rnn_body(u, itctr["rnn"] % 2, rchain["f"])
                        itctr["rnn"] += 1

                if rchain["f"] is not None:
                    rchain["f"]()
                    rchain["f"] = None
                for cp in range(4 if nphase >= 2 else 0):
                    ensure_loaded(gi)
                    ensure_loaded(gi + 1)
                    scc, sch, sgc_, scb = gslots[gi]
                    gi += 1
                    its = [(u, c) for u in tl_list for c in (2 * cp, 2 * cp + 1)]
                    pend = None
                    for (u, c) in its:
                        it = itctr["conv"]
                        itctr["conv"] += 1
                        par_ = it % 2
                        civ = cp * 4 + its.index((u, c))
                        N, np_, has_s = seg_views(u)
                        toff = TILES[u][1]
                        lt = u % 2
                        cc_ = c % 2
                        b_cc, b_ch, b_gc, b_cb, b_v = 0, 1, 2 + par_, 4 + par_, 6 + par_
                        last = (u == 5)
                        ccs, sgc = tf[:, 6 * par_ + 0, :], tf[:, 6 * par_ + 1, :]
                        CCS, SGC = ("tf", 6 * par_ + 0), ("tf", 6 * par_ + 1)
                        ub = tb[:, 2 * par_, :]
                        UB = ("tb", 2 * par_)
                        sbase = 2 + np_

                        def mm(pe, slot, bank, N=N, toff=toff, cc_=cc_):
                            for k in range(NCH):
                                ins = pe.matmul(ps[:, bank, 0:N], wring[:, slot, k, cc_ * 128:(cc_ + 1) * 128],
                                                xb[:, k, toff:toff + N], start=(k == 0), stop=(k == NCH - 1))
                            return ins
                        for (s_, b_) in ((scc, b_cc), (sch, b_ch), (sgc_, b_gc), (scb, b_cb)):
                            P.op("pe", lambda pe, mm=mm, s=s_, b=b_: mm(pe, s, b),
                                 reads=[("ws", s_), ("xb", lt)], writes=[("ps", b_)])
                        if pend is not None:
                            pend()
                        if u == tl_list[0]:
                            build_diag(l, c, par_, "conv")
                        P.op("act", lambda e, ccs=ccs, c=c, N=N: e.activation(
                            out=ccs[:, 0:N], in_=ps[:, b_cc, 0:N], func=AF.Identity, bias=par(l, 3, c)),
                            reads=[("ps", b_cc)] + PARK, writes=[CCS])
                        P.op("act", lambda e, ub=ub, c=c: e.activation(out=ub[:, 0:2], in_=c3h[:, c, :],
                                                                       func=AF.Identity),
                             reads=[("c3h", c)], writes=[UB])
                        if has_s:
                            P.op("act", lambda e, ub=ub, c=c, sbase=sbase: e.activation(
                                out=ub[:, sbase:sbase + 160].rearrange("p (s k) -> p s k", k=10)[:, :, 0:2],
                                in_=stS[:, c, 48:80].rearrange("p (s k) -> p s k", k=2), func=AF.Identity),
                                reads=[("stS", c // 4)], writes=[UB])
                        P.op("dve", lambda e, ub=ub, ccs=ccs, c=c, np_=np_: e.scalar_tensor_tensor(
                            out=ub[:, 2:2 + np_], in0=ps[:, b_ch, 0:np_], scalar=par(l, 4, c), in1=ccs[:, 0:np_],
                            op0=ALU.add, op1=ALU.mult),
                            reads=[("ps", b_ch), CCS] + PARK, writes=[UB])
                        P.op("dve", lambda e, ccs=ccs, c=c, np_=np_: e.scalar_tensor_tensor(
                            out=c3h[:, c, :], in0=ps[:, b_ch, np_ - 2:np_], scalar=par(l, 4, c),
                            in1=ccs[:, np_ - 2:np_], op0=ALU.add, op1=ALU.mult),
                            reads=[("ps", b_ch), CCS] + PARK, writes=[("c3h", c)])
                        if has_s:
                            P.op("dve", lambda e, ub=ub, ccs=ccs, c=c, np_=np_, sbase=sbase: e.scalar_tensor_tensor(
                                out=ub[:, sbase:sbase + 160].rearrange("p (s k) -> p s k", k=10)[:, :, 2:10],
                                in0=ps[:, b_ch, np_:np_ + 128].rearrange("p (s k) -> p s k", k=8),
                                scalar=par(l, 4, c),
                                in1=ccs[:, np_:np_ + 128].rearrange("p (s k) -> p s k", k=8),
                                op0=ALU.add, op1=ALU.mult),
                                reads=[("ps", b_ch), CCS] + PARK, writes=[UB])
                            P.op("dve", lambda e, ccs=ccs, c=c, np_=np_: e.scalar_tensor_tensor(
                                out=sout[:, c, 70:102].rearrange("p (s k) -> p s k", k=2),
                                in0=ps[:, b_ch, np_:np_ + 128].rearrange("p (s k) -> p s k", k=8)[:, :, 6:8],
                                scalar=par(l, 4, c),
                                in1=ccs[:, np_:np_ + 128].rearrange("p (s k) -> p s k", k=8)[:, :, 6:8],
                                op0=ALU.add, op1=ALU.mult),
                                reads=[("ps", b_ch), CCS] + PARK, writes=[("sout", c)])
                        if last:
                            P.op("dve", lambda e, c=c: e.tensor_copy(out=sout[:, c, 4:6], in_=c3h[:, c, :]),
                                 reads=[("c3h", c)], writes=[("sout", c)])
                        P.op("act", lambda e, sgc=sgc, c=c, N=N, b=b_gc: e.activation(
                            out=sgc[:, 0:N], in_=ps[:, b, 0:N], func=AF.Silu, bias=par(l, 5, c)),
                            reads=[("ps", b_gc)] + PARK, writes=[SGC])
                        P.op("dve", lambda e, sgc=sgc, c=c, N=N, b=b_cb: e.scalar_tensor_tensor(
                            out=sgc[:, 0:N], in0=ps[:, b, 0:N], scalar=par(l, 2, c), in1=sgc[:, 0:N],
                            op0=ALU.add, op1=ALU.mult),
                            reads=[("ps", b_cb), SGC] + PARK, writes=[SGC])
                        if pending_norm:
                            (u_, si_, ll_) = pending_norm[civ % 2]
                            ph3_norm_chunk(u_, si_, civ // 2, ll_)
                            if civ == 2 * NCH - 1:
                                del pending_norm[:]

                        def pe2(u=u, c=c, par_=par_, N=N, np_=np_, has_s=has_s, ub=ub, UB=UB, b_v=b_v, sgc=sgc,
                                SGC=SGC, toff=toff, lt=lt, sbase=sbase):
                            def conv3(pe):
                                for k in range(3):
                                    ins = pe.matmul(ps[:, b_v, 0:np_], dg[:, par_, 5 + k, :], ub[:, k:k + np_],
                                                    start=(k == 0), stop=(k == 2))
                                if has_s:
                                    us = ub[:, sbase:sbase + 160].rearrange("p (s k) -> p s k", k=10)
                                    o = ps[:, b_v, np_:np_ + 128].rearrange("p (s k) -> p s k", k=8)
                                    for k in range(3):
                                        ins = pe.matmul(o, dg[:, par_, 5 + k, :], us[:, :, k:k + 8],
                                                        start=(k == 0), stop=(k == 2))
                                return ins
                            P.op("pe", conv3, reads=[UB, ("dg", par_, "conv")], writes=[("ps", b_v)])
                            P.op("dve", lambda e: e.tensor_tensor(out=qb[:, c, toff:toff + N], in0=sgc[:, 0:N],
                                                                  in1=ps[:, b_v, 0:N], op=ALU.mult),
                                 reads=[SGC, ("ps", b_v)], writes=[("q", lt, c)])
                        pend = pe2
                    if pend is not None:
                        pend()
                        pend = None

                for jp in range(4 if nphase >= 3 else 0):
                    ensure_loaded(gi)
                    ensure_loaded(gi + 1)
                    sro, sco, sg1, sg2 = gslots[gi]
                    gi += 1
                    for u in tl_list:
                        for j in (2 * jp, 2 * jp + 1):
                            it = itctr["ph2"]
                            itctr["ph2"] += 1
                            par_ = it % 2
                            N, np_, has_s = seg_views(u)
                            toff = TILES[u][1]
                            lt = u % 2
                            jj = j % 2
                            b_yr, b_yc, b_g1, b_g2 = par_, 2 + par_, 4 + par_, 6 + par_
                            s1, s2 = tf[:, 6 * par_ + 0, :], tf[:, 6 * par_ + 1, :]
                            S1K, S2K = ("tf", 6 * par_ + 0), ("tf", 6 * par_ + 1)

                            def mm(pe, slot, bank, src, N=N, toff=toff, jj=jj):
                                for k in range(NCH):
                                    ins = pe.matmul(ps[:, bank, 0:N], wring[:, slot, k, jj * 128:(jj + 1) * 128],
                                                    src[:, k, toff:toff + N], start=(k == 0), stop=(k == NCH - 1))
                                return ins
                            P.op("pe", lambda pe, mm=mm, s=sg1, b=b_g1: mm(pe, s, b, xb),
                                 reads=[("ws", sg1), ("xb", lt)], writes=[("ps", b_g1)])
                            P.op("pe", lambda pe, mm=mm, s=sg2, b=b_g2: mm(pe, s, b, xb),
                                 reads=[("ws", sg2), ("xb", lt)], writes=[("ps", b_g2)])
                            P.op("pe", lambda pe, mm=mm, s=sro, b=b_yr: mm(pe, s, b, pb),
                                 reads=[("ws", sro)] + [("p", lt, c) for c in range(NCH)], writes=[("ps", b_yr)])
                            P.op("pe", lambda pe, mm=mm, s=sco, b=b_yc: mm(pe, s, b, qb),
                                 reads=[("ws", sco)] + [("q", lt, c) for c in range(NCH)], writes=[("ps", b_yc)])
                            P.op("act", lambda e, s1=s1, j=j, N=N, b=b_g1: e.activation(
                                out=s1[:, 0:N], in_=ps[:, b, 0:N], func=AF.Sigmoid, bias=par(l, 6, j)),
                                reads=[("ps", b_g1)] + PARK, writes=[S1K])
                            P.op("act", lambda e, s2=s2, j=j, N=N, b=b_g2: e.activation(
                                out=s2[:, 0:N], in_=ps[:, b, 0:N], func=AF.Sigmoid, bias=par(l, 7, j)),
                                reads=[("ps", b_g2)] + PARK, writes=[S2K])
                            P.op("dve", lambda e, s1=s1, N=N, b=b_yr: e.tensor_tensor(
                                out=s1[:, 0:N], in0=s1[:, 0:N], in1=ps[:, b, 0:N], op=ALU.mult),
                                reads=[S1K, ("ps", b_yr)], writes=[S1K])
                            P.op("dve", lambda e, s2=s2, N=N, b=b_yc: e.tensor_tensor(
                                out=s2[:, 0:N], in0=s2[:, 0:N], in1=ps[:, b, 0:N], op=ALU.mult),
                                reads=[S2K, ("ps", b_yc)], writes=[S2K])
                            P.op("dve", lambda e, s1=s1, s2=s2, N=N, j=j, toff=toff: e.tensor_tensor(
                                out=mb[:, j, toff:toff + N], in0=s1[:, 0:N], in1=s2[:, 0:N], op=ALU.add),
                                reads=[S1K, S2K], writes=[("m", lt, j)])

                ensure_loaded(gi)
                ensure_loaded(gi + 1)
                swo = gslots[gi]
                gi += 1
                nxt = (l, pa + 1) if pa < npass - 1 else ((l + 1, 0) if l + 1 < depth else None)
                def ph3_loop1(u, si, swo=swo, cast_pa=None, hook=None):
                    b_S1, b_S2 = 2 + 2 * si, 3 + 2 * si
                    N, np_, has_s = seg_views(u)
                    toff = TILES[u][1]
                    lt = u % 2
                    segs = TILES[u][2]
                    pend_s = None
                    for j in range(NCH):
                        it = itctr["ph3"]
                        itctr["ph3"] += 1
                        par_ = it % 2
                        b_o = par_
                        vb, vsq = tb[:, 2 * par_, :], tb[:, 2 * par_ + 1, :]
                        VB, VSQ = ("tb", 2 * par_), ("tb", 2 * par_ + 1)

                        def mmo(pe, j=j, b_o=b_o):
                            slot = swo[j // 2]
                            jj = j % 2
                            for k in range(NCH):
                                ins = pe.matmul(ps[:, b_o, 0:N], wring[:, slot, k, jj * 128:(jj + 1) * 128],
                                                mb[:, k, toff:toff + N], start=(k == 0), stop=(k == NCH - 1))
                            return ins
                        P.op("pe", mmo, reads=[("ws", swo[j // 2])] + [("m", lt, k) for k in range(NCH)],
                             writes=[("ps", b_o)])
                        if pend_s is not None:
                            pend_s()
                        for (kind, g0, n, l0) in segs:
                            P.op("dve", lambda e, j=j, g0=g0, n=n, l0=l0, b_o=b_o: e.scalar_tensor_tensor(
                                out=x32[:, j, g0:g0 + n], in0=x32[:, j, g0:g0 + n], scalar=ALPHA,
                                in1=ps[:, b_o, l0:l0 + n], op0=ALU.mult, op1=ALU.add),
                                reads=[("x32", u, j), ("ps", b_o)], writes=[("x32", u, j)])
                        for (kind, g0, n, l0) in segs:
                            P.op("act", lambda e, j=j, g0=g0, n=n, l0=l0, vb=vb: e.activation(
                                out=vb[:, l0:l0 + n], in_=x32[:, j, g0:g0 + n], func=AF.Copy),
                                reads=[("x32", u, j)], writes=[VB])
                            P.op("act", lambda e, j=j, g0=g0, n=n, l0=l0, vsq=vsq: e.activation(
                                out=vsq[:, l0:l0 + n], in_=x32[:, j, g0:g0 + n], func=AF.Square),
                                reads=[("x32", u, j)], writes=[VSQ])

                        if cast_pa is not None:
                            cast_xb(cast_pa, "act", j, si)

                        def smm(j=j, vb=vb, vsq=vsq, VB=VB, VSQ=VSQ):
                            P.op("pe", lambda pe: pe.matmul(
                                ps[:, b_S1, 0:N], ones[:, 0:128], vb[:, 0:N], start=(j == 0), stop=(j == NCH - 1)),
                                reads=[VB, ("ones",)], writes=[("ps", b_S1)])
                            P.op("pe", lambda pe: pe.matmul(
                                ps[:, b_S2, 0:N], ones[:, 0:128], vsq[:, 0:N], start=(j == 0), stop=(j == NCH - 1)),
                                reads=[VSQ, ("ones",)], writes=[("ps", b_S2)])
                        pend_s = smm
                        if hook is not None:
                            hook(j)
                    pend_s()

                def ph3_stats(u, si):
                    N, np_, has_s = seg_views(u)
                    b_S1, b_S2 = 2 + 2 * si, 3 + 2 * si
                    mean, msq = tf[:, 8, :], tf[:, 9, :]
                    MEAN, MSQ = ("tf", 8), ("tf", 9)
                    Av, Bv = nrm[:, 2 * si, :], nrm[:, 2 * si + 1, :]
                    SK = ("stg", si)
                    g0, g1, g2 = [], [], []
                    g0.append(lambda: P.op("dve", lambda e: e.tensor_scalar(out=mean[:, 0:N], in0=ps[:, b_S1, 0:N],
                                                          scalar1=1.0 / D, scalar2=None, op0=ALU.mult),
                         reads=[("ps", b_S1)], writes=[MEAN]))
                    g0.append(lambda: P.op("act", lambda e: e.activation(out=msq[:, 0:N], in_=ps[:, b_S1, 0:N], func=AF.Square,
                                                       scale=1.0 / D),
                         reads=[("ps", b_S1)], writes=[MSQ]))
                    g1.append(lambda: P.op("dve", lambda e: e.scalar_tensor_tensor(out=msq[:, 0:N], in0=ps[:, b_S2, 0:N],
                                                                 scalar=1.0 / D, in1=msq[:, 0:N],
                                                                 op0=ALU.mult, op1=ALU.subtract),
                         reads=[("ps", b_S2), MSQ], writes=[MSQ]))
                    g1.append(lambda: P.op("act", lambda e: e.activation(out=msq[:, 0:N], in_=msq[:, 0:N], func=AF.Ln,
                                                       bias=epst[:, 0:1], scale=1.0),
                         reads=[MSQ, ("eps",)], writes=[MSQ]))
                    g1.append(lambda: P.op("act", lambda e: e.activation(out=Av[:, 0:N], in_=msq[:, 0:N], func=AF.Exp, scale=-0.5),
                         reads=[MSQ], writes=[SK]))
                    g2.append(lambda: P.op("dve", lambda e: e.scalar_tensor_tensor(out=Bv[:, 0:N], in0=mean[:, 0:N], scalar=-1.0,
                                                                 in1=Av[:, 0:N], op0=ALU.mult, op1=ALU.mult),
                         reads=[MEAN, SK], writes=[SK]))
                    return [g0, g1, g2]

                u0, u1 = tl_list
                flush_deferred()
                ph3_loop1(u0, 0, cast_pa=(nxt[1] if nxt is not None else None))
                st0 = ph3_stats(u0, 0)

                def hook0(j, st0=st0):
                    if j in (1, 3, 5):
                        for f in st0[(j - 1) // 2]:
                            f()
                ph3_loop1(u1, 1, hook=hook0, cast_pa=(nxt[1] if nxt is not None else None))
                for g in ph3_stats(u1, 1):
                    for f in g:
                        f()
                pending_norm.extend([(u0, 0, l), (u1, 1, l)])
                if pa == npass - 1 and l == depth - 1:
                    flush_deferred()

            return gi

        def cast_xb(pa, eng="pool", only_k=None, only_t=None):
            for ti_, u in enumerate(PASS_TILES[pa]):
                if only_t is not None and ti_ != only_t:
                    continue
                toff = TILES[u][1]
                lt = u % 2
                for (kind, g0, n, l0) in TILES[u][2]:
                    ks = range(NCH) if only_k is None else [only_k]
                    if eng == "pool":
                        P.op("pool", lambda e, g0=g0, n=n, o=toff + l0: e.tensor_copy(
                            out=xb[:, :, o:o + n], in_=x32[:, :, g0:g0 + n]),
                            reads=x32keys(u), writes=[("xb", lt)])
                    else:
                        for k in ks:
                            P.op("act", lambda e, g0=g0, n=n, o=toff + l0, k=k: e.activation(
                                out=xb[:, k, o:o + n], in_=x32[:, k, g0:g0 + n], func=AF.Copy),
                                reads=[("x32", u, k)], writes=[("xb", lt)])

        def store_states(l):
            def tr(pe):
                for c in range(NCH):
                    ins = pe.transpose(ps[0:102, 4 + c // 4, (c % 4) * 128:(c % 4 + 1) * 128], sout[:, c, :], ident[:])
                return ins
            P.op("pe", tr, reads=[("sout", c) for c in range(NCH)] + [("ident",)], writes=[("ps", 4), ("ps", 5)])
            for h in range(2):
                P.op("act", lambda e, h=h: e.activation(out=stg_out[0:102, 512 * h:512 * (h + 1)],
                                                        in_=ps[0:102, 4 + h, :], func=AF.Identity),
                     reads=[("ps", 4 + h)], writes=SOUTK)
            P.op("sp", lambda e: e.dma_start(out=st_out[l], in_=stg_out[0:102, :]),
                 reads=SOUTK, writes=[("st_out", l)], dma="out0")

        ensure_loaded(0)
        ensure_loaded(1)
        cast_xb(0)
        gi = 0
        for l in range(depth):
            gi = run_layer(l, gi)
            store_states(l)

        for b in range(NBLK):
            s = b % 6
            sap, skeys = XS[s]
            pbk = 2 * (b % 4)
            rk = []
            for u in tiles_of_block(b):
                rk += x32keys(u)

            def tr_out(pe, b=b, pbk=pbk):
                for c in range(NCH):
                    ins = pe.transpose(ps[:, pbk + c // 4, (c % 4) * 128:(c % 4 + 1) * 128],
                                       x32[:, c, b * 128:(b + 1) * 128], ident[:])
                return ins
            P.op("pe", tr_out, reads=rk + [("ident",)], writes=[("ps", pbk), ("ps", pbk + 1)])
            for h in range(2):
                eng = "act" if h == 0 else "dve"

                def ev(e, sap=sap, h=h, pbk=pbk, eng=eng):
                    o = sap[:, 512 * h:512 * (h + 1)]
                    i = ps[:, pbk + h, :]
                    if eng == "act":
                        return e.activation(out=o, in_=i, func=AF.Identity)
                    return e.tensor_copy(out=o, in_=i)
                P.op(eng, ev, reads=[("ps", pbk + h)], writes=skeys)
            P.op("sp", lambda e, b=b, sap=sap: e.dma_start(out=y[b * 128:(b + 1) * 128, :], in_=sap),
                 reads=skeys, writes=[("y", b)], dma="out%d" % s)
        P.op("sp", lambda e: e.nop(), reads=[("y", b) for b in range(NBLK)] + [("st_out", l) for l in range(depth)])

        P.finalize()
        with nc.Block() as block:
            @block.tensor
            def _(e):
                P.emit_stream("pe", e, esem, dsem)

            @block.scalar
            def _(e):
                P.emit_stream("act", e, esem, dsem)

            @block.vector
            def _(e):
                P.emit_stream("dve", e, esem, dsem)

            @block.gpsimd
            def _(e):
                P.emit_stream("pool", e, esem, dsem)

            @block.sync
            def _(e):
                P.emit_stream("sp", e, esem, dsem)
    return nc


_NC_CACHE = {}


def _prep_inputs(inputs):
    f = lambda k: np.ascontiguousarray(np.asarray(inputs[k], dtype=np.float32))
    x_prompt, x_sample = f("x_prompt"), f("x_sample")
    s_h, s_c4, s_c3 = f("state_rglru"), f("state_conv4"), f("state_conv3")
    rows = []
    for l in range(DEPTH):
        rows.append(f("b_in")[l].reshape(8, D))
        rows.append(f("conv4_w")[l].reshape(4, D))
        rows.append(f("conv4_b")[l].reshape(1, D))
        rows.append(f("b_rg_a")[l].reshape(1, D))
        rows.append(f("b_rg_x")[l].reshape(1, D))
        rows.append(f("rg_lambda")[l].reshape(1, D))
        rows.append(f("conv3_w")[l].reshape(3, D))
        rows.append(f("ln_g")[l].reshape(1, D))
        rows.append(f("ln_b")[l].reshape(1, D))
    params = np.ascontiguousarray(np.concatenate(rows, axis=0))
    shared = {
        "params": params, "w_in": f("w_in"), "w_ro": f("w_rnn_out"), "w_co": f("w_conv_out"), "w_o": f("w_out"),
        "w_a": f("w_rg_a"), "w_x": f("w_rg_x"), "ident": np.eye(128, dtype=np.float32),
    }
    in_maps = []
    for c in range(8):
        sl = slice(16 * c, 16 * c + 16)
        xin = np.concatenate([x_prompt[c], x_sample[sl].reshape(128, D)], axis=0)
        st = np.concatenate([s_c4[:, sl].reshape(DEPTH, 48, D), s_c3[:, sl].reshape(DEPTH, 32, D),
                             s_h[:, sl].reshape(DEPTH, 16, D)], axis=1)
        m = dict(shared)
        m["xin"] = np.ascontiguousarray(xin)
        m["st_in"] = np.ascontiguousarray(st)
        in_maps.append(m)
    return in_maps


def kernel(**inputs):
    if "nc" not in _NC_CACHE:
        _NC_CACHE["nc"] = build_nc()
    nc = _NC_CACHE["nc"]
    in_maps = _prep_inputs(inputs)
    res = run_bass_kernel_spmd(nc, in_maps, core_ids=list(range(8)))
    ys = [np.asarray(r["y"]) for r in res.results]
    sts = [np.asarray(r["st_out"]) for r in res.results]
    y_prompt = np.stack([yy[0:NPR] for yy in ys], axis=0)
    y_sample = np.concatenate([yy[NPR:].reshape(16, 8, D) for yy in ys], axis=0)
    ph = np.stack([s[:, 0] for s in sts], axis=1)
    pc4 = np.stack([s[:, 1:4] for s in sts], axis=1)
    pc3 = np.stack([s[:, 4:6] for s in sts], axis=1)
    sh = np.concatenate([s[:, 6:22] for s in sts], axis=1)
    sc4 = np.concatenate([s[:, 22:70].reshape(DEPTH, 16, 3, D) for s in sts], axis=1)
    sc3 = np.concatenate([s[:, 70:102].reshape(DEPTH, 16, 2, D) for s in sts], axis=1)
    f32 = lambda a: np.ascontiguousarray(a, dtype=np.float32)
    return (f32(y_prompt), f32(y_sample), f32(ph), f32(pc4), f32(pc3), f32(sh), f32(sc4), f32(sc3))
```
